# Optimizing a Trainium2 kernel written in Bass

```python
import math
import jax, jax.numpy as jnp
from jax import lax
import numpy as np

D_MODEL = 2048
BATCH = 4
SEQ = 2048
DEPTH = 2
DEC_BATCH = 128
DEC_SEQ = 8
PAST_LEN = 16384
PAGE_SIZE = 128

SGU_WIDTH = D_MODEL // 2
SGU_GROUPS = 8
SGU_GROUP_DIM = SGU_WIDTH // SGU_GROUPS
SGU_CHUNK = 128
RET_WIDTH = D_MODEL // 2
RET_HEADS = 8
RET_DK = RET_WIDTH // RET_HEADS
RET_DV = RET_WIDTH // RET_HEADS
RET_CHUNK = 128
ROPE_BASE = 10000.0
D_FF = 11 * D_MODEL // 4
CONV_W = 3
EPS = 1e-6
IN_SIZES = (SGU_WIDTH, SGU_WIDTH, RET_HEADS * RET_DK, RET_HEADS * RET_DK, RET_WIDTH, RET_WIDTH, D_MODEL, D_MODEL)
IN_COLS = SGU_WIDTH * 2 + RET_HEADS * RET_DK * 2 + RET_WIDTH * 2 + D_MODEL * 2

kernel_name = "hybrid_sgu_retention_convffn_decode_step"


def rms_norm(x, g):
    xf = x.astype(jnp.float32)
    y = xf * lax.rsqrt(jnp.mean(xf * xf, axis=-1, keepdims=True) + EPS)
    return (y * g.astype(jnp.float32)).astype(x.dtype)


def layer_norm(x, g, b):
    xf = x.astype(jnp.float32)
    mu = jnp.mean(xf, axis=-1, keepdims=True)
    var = jnp.mean(jnp.square(xf - mu), axis=-1, keepdims=True)
    y = (xf - mu) * lax.rsqrt(var + EPS)
    return (y * g.astype(jnp.float32) + b.astype(jnp.float32)).astype(x.dtype)


def rope(x, pos):
    half = x.shape[-1] // 2
    inv = ROPE_BASE ** (-jnp.arange(half, dtype=jnp.float32) / half)
    ang = pos.astype(jnp.float32)[:, None] * inv[None, :]
    cos = jnp.cos(ang)[None, :, None, :]
    sin = jnp.sin(ang)[None, :, None, :]
    xf = x.astype(jnp.float32)
    x1, x2 = xf[..., :half], xf[..., half:]
    return jnp.concatenate([x1 * cos - x2 * sin, x1 * sin + x2 * cos], axis=-1)


def retention_chunkwise(q, k, v, s0):
    B, L, H, dk = q.shape
    dv = v.shape[-1]
    C = math.gcd(L, RET_CHUNK)
    n = L // C
    log_g = jnp.log(1.0 - jnp.exp2(-5.0 - jnp.arange(H, dtype=jnp.float32)))
    idx = jnp.arange(C, dtype=jnp.float32)
    diff = idx[:, None] - idx[None, :]
    causal = diff >= 0
    decay = jnp.where(causal[None], jnp.exp(log_g[:, None, None] * jnp.where(causal, diff, 0.0)[None]), 0.0)
    q_dec = jnp.exp(log_g[:, None] * (idx[None, :] + 1.0))
    k_dec = jnp.exp(log_g[:, None] * (C - 1.0 - idx[None, :]))
    chunk_dec = jnp.exp(log_g * C)

    def to_chunks(t):
        return t.reshape(B, n, C, H, t.shape[-1]).transpose(1, 0, 3, 2, 4)

    def step(s, inp):
        qi, ki, vi = inp
        scores = jnp.einsum('bhid,bhjd->bhij', qi, ki) * decay[None]
        inner = jnp.einsum('bhij,bhjv->bhiv', scores, vi)
        cross = jnp.einsum('bhid,bhdv->bhiv', qi, s) * q_dec[None, :, :, None]
        s_new = s * chunk_dec[None, :, None, None] + jnp.einsum('bhjd,bhjv->bhdv', ki * k_dec[None, :, :, None], vi)
        return s_new, inner + cross

    s_fin, out = lax.scan(step, s0, (to_chunks(q), to_chunks(k), to_chunks(v)))
    out = out.transpose(1, 0, 3, 2, 4).reshape(B, L, H, dv)
    return out, s_fin


def head_norm(o, g):
    mu = jnp.mean(o, axis=-1, keepdims=True)
    var = jnp.mean(jnp.square(o - mu), axis=-1, keepdims=True)
    y = ((o - mu) * lax.rsqrt(var + EPS)).reshape(o.shape[0], o.shape[1], -1)
    return y * g.astype(jnp.float32)


def token_mixer(h, s0, pos0, w_in, w_s, b_s, sgu_ln_g, sgu_ln_b, ret_gn_g, w_branch_a, w_branch_b, w_out):
    B, L, _ = h.shape
    proj = h @ w_in
    offs = np.cumsum((0,) + IN_SIZES)
    u, v, q, k, vr, gr, ga, gb = [proj[..., int(offs[i]):int(offs[i + 1])] for i in range(len(IN_SIZES))]

    u = jax.nn.gelu(u)
    v = layer_norm(jax.nn.gelu(v), sgu_ln_g, sgu_ln_b)
    C = min(SGU_CHUNK, L)
    n = L // C
    w_sp = jnp.where(jnp.tril(jnp.ones((C, C), dtype=bool))[None], w_s[:, :C, :C], 0.0).astype(v.dtype)
    vg = v.reshape(B, n, C, SGU_GROUPS, SGU_GROUP_DIM)
    z = jnp.einsum('gts,bnsgd->bntgd', w_sp, vg) + b_s[:, :C].T[None, None, :, :, None]
    y_a = u * z.reshape(B, L, SGU_WIDTH)

    pos = pos0 + jnp.arange(L, dtype=jnp.int32)
    qr = rope(q.reshape(B, L, RET_HEADS, RET_DK), pos)
    kr = rope(k.reshape(B, L, RET_HEADS, RET_DK), pos) * (RET_DK ** -0.5)
    vv = vr.reshape(B, L, RET_HEADS, RET_DV).astype(jnp.float32)
    o, s_fin = retention_chunkwise(qr, kr, vv, s0.astype(jnp.float32))
    y_b = jax.nn.silu(gr) * head_norm(o, ret_gn_g).astype(h.dtype)

    merged = jax.nn.sigmoid(ga) * (y_a @ w_branch_a) + jax.nn.sigmoid(gb) * (y_b @ w_branch_b)
    return merged @ w_out, s_fin.astype(s0.dtype), v


def conv_ffn(h, conv_prev, w_gate, w_up, conv_w, conv_b, w_down):
    L = h.shape[1]
    g = h @ w_gate
    up = h @ w_up
    gp = jnp.concatenate([conv_prev.astype(g.dtype), g], axis=1)
    conv = sum(gp[:, i:i + L] * conv_w[i] for i in range(CONV_W)) + conv_b
    y = jax.nn.gelu(conv) * up
    return y @ w_down, gp[:, L:]


def decoder_layer(x, c, pos0, s_ret, conv_prev, w_ada, b_ada, norm_pre1, norm_post1, norm_pre2, norm_post2,
                  w_in, sgu_w_s, sgu_b_s, sgu_ln_g, sgu_ln_b, ret_gn_g, w_branch_a, w_branch_b, w_out,
                  ffn_w_gate, ffn_w_up, ffn_conv_w, ffn_conv_b, ffn_w_down):
    mod = (jax.nn.silu(c) @ w_ada + b_ada)[:, None, :]
    sh1, sc1, g1, sh2, sc2, g2 = jnp.split(mod, 6, axis=-1)
    h = rms_norm(x, norm_pre1) * (1.0 + sc1) + sh1
    t, s_new, v_rows = token_mixer(h, s_ret, pos0, w_in, sgu_w_s, sgu_b_s, sgu_ln_g, sgu_ln_b, ret_gn_g,
                                   w_branch_a, w_branch_b, w_out)
    x = x + g1 * rms_norm(t, norm_post1)
    h2 = rms_norm(x, norm_pre2) * (1.0 + sc2) + sh2
    f, conv_new = conv_ffn(h2, conv_prev, ffn_w_gate, ffn_w_up, ffn_conv_w, ffn_conv_b, ffn_w_down)
    x = x + g2 * rms_norm(f, norm_post2)
    return x, s_new, conv_new, v_rows


def setup_inputs(seed: int = 0) -> dict:
    key = jax.random.key(seed)
    ks = jax.random.split(key, 32)
    f32 = jnp.float32
    nrm = lambda k, shape, s: jax.random.normal(k, shape, f32) * s
    return {
        "x_prompt": nrm(ks[0], (BATCH, SEQ, D_MODEL), 1.0),
        "x_sample": nrm(ks[1], (DEC_BATCH, DEC_SEQ, D_MODEL), 1.0),
        "state_ret": nrm(ks[2], (DEPTH, DEC_BATCH, RET_HEADS, RET_DK, RET_DV), 0.5),
        "state_conv": nrm(ks[3], (DEPTH, DEC_BATCH, CONV_W - 1, D_FF), 1.0),
        "c_prompt": nrm(ks[4], (BATCH, D_MODEL), 1.0),
        "c_sample": nrm(ks[5], (DEC_BATCH, D_MODEL), 1.0),
        "w_ada": nrm(ks[6], (DEPTH, D_MODEL, 6 * D_MODEL), 0.5 * D_MODEL ** -0.5),
        "b_ada": nrm(ks[7], (DEPTH, 6 * D_MODEL), 0.01),
        "norm_pre1": 1.0 + nrm(ks[8], (DEPTH, D_MODEL), 0.05),
        "norm_post1": 1.0 + nrm(ks[9], (DEPTH, D_MODEL), 0.05),
        "norm_pre2": 1.0 + nrm(ks[10], (DEPTH, D_MODEL), 0.05),
        "norm_post2": 1.0 + nrm(ks[11], (DEPTH, D_MODEL), 0.05),
        "w_in": nrm(ks[12], (DEPTH, D_MODEL, IN_COLS), D_MODEL ** -0.5),
        "sgu_w_s": nrm(ks[13], (DEPTH, SGU_GROUPS, SGU_CHUNK, SGU_CHUNK), SGU_CHUNK ** -0.5),
        "sgu_b_s": 1.0 + nrm(ks[14], (DEPTH, SGU_GROUPS, SGU_CHUNK), 0.05),
        "sgu_ln_g": 1.0 + nrm(ks[15], (DEPTH, SGU_WIDTH), 0.05),
        "sgu_ln_b": nrm(ks[16], (DEPTH, SGU_WIDTH), 0.01),
        "ret_gn_g": 1.0 + nrm(ks[17], (DEPTH, RET_WIDTH), 0.05),
        "w_branch_a": nrm(ks[18], (DEPTH, SGU_WIDTH, D_MODEL), SGU_WIDTH ** -0.5),
        "w_branch_b": nrm(ks[19], (DEPTH, RET_WIDTH, D_MODEL), RET_WIDTH ** -0.5),
        "w_out": nrm(ks[20], (DEPTH, D_MODEL, D_MODEL), D_MODEL ** -0.5),
        "ffn_w_gate": nrm(ks[21], (DEPTH, D_MODEL, D_FF), D_MODEL ** -0.5),
        "ffn_w_up": nrm(ks[22], (DEPTH, D_MODEL, D_FF), D_MODEL ** -0.5),
        "ffn_conv_w": nrm(ks[23], (DEPTH, CONV_W, D_FF), CONV_W ** -0.5),
        "ffn_conv_b": nrm(ks[24], (DEPTH, D_FF), 0.01),
        "ffn_w_down": nrm(ks[25], (DEPTH, D_FF, D_MODEL), D_FF ** -0.5),
    }


def reference(x_prompt, x_sample, state_ret, state_conv, c_prompt, c_sample, w_ada, b_ada,
              norm_pre1, norm_post1, norm_pre2, norm_post2, w_in, sgu_w_s, sgu_b_s, sgu_ln_g, sgu_ln_b,
              ret_gn_g, w_branch_a, w_branch_b, w_out, ffn_w_gate, ffn_w_up, ffn_conv_w, ffn_conv_b, ffn_w_down):
    xp, xs = x_prompt, x_sample
    nb = xp.shape[0]
    ret_p, ret_s, conv_p, conv_s, v_s = [], [], [], [], []
    for l in range(DEPTH):
        lw = (w_ada[l], b_ada[l], norm_pre1[l], norm_post1[l], norm_pre2[l], norm_post2[l], w_in[l],
              sgu_w_s[l], sgu_b_s[l], sgu_ln_g[l], sgu_ln_b[l], ret_gn_g[l], w_branch_a[l], w_branch_b[l],
              w_out[l], ffn_w_gate[l], ffn_w_up[l], ffn_conv_w[l], ffn_conv_b[l], ffn_w_down[l])
        s0_p = jnp.zeros((nb, RET_HEADS, RET_DK, RET_DV), xp.dtype)
        c0_p = jnp.zeros((nb, CONV_W - 1, D_FF), xp.dtype)
        xp, sp, cp, _ = decoder_layer(xp, c_prompt, 0, s0_p, c0_p, *lw)
        xs, ss, cs, vs = decoder_layer(xs, c_sample, PAST_LEN, state_ret[l], state_conv[l], *lw)
        ret_p.append(sp); ret_s.append(ss); conv_p.append(cp); conv_s.append(cs); v_s.append(vs)
    return (xp, xs, jnp.stack(ret_p), jnp.stack(ret_s), jnp.stack(conv_p), jnp.stack(conv_s), jnp.stack(v_s))
```

```python
import contextlib
import numpy as np
import concourse.bass as bass
import concourse.mybir as mybir
from concourse.bass_utils import run_bass_kernel_spmd

F32 = mybir.dt.float32
BF16 = mybir.dt.bfloat16
ALU = mybir.AluOpType
AF = mybir.ActivationFunctionType
AX = mybir.AxisListType

D = 2048
KB = 16
NT = 9
T = NT * 128
H = 8
FF = 5632
FB = 44
NS = 17
DEPTH = 2
EPS = 1e-6
PW = 256
NSLOT = 3
OFF_U, OFF_V, OFF_Q, OFF_K, OFF_VR, OFF_GR, OFF_GA, OFF_GB = 0, 1024, 2048, 3072, 4096, 5120, 6144, 8192
TG = [(0, 384), (384, 384), (768, 384)]


class _Stop(Exception):
    pass


class Buf:
    __slots__ = ("name", "w", "r", "excl")

    def __init__(self, name, inherit=(), excl=False):
        self.name = name
        self.w = None
        self.r = {}
        self.excl = excl
        for b in inherit:
            if b.w is not None:
                k, v = b.w
                if self.r.get(k, 0) < v:
                    self.r[k] = v
            for k, v in b.r.items():
                if self.r.get(k, 0) < v:
                    self.r[k] = v


class Prog:
    ENG = ("pe", "act", "dve", "pool", "sp")

    def __init__(self, nc, dry=False):
        self.nc = nc
        self.dry = dry
        self.ops = {e: [] for e in self.ENG}
        self.cnt = {e: 0 for e in self.ENG}
        self.seen = {e: {} for e in self.ENG}
        self.pending = {e: False for e in self.ENG}
        self.label = "init"
        self.pe_labels = []
        self.dsem = []
        self.dcnt = []
        if not dry:
            self.sems = {e: nc.alloc_semaphore("s_" + e) for e in self.ENG}

    def chan(self, name):
        self.dsem.append(None if self.dry else self.nc.alloc_semaphore("d_" + name))
        self.dcnt.append(0)
        return len(self.dsem) - 1

    def _need(self, eng, ev):
        key, val = ev
        if self.seen[eng].get(key, 0) >= val:
            return
        self.seen[eng][key] = val
        if self.dry:
            return
        sem = self.sems[key] if isinstance(key, str) else self.dsem[key]
        self.ops[eng].append(("wait", sem, val))

    def _deps(self, eng, reads, writes):
        for b in reads:
            if b.w is not None:
                if not (b.w[0] == eng and eng == "pe"):
                    self._need(eng, b.w)
            if b.excl:
                for k, v in b.r.items():
                    if k != eng:
                        self._need(eng, (k, v))
        for b in writes:
            if b.w is not None and b.w[0] != eng:
                self._need(eng, b.w)
            for k, v in b.r.items():
                if k != eng:
                    self._need(eng, (k, v))

    def _mark(self, ev, reads, writes):
        k, v = ev
        for b in reads:
            if b.r.get(k, 0) < v:
                b.r[k] = v
        for b in writes:
            b.w = ev
            b.r = {}

    def op(self, eng, fn, reads=(), writes=(), sig=True):
        if eng == "pe":
            self.pe_labels.append(self.label)
        self._deps(eng, reads, writes)
        if sig:
            self.cnt[eng] += 1
            ev = (eng, self.cnt[eng])
            if not self.dry:
                self.ops[eng].append(("ins", fn, self.sems[eng], 1))
            self.pending[eng] = False
        else:
            ev = (eng, self.cnt[eng] + 1)
            if not self.dry:
                self.ops[eng].append(("ins", fn, None, 0))
            self.pending[eng] = True
        self._mark(ev, reads, writes)

    def dma(self, q, ch, out, in_, reads=(), writes=(), serial=True):
        if serial and self.dcnt[ch] > 0:
            self._need(q, (ch, self.dcnt[ch]))
        self._deps(q, reads, writes)
        self.dcnt[ch] += 16
        ev = (ch, self.dcnt[ch])
        if not self.dry:
            self.ops[q].append(("ins", lambda e: e.dma_start(out=out, in_=in_), self.dsem[ch], 16))
        self._mark(ev, reads, writes)

    def emit(self):
        nc = self.nc
        for e in self.ENG:
            assert not self.pending[e], e
        with nc.Block() as block:
            def run(engname):
                def f(eng):
                    for o in self.ops[engname]:
                        if o[0] == "wait":
                            eng.wait_ge(o[1], o[2])
                        else:
                            ins = o[1](eng)
                            if o[2] is not None:
                                ins.then_inc(o[2], o[3])
                return f
            block.tensor(run("pe"))
            block.scalar(run("act"))
            block.vector(run("dve"))
            block.gpsimd(run("pool"))
            block.sync(run("sp"))


def _bc(ap, axis, shape):
    return ap.unsqueeze(axis).broadcast_to(list(shape))


def build(plan=None, n_layers=DEPTH, dry=False, stop_after=None, n_units=2):
    nc = bass.Bass("TRN2", target_bir_lowering=False)
    P = Prog(nc, dry=dry)
    rec = []

    def din(name, shape):
        return nc.dram_tensor(name, list(shape), F32, kind="ExternalInput").ap()

    def dout(name, shape):
        return nc.dram_tensor(name, list(shape), F32, kind="ExternalOutput").ap()

    NU = n_units
    xin_u = [din("xin%d" % u, [NT, 128, D]) for u in range(NU)]
    cin_u = [din("cin%d" % u, [NS, D]) for u in range(NU)]
    sret_u = [din("sret%d" % u, [DEPTH, 16, H, 128, 128]) for u in range(NU)]
    sconv_u = [din("sconv%d" % u, [DEPTH, 32, FF]) for u in range(NU)]
    w_ada = din("w_ada", [DEPTH, D, 6 * D]); b_ada = din("b_ada", [DEPTH, 6 * D])
    norms = [din(n, [DEPTH, D]) for n in ("norm_pre1", "norm_post1", "norm_pre2", "norm_post2")]
    w_in = din("w_in", [DEPTH, D, 10240])
    sgu_w_s = din("sgu_w_s", [DEPTH, 8, 128, 128]); sgu_b_s = din("sgu_b_s", [DEPTH, 8, 128])
    sgu_ln_g = din("sgu_ln_g", [DEPTH, 1024]); sgu_ln_b = din("sgu_ln_b", [DEPTH, 1024])
    ret_gn_g = din("ret_gn_g", [DEPTH, 1024])
    w_ba = din("w_branch_a", [DEPTH, 1024, D]); w_bb = din("w_branch_b", [DEPTH, 1024, D])
    w_out = din("w_out", [DEPTH, D, D])
    w_gate = din("ffn_w_gate", [DEPTH, D, FF]); w_up = din("ffn_w_up", [DEPTH, D, FF])
    conv_w = din("ffn_conv_w", [DEPTH, 3, FF]); conv_b = din("ffn_conv_b", [DEPTH, FF])
    w_down = din("ffn_w_down", [DEPTH, FF, D])
    t_cos_u = [din("tab_cos%d" % u, [128, NT, 64]) for u in range(NU)]
    t_sin_u = [din("tab_sin%d" % u, [128, NT, 64]) for u in range(NU)]
    t_dec = din("tab_dec", [128, 2, 2, 8]); t_g = din("tab_g", [128, 2, 8])
    t_maskr = din("tab_maskr", [128, 2, 128]); t_maskg = din("tab_maskg", [128, 2, 128])
    t_E = din("tab_E", [NS, NT, 128]); t_rowm = din("tab_rowmask", [128, 16])
    t_ident = din("ident", [128, 128])
    smid = nc.dram_tensor("smid", [DEPTH, 128, H, 128], F32, kind="Internal").ap()
    halo = nc.dram_tensor("halo", [DEPTH, 2, FF], F32, kind="Internal").ap()
    b_smid = [Buf("smid%d" % i) for i in range(DEPTH)]
    b_halo = [Buf("halo%d" % i) for i in range(DEPTH)]

    y_u = [dout("y%d" % u, [NT, 128, D]) for u in range(NU)]
    rs_p = dout("rs_p", [DEPTH, H, 128, 128])
    rs_s_u = [dout("rs_s%d" % u, [DEPTH, 16, H, 128, 128]) for u in range(NU)]
    cs_u = [dout("cs%d" % u, [DEPTH, 34, FF]) for u in range(NU)]
    vs_u = [dout("vs_s%d" % u, [DEPTH, 128, 1024]) for u in range(NU)]
    xcur_u = [nc.dram_tensor("xcur%d" % u, [NT, 128, D], F32, kind="Internal").ap() for u in range(NU)]
    b_xcur_u = [[Buf("xcur%d_%d" % (u, i)) for i in range(NT)] for u in range(NU)]

    es = contextlib.ExitStack()
    with es:
        def sb(name, shape, dt=F32):
            return es.enter_context(nc.sbuf_tensor(name, list(shape), dt))

        ident_f = sb("ident_f", [128, 128]); ident_b = sb("ident_b", [128, 128], BF16)
        ones_b = sb("ones_b", [128, 128], BF16)
        mhalf = sb("mhalf", [128, 16])
        epsb = sb("epsb", [128, 1])
        Etab = sb("Etab", [NS, NT, 128], BF16)
        dec = sb("dec", [128, 2, 2, 8]); gtab = sb("gtab", [128, 2, 8])
        maskr = sb("maskr", [128, 2, 128]); maskg = sb("maskg", [128, 2, 128])
        rowm = sb("rowm", [128, 16])
        csT_u = [sb("csT%d" % u, [128, KB, NS], BF16) for u in range(NU)]
        gnT = sb("gnT", [128, 8])
        MOD = [sb("mod%d" % i, [NS, D], BF16) for i in range(4)]
        b_const = Buf("const"); b_csT_u = [Buf("csT%d" % u) for u in range(NU)]; b_gnT = Buf("gnT")
        b_MOD = [Buf("mod%d" % i) for i in range(4)]
        arena = sb("arena", [128, 65536], BF16)
        b_R = {k: Buf(k) for k in ("R1", "R2a", "R2b", "R3a", "R3b", "R4")}
        R1 = arena[:, 0:18432]
        R2a = arena[:, 18432:27648]; R2b = arena[:, 27648:36864]
        R3a = arena[:, 36864:46080]; R3b = arena[:, 46080:55296]
        R4 = arena[:, 55296:65536]
        wslots = [sb("wslot%d" % i, [128, KB, PW], BF16) for i in range(NSLOT)]
        b_ws = [Buf("ws%d" % i) for i in range(NSLOT)]
        SCR = 33920
        scr = sb("scr", [128, SCR], mybir.dt.uint8)
        ps = es.enter_context(nc.psum_tensor("ps", [128, 7, 512], F32))
        ptb = es.enter_context(nc.psum_tensor("ptb", [128, 8, 128], BF16))
        b_ps = [Buf("ps%d" % i, excl=True) for i in range(7)]
        b_ptb = Buf("ptb", excl=True)

        ch_w = [P.chan("w%d" % i) for i in range(NSLOT)]
        ch_c = P.chan("const")
        ch_x = [P.chan("x0"), P.chan("x1")]
        ch_o = P.chan("out")
        ch_ms = [P.chan("misc%d" % i) for i in range(12)]
        mrot = {"i": 0}

        def mch():
            mrot["i"] = (mrot["i"] + 1) % len(ch_ms)
            return ch_ms[mrot["i"]]
        ch_s = [P.chan("s0"), P.chan("s1"), P.chan("s2")]
        ch_sb = [P.chan("sb0"), P.chan("sb1")]

        scr_state = {"bufs": [], "off": 0}

        def scratch_reset():
            old = scr_state["bufs"]
            scr_state["bufs"] = []
            scr_state["off"] = 0
            scr_state["inherit"] = old

        scr_state["inherit"] = []

        def scratch_mark():
            return (scr_state["off"], len(scr_state["bufs"]))

        def scratch_reset_to(mark):
            off, nb = mark
            scr_state["inherit"] = list(scr_state["inherit"]) + scr_state["bufs"][nb:]
            scr_state["bufs"] = scr_state["bufs"][:nb]
            scr_state["off"] = off

        def salloc(name, shape, dt=F32):
            esz = 4 if dt == F32 else 2
            n = 1
            for s in shape[1:]:
                n *= s
            nbytes = (n * esz + 31) // 32 * 32
            off = scr_state["off"]
            assert off + nbytes <= SCR, (name, off, nbytes)
            scr_state["off"] = off + nbytes
            flat = scr[0:shape[0], off:off + n * esz].bitcast(dt)
            if len(shape) == 2:
                ap = flat
            elif len(shape) == 3:
                ap = flat.rearrange("p (a b) -> p a b", a=shape[1])
            else:
                ap = flat.rearrange("p (a b c) -> p a b c", a=shape[1], b=shape[2])
            b = Buf(name, inherit=scr_state["inherit"])
            scr_state["bufs"].append(b)
            return ap, b

        def region(name_old_list, name):
            nb = Buf(name, inherit=[b_R[k] for k in name_old_list])
            for k in name_old_list:
                b_R[k] = nb
            return nb

        def mm(out, lhsT, rhs, start, stop, reads, writes, sig):
            P.op("pe", lambda e: e.matmul(out=out, lhsT=lhsT, rhs=rhs, start=start, stop=stop),
                 reads, writes, sig)

        def tr(out, in_, ident, reads, writes, sig):
            P.op("pe", lambda e: e.transpose(out=out, in_=in_, identity=ident), reads, writes, sig)

        def act(out, in_, func, reads, writes, **kw):
            P.op("act", lambda e: e.activation(out=out, in_=in_, func=func, **kw), reads, writes)

        def tt(eng, out, in0, in1, op, reads, writes):
            P.op(eng, lambda e: e.tensor_tensor(out=out, in0=in0, in1=in1, op=op), reads, writes)

        def ts(eng, out, in0, s1, s2, op0, op1, reads, writes):
            if s2 is None:
                P.op(eng, lambda e: e.tensor_scalar(out=out, in0=in0, scalar1=s1, scalar2=None, op0=op0),
                     reads, writes)
            else:
                P.op(eng, lambda e: e.tensor_scalar(out=out, in0=in0, scalar1=s1, scalar2=s2, op0=op0, op1=op1),
                     reads, writes)

        def stt(eng, out, in0, scalar, in1, op0, op1, reads, writes):
            P.op(eng, lambda e: e.scalar_tensor_tensor(out=out, in0=in0, scalar=scalar, in1=in1, op0=op0, op1=op1),
                 reads, writes)

        def cp(eng, out, in_, reads, writes):
            if eng == "act":
                P.op("act", lambda e: e.copy(out=out, in_=in_), reads, writes)
            else:
                P.op(eng, lambda e: e.tensor_copy(out=out, in_=in_), reads, writes)

        def memset(eng, ap, val, writes):
            P.op(eng, lambda e: e.memset(ap, val), (), writes)

        def rsqrt_inplace(ap, buf, shape, in1=None, b_in1=None):
            ts("dve", ap, ap, EPS, None, ALU.add, None, [buf], [buf])
            if in1 is None:
                assert shape[1] <= 16
                in1 = mhalf[0:shape[0], 0:shape[1]]
                b_in1 = b_const
            tt("pool", ap, ap, in1, ALU.pow, [buf, b_in1], [buf])

        wstate = {"i": 0, "issued": 0}

        def wview(slot, ncols):
            if ncols <= PW:
                return wslots[slot]
            return wslots[slot][:].rearrange("p a b -> p (a b)").rearrange("p (a b) -> p a b", b=ncols)

        def w_issue(idx):
            src, K, ncols = plan[idx]
            slot = idx % NSLOT
            P.dma("pool", ch_w[slot], wview(slot, ncols)[:, 0:K // 128, 0:ncols],
                  src.rearrange("(kb p) c -> p kb c", p=128), writes=[b_ws[slot]])

        def wnext(src, K, ncols=PW):
            i = wstate["i"]
            wstate["i"] = i + 1
            if plan is None:
                rec.append((src, K, ncols))
                return wview(i % NSLOT, ncols), b_ws[i % NSLOT]
            assert plan[i][1] == K and plan[i][2] == ncols
            while wstate["issued"] < min(len(plan), i + NSLOT - 1):
                w_issue(wstate["issued"])
                wstate["issued"] += 1
            return wview(i % NSLOT, ncols), b_ws[i % NSLOT]

        def load_consts():
            for dst, src in ((ident_f, t_ident), (dec, t_dec), (gtab, t_g), (maskr, t_maskr),
                             (maskg, t_maskg), (rowm, t_rowm)):
                P.dma("sp", mch(), dst[:], src, writes=[Buf("c")] if False else [b_const])
            P.dma("pool", ch_c, Etab[:], t_E, writes=[b_const])
            cp("dve", ident_b[:], ident_f[:], [b_const], [b_const])
            memset("dve", ones_b[:], 1.0 / 128.0, [b_const])
            memset("dve", mhalf[:], -0.5, [b_const])
            memset("dve", epsb[:], EPS, [b_const])

        def compute_csT(cin, csT, b_csT):
            scratch_reset()
            c_sb, b_c = salloc("c_sb", [NS, D])
            sg_sb, b_sg = salloc("sg_sb", [NS, D])
            P.dma("sp", mch(), c_sb, cin, writes=[b_c])
            act(sg_sb, c_sb, AF.Silu, [b_c], [b_sg])
            pv = ps[:, 0, 0:KB * NS].rearrange("p (a b) -> p a b", a=KB)
            for kb in range(KB):
                tr(pv[:, kb, :], sg_sb[:, kb * 128:(kb + 1) * 128], ident_f[0:NS, 0:NS],
                   [b_sg, b_const], [b_ps[0]], sig=(kb == KB - 1))
            cp("dve", csT[:], pv, [b_ps[0]], [b_csT])

        def mod_scratch():
            bq, b_bq = salloc("bq", [NS, PW]); nq, b_nq = salloc("nq", [NS, PW]); tmp, b_tmp = salloc("mtmp", [NS, PW])
            return (bq, b_bq, nq, b_nq, tmp, b_tmp)

        def mod_steps(l, m, slot, u, bank, sc):
            bq, b_bq, nq, b_nq, tmp, b_tmp = sc
            csT, b_csT = csT_u[u], b_csT_u[u]
            kind = m % 3
            for q in range(D // PW):
                prev_label = P.label
                P.label = "mod"
                c0 = m * D + q * PW
                P.dma("sp", mch(), bq, b_ada[l, c0:c0 + PW].partition_broadcast(NS), writes=[b_bq])
                if kind != 0:
                    nsrc = norms[{1: 0, 2: 1, 4: 2, 5: 3}[m]]
                    P.dma("sp", mch(), nq, nsrc[l, q * PW:(q + 1) * PW].partition_broadcast(NS), writes=[b_nq])
                wp, b_wp = wnext(w_ada[l, :, c0:c0 + PW], D)
                o = ps[0:NS, bank, 0:PW]
                for kb in range(KB):
                    mm(o, csT[:, kb, :], wp[:, kb, :], kb == 0, kb == KB - 1,
                       [b_csT, b_wp], [b_ps[bank]], sig=(kb == KB - 1))
                dst = MOD[slot][:, q * PW:(q + 1) * PW]
                if kind == 0:
                    tt("dve", dst, o, bq, ALU.add, [b_ps[bank], b_bq], [b_MOD[slot]])
                else:
                    tt("dve", tmp, o, bq, ALU.add, [b_ps[bank], b_bq], [b_tmp])
                    if kind == 1:
                        stt("dve", dst, tmp, 1.0, nq, ALU.add, ALU.mult, [b_tmp, b_nq], [b_MOD[slot]])
                    else:
                        tt("dve", dst, tmp, nq, ALU.mult, [b_tmp, b_nq], [b_MOD[slot]])
                P.label = prev_label
                yield

        def mod_piece(l, m, slot, u):
            scratch_reset()
            sc = mod_scratch()
            for _ in mod_steps(l, m, slot, u, 0, sc):
                pass

        def expand(slot, tile, half, banks):
            for j in range(2):
                c0 = half * 1024 + j * 512
                mm(ps[:, banks[j], :], Etab[:, tile, :], MOD[slot][:, c0:c0 + 512], True, True,
                   [b_const, b_MOD[slot]], [b_ps[banks[j]]], sig=True)

        def prenorm(tile, xt, b_xt, slotA, slotB, dstT, b_dst, sc):
            junk, b_junk, ss, b_ss, tmp, b_tmp, htm, b_htm = sc
            act(junk, xt, AF.Square, [b_xt], [b_junk, b_ss], scale=float(D ** -0.5), accum_out=ss)
            rsqrt_inplace(ss, b_ss, [128, 1])
            for half in range(2):
                expand(slotA, tile, half, (0, 1))
                expand(slotB, tile, half, (2, 3))
                sl = slice(half * 1024, (half + 1) * 1024)
                pa = ps[:, 0:2, :].rearrange("p a b -> p (a b)")
                pb = ps[:, 2:4, :].rearrange("p a b -> p (a b)")
                stt("dve", tmp, xt[:, sl], ss, pa, ALU.mult, ALU.mult, [b_xt, b_ss, b_ps[0], b_ps[1]], [b_tmp])
                tt("dve", htm[:, sl], tmp, pb, ALU.add, [b_tmp, b_ps[2], b_ps[3]], [b_htm])
            for g in range(2):
                for j in range(8):
                    kb = g * 8 + j
                    tr(ptb[:, j, :], htm[:, kb * 128:(kb + 1) * 128], ident_b[:], [b_htm, b_const], [b_ptb],
                       sig=(j == 7))
                cp("act", dstT[:, g * 8:(g + 1) * 8, tile * 128:(tile + 1) * 128], ptb[:], [b_ptb], [b_dst])

        def prenorm_scratch():
            junk, b_junk = salloc("junk", [128, D], BF16)
            ss, b_ss = salloc("ss", [128, 1])
            tmp, b_tmp = salloc("ptmp", [128, 1024])
            htm, b_htm = salloc("htm", [128, D], BF16)
            return (junk, b_junk, ss, b_ss, tmp, b_tmp, htm, b_htm)

        def proj_tm(srcT, b_src, tile, wp, b_wp, nkb, bank, ncols=PW):
            o = ps[:, bank, 0:ncols]
            for kb in range(nkb):
                mm(o, srcT[:, kb, tile * 128:(tile + 1) * 128], wp[:, kb, 0:ncols], kb == 0, kb == nkb - 1,
                   [b_src, b_wp], [b_ps[bank]], sig=(kb == nkb - 1))
            return o

        def proj_fm(srcT, b_src, wp, b_wp, blk, nkb, banks, groups=TG):
            outs = [ps[:, banks[gi], 0:n] for gi, (t0, n) in enumerate(groups)]
            for kb in range(nkb):
                for gi, (t0, n) in enumerate(groups):
                    last = (kb == nkb - 1)
                    mm(outs[gi], wp[:, kb, blk * 128:(blk + 1) * 128], srcT[:, kb, t0:t0 + n], kb == 0, last,
                       [b_src, b_wp], [b_ps[banks[gi]]], sig=last)
            return outs

        load_consts()
        rr = {"b": 0}

        def rot(n, mod=7):
            b = rr["b"]
            rr["b"] = (b + n) % mod
            return b

        def chk(name):
            if stop_after == name:
                raise _Stop()

        for u in range(NU):
            compute_csT(cin_u[u], csT_u[u], b_csT_u[u])
        stopped = [False]
        for l in range(n_layers):
         for u in range(NU):
          if stopped[0]:
              break
          try:
            xin, cin, sret, sconv = xin_u[u], cin_u[u], sret_u[u], sconv_u[u]
            t_cos, t_sin = t_cos_u[u], t_sin_u[u]
            y_o, rs_s, cs_o, vs_o, xcur, b_xcur = y_u[u], rs_s_u[u], cs_u[u], vs_u[u], xcur_u[u], b_xcur_u[u]
            xsrc = xin if l == 0 else xcur
            b_xsrc = [Buf("xin%d" % i) for i in range(NT)] if l == 0 else b_xcur
            P.label = "L%d.U%d.A" % (l, u)
            if l == 0 and u == 0:
                mod_piece(l, 1, 0, u)
                mod_piece(l, 0, 1, u)
            scratch_reset()
            rows, b_rows = salloc("rows", [38, 512])
            P.dma("sp", mch(), rows[0:8, 0:128], ret_gn_g[l].rearrange("(h v) -> h v", h=8), writes=[b_rows])
            tr(ps[:, 4, 0:8], rows[0:8, 0:128], ident_f[0:8, 0:8], [b_rows, b_const], [b_ps[4]], sig=True)
            cp("dve", gnT[:], ps[:, 4, 0:8], [b_ps[4]], [b_gnT])

            P.label = "L%d.U%d.B" % (l, u)
            scratch_reset()
            hT = R1.rearrange("p (a b) -> p a b", a=KB)
            b_hT = region(["R1"], "hT")
            psc = prenorm_scratch()
            xts = [salloc("xt%d" % i, [128, D]) for i in range(2)]
            for tile in range(NT):
                xt, b_xt = xts[tile % 2]
                P.dma("sp", ch_x[tile % 2], xt, xsrc[tile], reads=[b_xsrc[tile]], writes=[b_xt])
                prenorm(tile, xt, b_xt, 0, 1, hT, b_hT, psc)
            chk("B")

            P.label = "L%d.U%d.C" % (l, u)
            scratch_reset()
            cosT, b_cos = salloc("cosT", [128, NT, 64])
            sinT, b_sin = salloc("sinT", [128, NT, 64])
            P.dma("sp", mch(), cosT, t_cos, writes=[b_cos])
            P.dma("sp", mch(), sinT, t_sin, writes=[b_sin])
            ra, b_ra = salloc("ra", [128, 2, 64]); rb, b_rb = salloc("rb", [128, 2, 64])
            rt, b_rt = salloc("rt", [128, 2, 128])
            qtms = [salloc("qtm%d" % i, [128, 2, 128], BF16) for i in range(2)]
            k_tm = R2a.rearrange("p (a b) -> p a b", a=NT); b_ktm0 = region(["R2a"], "k_tm")
            b_ktm = [[Buf("k_tm_%d_%d" % (t_, p_), inherit=[b_ktm0]) for p_ in range(4)] for t_ in range(NT)]
            kT = R2b.rearrange("p (a b) -> p a b", a=H); b_kT = region(["R2b"], "kT")
            v_tm = R3a.rearrange("p (a b) -> p a b", a=NT); b_vtm = region(["R3a"], "v_tm")
            qT = R3b.rearrange("p (a b) -> p a b", a=H); b_qT = region(["R3b"], "qT")

            def rope(o, tile, qk, h0, dst, b_dstbuf):
                kind = 1 if tile == NT - 1 else 0
                pv = o.rearrange("p (h d) -> p h d", h=2)
                x1 = pv[:, :, 0:64]; x2 = pv[:, :, 64:128]
                cs = _bc(cosT[:, tile, :], 1, [128, 2, 64]); sn = _bc(sinT[:, tile, :], 1, [128, 2, 64])
                tt("dve", ra, x1, cs, ALU.mult, [pbuf[0], b_cos], [b_ra])
                tt("dve", rb, x2, sn, ALU.mult, [pbuf[0], b_sin], [b_rb])
                tt("dve", rt[:, :, 0:64], ra, rb, ALU.subtract, [b_ra, b_rb], [b_rt])
                tt("dve", ra, x1, sn, ALU.mult, [pbuf[0], b_sin], [b_ra])
                tt("dve", rb, x2, cs, ALU.mult, [pbuf[0], b_cos], [b_rb])
                tt("dve", rt[:, :, 64:128], ra, rb, ALU.add, [b_ra, b_rb], [b_rt])
                dd = _bc(dec[:, kind, qk, h0:h0 + 2], 2, [128, 2, 128])
                tt("dve", dst, rt, dd, ALU.mult, [b_rt, b_const], [b_dstbuf])

            pbuf = [None]
            pend = [None]

            def flush():
                if pend[0] is not None:
                    pend[0]()
                    pend[0] = None

            for p in range(4):
                wp, b_wp = wnext(w_in[l, :, OFF_K + p * PW: OFF_K + (p + 1) * PW], D)
                for tile in range(NT):
                    bank = rot(1)
                    o = proj_tm(hT, b_hT, tile, wp, b_wp, KB, bank)
                    pbuf[0] = b_ps[bank]
                    dst = k_tm[:, tile, p * PW:(p + 1) * PW].rearrange("p (h d) -> p h d", h=2)
                    rope(o, tile, 1, 2 * p, dst, b_ktm[tile][p])
                    flush()

                    def post(p=p, tile=tile):
                        for j in range(2):
                            tr(ptb[:, j, :], k_tm[:, tile, p * PW + j * 128: p * PW + (j + 1) * 128], ident_b[:],
                               [b_ktm[tile][p], b_const], [b_ptb], sig=(j == 1))
                        cp("act", kT[:, 2 * p:2 * p + 2, tile * 128:(tile + 1) * 128], ptb[:, 0:2, :], [b_ptb], [b_kT])
                    pend[0] = post
            flush()
            P.label = "L%d.U%d.Cv" % (l, u)
            for p in range(4):
                wp, b_wp = wnext(w_in[l, :, OFF_VR + p * PW: OFF_VR + (p + 1) * PW], D)
                for tile in range(NT):
                    bank = rot(1)
                    o = proj_tm(hT, b_hT, tile, wp, b_wp, KB, bank)
                    cp("act", v_tm[:, tile, p * PW:(p + 1) * PW], o, [b_ps[bank]], [b_vtm])
            P.label = "L%d.U%d.E" % (l, u)
            qi = 0
            for p in range(4):
                wp, b_wp = wnext(w_in[l, :, OFF_Q + p * PW: OFF_Q + (p + 1) * PW], D)
                for tile in range(NT):
                    bank = rot(1)
                    o = proj_tm(hT, b_hT, tile, wp, b_wp, KB, bank)
                    pbuf[0] = b_ps[bank]
                    qt_i, b_qt_i = qtms[qi % 2]
                    qi += 1
                    rope(o, tile, 0, 2 * p, qt_i, b_qt_i)
                    flush()

                    def post(p=p, tile=tile, qt_i=qt_i, b_qt_i=b_qt_i):
                        for j in range(2):
                            tr(ptb[:, j, :], qt_i[:, j, :], ident_b[:], [b_qt_i, b_const], [b_ptb], sig=(j == 1))
                        cp("act", qT[:, 2 * p:2 * p + 2, tile * 128:(tile + 1) * 128], ptb[:, 0:2, :], [b_ptb], [b_qT])
                    pend[0] = post
            flush()
            P.label = "L%d.U%d.F" % (l, u)
            grT = R4[:, 0:H * T].rearrange("p (a b) -> p a b", a=H); b_grT = region(["R4"], "grT")
            for p in range(4):
                wp, b_wp = wnext(w_in[l, :, OFF_GR + p * PW: OFF_GR + (p + 1) * PW], D)
                for blk in range(2):
                    banks = (0, 1, 2) if blk == 0 else (3, 4, 5)
                    outs = proj_fm(hT, b_hT, wp, b_wp, blk, KB, banks)
                    for gi, (t0, n) in enumerate(TG):
                        act(grT[:, 2 * p + blk, t0:t0 + n], outs[gi], AF.Silu, [b_ps[banks[gi]]], [b_grT])
            chk("F")

            P.label = "L%d.U%d.G" % (l, u)
            scratch_reset()
            S, b_S = salloc("S", [128, H, 128]); S_bf, b_Sbf = salloc("S_bf", [128, H, 128], BF16)
            kmk, b_kmk = salloc("kmk", [128, 1024], BF16)
            gmark = None
            import itertools
            msc = mod_scratch()
            gsteps = itertools.chain(mod_steps(l, 2, 2, u, 6, msc),
                                     mod_steps(l, 4, 0, u, 6, msc),
                                     mod_steps(l, 3, 1, u, 6, msc),
                                     mod_steps(l, 5, 3, u, 6, msc))
            gmark = scratch_mark()
            s_sb, b_ssb = salloc("s_sb", [128, H, 128], BF16)
            o_f, b_of = salloc("o_f", [128, H, 128]); o_bf, b_obf = salloc("o_bf", [128, H, 128], BF16)
            osq, b_osq = salloc("osq", [128, H, 128], BF16)
            m2, b_m2 = salloc("m2", [128, H, 128])
            s0b = [salloc("s0b%d" % i, [128, 16, 128], BF16) for i in range(2)]
            if u == 0:
                memset("dve", S, 0.0, [b_S])
            else:
                P.dma("sp", mch(), S, smid[l], reads=[b_smid[l]], writes=[b_S])
            cp("act", S_bf, S, [b_S], [b_Sbf])
            for tile in range(NT):
                kind = 1 if tile == NT - 1 else 0
                tok = slice(tile * 128, (tile + 1) * 128)
                for hb in range(2):
                    for h4 in range(4):
                        h = hb * 4 + h4
                        mm(ps[:, hb, h4 * 128:(h4 + 1) * 128], kT[:, h, tok], qT[:, h, tok], True, True,
                           [b_kT, b_qT], [b_ps[hb]], sig=(h4 == 3))
                    tt("dve", s_sb[:, hb * 4:(hb + 1) * 4, :], ps[:, hb, :].rearrange("p (a b) -> p a b", a=4),
                       _bc(maskr[:, kind, :], 1, [128, 4, 128]), ALU.mult, [b_ps[hb], b_const], [b_ssb])
                chk("Ga")
                po = ps[:, 2:4, :].rearrange("p a (h i) -> p (a h) i", h=4)
                for h in range(H):
                    bo = b_ps[2 + h // 4]
                    mm(po[:, h, :], v_tm[:, tile, h * 128:(h + 1) * 128], s_sb[:, h, :], True, False,
                       [b_vtm, b_ssb], [bo], sig=False)
                    if kind == 0:
                        mm(po[:, h, :], S_bf[:, h, :], qT[:, h, tok], False, True, [b_Sbf, b_qT], [bo], sig=True)
                    else:
                        sbt, b_sbt = s0b[h % 2]
                        P.dma("pool", ch_sb[h % 2], sbt, sret[l, :, h, :, :].rearrange("s d v -> d s v"),
                              writes=[b_sbt])
                        for s in range(16):
                            mm(po[:, h, 8 * s:8 * s + 8], sbt[:, s, :], qT[:, h, tile * 128 + 8 * s: tile * 128 + 8 * s + 8],
                               False, s == 15, [b_sbt, b_qT], [bo], sig=(s == 15))
                chk("Gb1")
                pof = ps[:, 2:4, :].rearrange("p a b -> p (a b)")
                o_f2 = o_f.rearrange("p a b -> p (a b)"); o_bf2 = o_bf.rearrange("p a b -> p (a b)")
                osq2 = osq.rearrange("p a b -> p (a b)"); m22 = m2.rearrange("p a b -> p (a b)")
                cp("act", o_f2, pof, [b_ps[2], b_ps[3]], [b_of])
                chk("Gb2")
                cp("dve", o_bf2, pof, [b_ps[2], b_ps[3]], [b_obf])
                chk("Gb3")
                act(osq2, pof, AF.Square, [b_ps[2], b_ps[3]], [b_osq])
                chk("Gb")
                if kind == 0:
                    pd = ps[:, 4:6, :].rearrange("p a (h i) -> p (a h) i", h=4)
                    for h in range(H):
                        mm(pd[:, h, :], k_tm[:, tile, h * 128:(h + 1) * 128], v_tm[:, tile, h * 128:(h + 1) * 128],
                           True, True, [b_ktm[tile][h // 2], b_vtm], [b_ps[4 + h // 4]], sig=(h % 4 == 3))
                    pdf = ps[:, 4:6, :].rearrange("p a b -> p (a b)")
                    S2 = S.rearrange("p a b -> p (a b)")
                    tt("dve", S2, S2, pdf, ALU.add, [b_S, b_ps[4], b_ps[5]], [b_S])
                    tt("dve", S, S, _bc(gtab[:, 0, :], 2, [128, H, 128]), ALU.mult, [b_S, b_const], [b_S])
                    cp("act", S_bf, S, [b_S], [b_Sbf])
                    if tile == NT - 2:
                        if u == 0 and NU > 1:
                            P.dma("sp", mch(), smid[l], S, reads=[b_S], writes=[b_smid[l]])
                        else:
                            P.dma("sp", mch(), rs_p[l].rearrange("h d v -> d h v"), S, reads=[b_S])
                chk("Gc")
                for hb in range(2):
                    mm(ps[:, hb, :], ones_b[:], o_bf2[:, hb * 512:(hb + 1) * 512], True, True, [b_const, b_obf],
                       [b_ps[hb]], sig=True)
                for hb in range(2):
                    mm(ps[:, 4 + hb, :], ones_b[:], osq2[:, hb * 512:(hb + 1) * 512], True, True, [b_const, b_osq],
                       [b_ps[4 + hb]], sig=True)
                pm = ps[:, 0:2, :].rearrange("p a b -> p (a b)")
                pq = ps[:, 4:6, :].rearrange("p a b -> p (a b)")
                act(m22, pm, AF.Square, [b_ps[0], b_ps[1]], [b_m2])
                tt("dve", m22, pq, m22, ALU.subtract, [b_ps[4], b_ps[5], b_m2], [b_m2])
                act(m22, m22, AF.Sqrt, [b_m2], [b_m2], bias=epsb[:, 0:1], scale=1.0)
                P.op("dve", lambda e: e.reciprocal(out=m22, in_=m22), [b_m2], [b_m2])
                tt("dve", o_f2, o_f2, pm, ALU.subtract, [b_of, b_ps[0], b_ps[1]], [b_of])
                tt("dve", o_f2, o_f2, m22, ALU.mult, [b_of, b_m2], [b_of])
                tt("dve", o_f, o_f, _bc(gnT[:], 2, [128, H, 128]), ALU.mult, [b_of, b_gnT], [b_of])
                tt("dve", grT[:, :, tok], o_f, grT[:, :, tok], ALU.mult, [b_of, b_grT], [b_grT])
                for _ in range(4):
                    next(gsteps, None)
                if tile == 0:
                    chk("Gd")
                if tile == 7:
                    chk("Ge")
                if kind == 1:
                    chk("Gf")
                    P.label = "L%d.U%d.Gs" % (l, u)
                    scratch_reset_to(gmark)
                    s0f = [salloc("s0f%d" % i, [128, H, 128]) for i in range(3)]
                    for s in range(16):
                        sft, b_sft = s0f[s % 3]
                        P.dma("sp", ch_s[s % 3], sft, sret[l, s].rearrange("h d v -> d h v"), writes=[b_sft])
                        ts("dve", kmk, k_tm[:, tile, :], rowm[:, s:s + 1], None, ALU.mult, None,
                           b_ktm[tile] + [b_const], [b_kmk])
                        pd = ps[:, 4:6, :].rearrange("p a (h i) -> p (a h) i", h=4)
                        for h in range(H):
                            mm(pd[:, h, :], kmk[:, h * 128:(h + 1) * 128], v_tm[:, tile, h * 128:(h + 1) * 128],
                               True, True, [b_kmk, b_vtm], [b_ps[4 + h // 4]], sig=(h % 4 == 3))
                        pdf = ps[:, 4:6, :].rearrange("p a b -> p (a b)")
                        sf2 = sft.rearrange("p a b -> p (a b)")
                        tt("dve", sf2, sf2, pdf, ALU.add, [b_sft, b_ps[4], b_ps[5]], [b_sft])
                        tt("dve", sft, sft, _bc(gtab[:, 1, :], 2, [128, H, 128]), ALU.mult, [b_sft, b_const], [b_sft])
                        P.dma("sp", ch_s[s % 3], rs_s[l, s].rearrange("h d v -> d h v"), sft, reads=[b_sft])
            for _ in gsteps:
                pass
            y_bT = grT; b_ybT = b_grT
            b_R["R2a"] = Buf("k_tm_done", inherit=[b for row in b_ktm for b in row])
            chk("G")

            P.label = "L%d.U%d.H" % (l, u)
            scratch_reset()
            lnG, b_lnG = salloc("lnG", [128, 1024]); lnB, b_lnB = salloc("lnB", [128, 1024])
            WspT, b_Wsp = salloc("WspT", [128, 2, 8, 128], BF16)
            bsP, b_bsP = salloc("bsP", [128, 8, 128]); bs8, b_bs8 = salloc("bs8", [128, 8, 8])
            hmark = scratch_mark()
            wtmp, b_wtmp = salloc("wtmp", [128, 8, 128]); wmk, b_wmk = salloc("wmk", [128, 8, 128], BF16)
            rep, b_rep = salloc("rep", [128, 8, 8])
            P.dma("sp", mch(), lnG, sgu_ln_g[l, :].partition_broadcast(128), writes=[b_lnG])
            P.dma("sp", mch(), lnB, sgu_ln_b[l, :].partition_broadcast(128), writes=[b_lnB])
            P.dma("sp", mch(), bsP, sgu_b_s[l].partition_broadcast(128), writes=[b_bsP])
            P.dma("sp", mch(), bs8, sgu_b_s[l, :, 0:8].partition_broadcast(128), writes=[b_bs8])
            P.dma("sp", mch(), wtmp, sgu_w_s[l].rearrange("g t s -> t g s"), writes=[b_wtmp])
            for s in range(16):
                P.dma("sp", mch(), rep[8 * s:8 * s + 8, :, :], sgu_w_s[l, :, 0:8, 0:8].rearrange("g t s -> t g s"),
                      writes=[b_rep])
            tt("dve", wmk, wtmp, _bc(maskg[:, 0, :], 1, [128, 8, 128]), ALU.mult, [b_wtmp, b_const], [b_wmk])
            for g in range(8):
                tr(ptb[:, g, :], wmk[:, g, :], ident_b[:], [b_wmk, b_const], [b_ptb], sig=(g == 7))
            cp("act", WspT[:, 0, :, :], ptb[:], [b_ptb], [b_Wsp])
            wmk4 = wmk.rearrange("p g (a b) -> p g a b", a=16)
            in0 = rep.unsqueeze(2).broadcast_to([128, 8, 16, 8])
            in1 = maskg[:, 1, :].rearrange("p (a b) -> p a b", a=16).unsqueeze(1).broadcast_to([128, 8, 16, 8])
            tt("dve", wmk4, in0, in1, ALU.mult, [b_rep, b_const], [b_wmk])
            for g in range(8):
                tr(ptb[:, g, :], wmk[:, g, :], ident_b[:], [b_wmk, b_const], [b_ptb], sig=(g == 7))
            cp("act", WspT[:, 1, :, :], ptb[:], [b_ptb], [b_Wsp])
            scratch_reset_to(hmark)
            vn, b_vn = salloc("vn", [128, 1024])
            sums, b_sums = salloc("sums", [128, NT, 4]); sqs, b_sqs = salloc("sqs", [128, NT, 4])
            mean, b_mean = salloc("mean", [128, NT]); rstd, b_rstd = salloc("rstdv", [128, NT])
            jk, b_jk = salloc("jk", [128, PW], BF16)
            ug, b_ug = salloc("ug", [128, T], BF16)
            ztmp, b_ztmp = salloc("ztmp", [128, T])
            vs_tm = R2a.rearrange("p (a b) -> p a b", a=NT); b_vs = region(["R2a"], "vs_tm")
            y_aT = R2b.rearrange("p (a b) -> p a b", a=H); b_yaT = region(["R2b"], "y_aT")
            for p in range(4):
                wp, b_wp = wnext(w_in[l, :, OFF_V + p * PW: OFF_V + (p + 1) * PW], D)
                for tile in range(NT):
                    bank = rot(1)
                    o = proj_tm(hT, b_hT, tile, wp, b_wp, KB, bank)
                    dstv = vs_tm[:, tile, p * PW:(p + 1) * PW]
                    act(dstv, o, AF.Gelu_apprx_tanh, [b_ps[bank]], [b_vs, b_sums], accum_out=sums[:, tile, p:p + 1])
                    act(jk, dstv, AF.Square, [b_vs], [b_jk, b_sqs], accum_out=sqs[:, tile, p:p + 1])
            P.op("dve", lambda e: e.reduce_sum(out=mean, in_=sums, axis=AX.X), [b_sums], [b_mean])
            P.op("dve", lambda e: e.reduce_sum(out=rstd, in_=sqs, axis=AX.X), [b_sqs], [b_rstd])
            ts("dve", mean, mean, 1.0 / 1024, None, ALU.mult, None, [b_mean], [b_mean])
            ts("dve", rstd, rstd, 1.0 / 1024, None, ALU.mult, None, [b_rstd], [b_rstd])
            msq, b_msq = salloc("msq", [128, NT])
            tt("dve", msq, mean, mean, ALU.mult, [b_mean], [b_msq])
            tt("dve", rstd, rstd, msq, ALU.subtract, [b_rstd, b_msq], [b_rstd])
            rsqrt_inplace(rstd, b_rstd, [128, NT])
            for tile in range(NT):
                ts("dve", vn, vs_tm[:, tile, :], mean[:, tile:tile + 1], rstd[:, tile:tile + 1], ALU.subtract, ALU.mult,
                   [b_vs, b_mean, b_rstd], [b_vn])
                tt("dve", vn, vn, lnG, ALU.mult, [b_vn, b_lnG], [b_vn])
                tt("dve", vn, vn, lnB, ALU.add, [b_vn, b_lnB], [b_vn])
                cp("act", vs_tm[:, tile, :], vn, [b_vn], [b_vs])
                if tile == NT - 1:
                    P.dma("sp", mch(), vs_o[l], vn, reads=[b_vn])
            for p in range(4):
                wp, b_wp = wnext(w_in[l, :, OFF_U + p * PW: OFF_U + (p + 1) * PW], D)
                for blk in range(2):
                    g = 2 * p + blk
                    outs = proj_fm(hT, b_hT, wp, b_wp, blk, KB, (0, 1, 2))
                    for gi, (t0, n) in enumerate(TG):
                        act(ug[:, t0:t0 + n], outs[gi], AF.Gelu_apprx_tanh, [b_ps[gi]], [b_ug])
                    for tile in range(NT):
                        kind = 1 if tile == NT - 1 else 0
                        bank = 3 + tile // 4
                        c0 = (tile % 4) * 128
                        mm(ps[:, bank, c0:c0 + 128], vs_tm[:, tile, g * 128:(g + 1) * 128], WspT[:, kind, g, :],
                           True, True, [b_vs, b_Wsp], [b_ps[bank]], sig=(tile % 4 == 3 or tile == NT - 1))
                    for bq in range(2):
                        tt("dve", ztmp[:, bq * 512:(bq + 1) * 512].rearrange("p (a b) -> p a b", a=4),
                           ps[:, 3 + bq, :].rearrange("p (a b) -> p a b", a=4),
                           _bc(bsP[:, g, :], 1, [128, 4, 128]), ALU.add, [b_ps[3 + bq], b_bsP], [b_ztmp])
                    tt("dve", ztmp[:, 1024:1152].rearrange("p (a b) -> p a b", a=16),
                       ps[:, 5, 0:128].rearrange("p (a b) -> p a b", a=16),
                       _bc(bs8[:, g, :], 1, [128, 16, 8]), ALU.add, [b_ps[5], b_bs8], [b_ztmp])
                    tt("dve", y_aT[:, g, :], ztmp, ug, ALU.mult, [b_ztmp, b_ug], [b_yaT])
            chk("H")

            P.label = "L%d.U%d.I" % (l, u)
            scratch_reset()
            sg1, b_sg1 = salloc("sg1", [128, 2, T], BF16)
            sg2, b_sg2 = salloc("sg2", [128, 2, T], BF16)
            mergedT = arena[:, 36864:55296].rearrange("p (a b) -> p a b", a=KB)
            b_mg = region(["R3a", "R3b"], "mergedT")
            for p in range(8):
                wp, b_wp = wnext(w_in[l, :, OFF_GA + p * PW: OFF_GA + (p + 1) * PW], D)
                for blk in range(2):
                    banks = (0, 1, 2) if blk == 0 else (3, 4, 5)
                    outs = proj_fm(hT, b_hT, wp, b_wp, blk, KB, banks)
                    for gi, (t0, n) in enumerate(TG):
                        act(sg1[:, blk, t0:t0 + n], outs[gi], AF.Sigmoid, [b_ps[banks[gi]]], [b_sg1])
                wp, b_wp = wnext(w_ba[l, :, p * PW:(p + 1) * PW], 1024)
                for blk in range(2):
                    banks = (0, 1, 2) if blk == 0 else (3, 4, 5)
                    outs = proj_fm(y_aT, b_yaT, wp, b_wp, blk, 8, banks)
                    for gi, (t0, n) in enumerate(TG):
                        tt("dve", sg1[:, blk, t0:t0 + n], sg1[:, blk, t0:t0 + n], outs[gi], ALU.mult,
                           [b_sg1, b_ps[banks[gi]]], [b_sg1])
                wp, b_wp = wnext(w_in[l, :, OFF_GB + p * PW: OFF_GB + (p + 1) * PW], D)
                for blk in range(2):
                    banks = (0, 1, 2) if blk == 0 else (3, 4, 5)
                    outs = proj_fm(hT, b_hT, wp, b_wp, blk, KB, banks)
                    for gi, (t0, n) in enumerate(TG):
                        act(sg2[:, blk, t0:t0 + n], outs[gi], AF.Sigmoid, [b_ps[banks[gi]]], [b_sg2])
                wp, b_wp = wnext(w_bb[l, :, p * PW:(p + 1) * PW], 1024)
                for blk in range(2):
                    banks = (0, 1, 2) if blk == 0 else (3, 4, 5)
                    outs = proj_fm(y_bT, b_ybT, wp, b_wp, blk, 8, banks)
                    for gi, (t0, n) in enumerate(TG):
                        tt("dve", sg2[:, blk, t0:t0 + n], sg2[:, blk, t0:t0 + n], outs[gi], ALU.mult,
                           [b_sg2, b_ps[banks[gi]]], [b_sg2])
                    tt("pool", mergedT[:, 2 * p + blk, :], sg1[:, blk, :], sg2[:, blk, :], ALU.add,
                       [b_sg1, b_sg2], [b_mg])
            chk("I")

            P.label = "L%d.U%d.J" % (l, u)
            scratch_reset()
            t_st = arena[:, 18432:36864].rearrange("p (a b) -> p a b", a=NT)
            b_tst = region(["R2a", "R2b"], "t_store")
            h2T = R1.rearrange("p (a b) -> p a b", a=KB)
            b_h2T = region(["R1"], "h2T")
            ssq, b_ssq = salloc("ssq", [128, NT, 8]); rs1, b_rs1 = salloc("rs1", [128, NT])
            jk, b_jk = salloc("jk2", [128, PW], BF16)
            psc = prenorm_scratch()
            xts = [salloc("xtj%d" % i, [128, D]) for i in range(2)]
            tmpj, b_tmpj = psc[4], psc[5]
            for p in range(8):
                wp, b_wp = wnext(w_out[l, :, p * PW:(p + 1) * PW], D)
                for tile in range(NT):
                    bank = 4 + rot(1) % 3
                    o = proj_tm(mergedT, b_mg, tile, wp, b_wp, KB, bank)
                    cp("dve", t_st[:, tile, p * PW:(p + 1) * PW], o, [b_ps[bank]], [b_tst])
                    act(jk, o, AF.Square, [b_ps[bank]], [b_jk, b_ssq], accum_out=ssq[:, tile, p:p + 1])
            P.op("dve", lambda e: e.reduce_sum(out=rs1, in_=ssq, axis=AX.X), [b_ssq], [b_rs1])
            ts("dve", rs1, rs1, 1.0 / D, None, ALU.mult, None, [b_rs1], [b_rs1])
            rsqrt_inplace(rs1, b_rs1, [128, NT])
            for tile in range(NT):
                xt, b_xt = xts[tile % 2]
                P.dma("sp", ch_x[tile % 2], xt, xsrc[tile], reads=[b_xsrc[tile]], writes=[b_xt])
                for half in range(2):
                    expand(2, tile, half, (0, 1))
                    sl = slice(half * 1024, (half + 1) * 1024)
                    pa = ps[:, 0:2, :].rearrange("p a b -> p (a b)")
                    stt("dve", tmpj, t_st[:, tile, sl], rs1[:, tile:tile + 1], pa, ALU.mult, ALU.mult,
                        [b_tst, b_rs1, b_ps[0], b_ps[1]], [b_tmpj])
                    tt("pool", xt[:, sl], xt[:, sl], tmpj, ALU.add, [b_xt, b_tmpj], [b_xt])
                P.dma("sp", ch_x[tile % 2], xcur[tile], xt, reads=[b_xt], writes=[b_xcur[tile]])
                prenorm(tile, xt, b_xt, 0, 1, h2T, b_h2T, psc)
            chk("J")

            P.label = "L%d.U%d.L" % (l, u)
            scratch_reset()
            ctab, b_ctab = salloc("ctab", [128, FB, 38])
            gsave, b_gsave = salloc("gsave", [128, FB, 2])
            csave, b_csave = salloc("csave", [128, FB, 34])
            lsc = mod_scratch()
            nxt = (l, u + 1) if u + 1 < NU else ((l + 1, 0) if l + 1 < n_layers else None)
            if nxt is not None:
                import itertools
                lsteps = itertools.chain(mod_steps(nxt[0], 1, 0, nxt[1], 6, lsc), mod_steps(nxt[0], 0, 1, nxt[1], 6, lsc))
            else:
                lsteps = iter(())
            lmark = scratch_mark()
            rows2 = [salloc("rows%d" % i, [38, 512]) for i in range(2)]
            b_rparts = [[Buf("rp%d_%d" % (i, j), inherit=[rows2[i][1]]) for j in range(4)] for i in range(2)]
            for cchunk in range(FF // 512):
                c0 = cchunk * 512
                rows = rows2[cchunk % 2][0]
                brp = b_rparts[cchunk % 2]
                P.dma("sp", mch(), rows[0:32, :], sconv[l, :, c0:c0 + 512], writes=[brp[0]])
                if u == 0:
                    memset("dve", rows[32:34, :], 0.0, [brp[1]])
                else:
                    P.dma("sp", mch(), rows[32:34, :], halo[l, :, c0:c0 + 512], reads=[b_halo[l]], writes=[brp[1]])
                P.dma("sp", mch(), rows[34:37, :], conv_w[l, :, c0:c0 + 512], writes=[brp[2]])
                P.dma("sp", mch(), rows[37:38, :], conv_b[l:l + 1, c0:c0 + 512], writes=[brp[3]])
                pv = ps[:, 4 + cchunk % 2, 0:4 * 38].rearrange("p (a b) -> p a b", a=4)
                for j in range(4):
                    tr(pv[:, j, :], rows[:, j * 128:(j + 1) * 128], ident_f[0:38, 0:38], brp + [b_const],
                       [b_ps[4 + cchunk % 2]], sig=(j == 3))
                cp("dve", ctab[:, cchunk * 4:(cchunk + 1) * 4, :], pv, [b_ps[4 + cchunk % 2]], [b_ctab])
            yT = arena[:, 18432:18432 + FB * 640].rearrange("p (a b) -> p a b", a=FB)
            b_yT = region(["R2a", "R2b", "R3a", "R3b"], "yT")
            f_st = R4.rearrange("p (a b) -> p a b", a=5)
            b_fst = region(["R4"], "f_store")
            batches = [(0, 5), (5, 9)]
            for bi, (t_lo, t_hi) in enumerate(batches):
                ntile = t_hi - t_lo
                tok0 = t_lo * 128
                ntok = ntile * 128
                groups = [(tok0, 320), (tok0 + 320, 320)] if bi == 0 else [(tok0, 512)]
                ng = len(groups)
                P.label = "L%d.U%d.Lgu%d" % (l, u, bi)
                scratch_reset_to(lmark)
                gsets = [(salloc("gext%d" % i, [128, 642]), salloc("gexs%d" % i, [128, 16, 10]),
                          salloc("acc%d" % i, [128, 640]), salloc("ge%d" % i, [128, 640], BF16),
                          salloc("upsb%d" % i, [128, 640], BF16)) for i in range(2)]
                for p in range(FB // 2):
                    next(lsteps, None)
                    wg, b_wg = wnext(w_gate[l, :, p * PW:(p + 1) * PW], D)
                    wu, b_wu = wnext(w_up[l, :, p * PW:(p + 1) * PW], D)
                    for blk in range(2):
                        fb = 2 * p + blk
                        (gext, b_gext), (gexs, b_gexs), (acc, b_acc), (ge, b_ge), (upsb, b_upsb) = gsets[fb % 2]
                        if bi == 0:
                            gb = (0, 1) if blk == 0 else (4, 5)
                            ub = (2, 3) if blk == 0 else (6, 3)
                        else:
                            gb = (2 * (fb % 3),)
                            ub = (2 * (fb % 3) + 1,)
                        og = proj_fm(h2T, b_h2T, wg, b_wg, blk, KB, gb, groups)
                        ou = proj_fm(h2T, b_h2T, wu, b_wu, blk, KB, ub, groups)
                        for gi, (t0, n) in enumerate(groups):
                            cp("act", upsb[:, t0 - tok0:t0 - tok0 + n], ou[gi], [b_ps[ub[gi]]], [b_upsb])
                        w0 = ctab[:, fb, 34:35]; w1 = ctab[:, fb, 35:36]; w2 = ctab[:, fb, 36:37]; cb = ctab[:, fb, 37:38]
                        if bi == 0:
                            cp("dve", gext[:, 0:2], ctab[:, fb, 32:34], [b_ctab], [b_gext])
                            for gi in range(ng):
                                cp("act", gext[:, 2 + gi * 320: 2 + (gi + 1) * 320], og[gi], [b_ps[gb[gi]]], [b_gext])
                            cp("dve", gsave[:, fb, :], gext[:, 640:642], [b_gext], [b_gsave])
                            npr = 640
                        else:
                            cp("dve", gext[:, 0:2], gsave[:, fb, :], [b_gsave], [b_gext])
                            cp("act", gext[:, 2:386], og[0][:, 0:384], [b_ps[gb[0]]], [b_gext])
                            cp("dve", gexs[:, :, 0:2], ctab[:, fb, 0:32].rearrange("p (a b) -> p a b", a=16),
                               [b_ctab], [b_gexs])
                            cp("act", gexs[:, :, 2:10], og[0][:, 384:512].rearrange("p (a b) -> p a b", a=16),
                               [b_ps[gb[0]]], [b_gexs])
                            cp("dve", csave[:, fb, 32:34], gext[:, 384:386], [b_gext], [b_csave])
                            cp("dve", csave[:, fb, 0:32].rearrange("p (a b) -> p a b", a=16), gexs[:, :, 8:10],
                               [b_gexs], [b_csave])
                            npr = 384
                        act(acc[:, 0:npr], gext[:, 2:2 + npr], AF.Identity, [b_gext, b_ctab], [b_acc], scale=w2, bias=cb)
                        stt("dve", acc[:, 0:npr], gext[:, 1:1 + npr], w1, acc[:, 0:npr], ALU.mult, ALU.add,
                            [b_gext, b_ctab, b_acc], [b_acc])
                        stt("dve", acc[:, 0:npr], gext[:, 0:npr], w0, acc[:, 0:npr], ALU.mult, ALU.add,
                            [b_gext, b_ctab, b_acc], [b_acc])
                        if bi == 1:
                            a3 = acc[:, 384:512].rearrange("p (a b) -> p a b", a=16)
                            act(a3, gexs[:, :, 2:10], AF.Identity, [b_gexs, b_ctab], [b_acc], scale=w2, bias=cb)
                            stt("dve", a3, gexs[:, :, 1:9], w1, a3, ALU.mult, ALU.add, [b_gexs, b_ctab, b_acc], [b_acc])
                            stt("dve", a3, gexs[:, :, 0:8], w0, a3, ALU.mult, ALU.add, [b_gexs, b_ctab, b_acc], [b_acc])
                        act(ge[:, 0:ntok], acc[:, 0:ntok], AF.Gelu_apprx_tanh, [b_acc], [b_ge])
                        for gi, (t0, n) in enumerate(groups):
                            tt("dve", yT[:, fb, t0 - tok0:t0 - tok0 + n], ge[:, t0 - tok0:t0 - tok0 + n],
                               upsb[:, t0 - tok0:t0 - tok0 + n], ALU.mult, [b_ge, b_upsb], [b_yT])
                scratch_reset_to(lmark)
                fsq, b_fsq = salloc("fsq", [128, 5, 4]); rs2, b_rs2 = salloc("rs2", [128, 5])
                jk, b_jk = salloc("jk3", [128, 512], BF16)
                tmpl, b_tmpl = salloc("tmpl", [128, 1024])
                crow, b_crow = salloc("crow", [34, 512])
                xts = [salloc("xtl0", [128, D])] * 2
                if bi == 1:
                    for cchunk in range(FF // 512):
                        for j in range(4):
                            tr(ps[0:34, 6, j * 128:(j + 1) * 128], csave[:, cchunk * 4 + j, :], ident_f[:],
                               [b_csave, b_const], [b_ps[6]], sig=(j == 3))
                        cp("act", crow, ps[0:34, 6, :], [b_ps[6]], [b_crow])
                        P.dma("sp", ch_o, cs_o[l, :, cchunk * 512:(cchunk + 1) * 512], crow, reads=[b_crow])
                        if u == 0 and NU > 1:
                            P.dma("sp", mch(), halo[l, :, cchunk * 512:(cchunk + 1) * 512], crow[32:34, :],
                                  reads=[b_crow], writes=[b_halo[l]])
                P.label = "L%d.U%d.Ldn%d" % (l, u, bi)
                kgs = [(k0, min(8, FB - k0)) for k0 in range(0, FB, 8)]
                for q in range(D // 512):
                    for (k0, nk) in kgs:
                        wp, b_wp = wnext(w_down[l, k0 * 128:(k0 + nk) * 128, q * 512:(q + 1) * 512], nk * 128, 512)
                        for kk in range(nk):
                            kb = k0 + kk
                            for ti in range(ntile):
                                last = (kb == FB - 1)
                                mm(ps[:, ti, :], yT[:, kb, ti * 128:(ti + 1) * 128], wp[:, kk, :], kb == 0, last,
                                   [b_yT, b_wp], [b_ps[ti]], sig=(last or (kk == nk - 1 and ti == ntile - 1)))
                    for ti in range(ntile):
                        cp("dve", f_st[:, ti, q * 512:(q + 1) * 512], ps[:, ti, :], [b_ps[ti]], [b_fst])
                        act(jk, ps[:, ti, :], AF.Square, [b_ps[ti]], [b_jk, b_fsq], accum_out=fsq[:, ti, q:q + 1])
                P.label = "L%d.U%d.Lrs%d" % (l, u, bi)
                P.op("dve", lambda e: e.reduce_sum(out=rs2, in_=fsq, axis=AX.X), [b_fsq], [b_rs2])
                ts("dve", rs2, rs2, 1.0 / D, None, ALU.mult, None, [b_rs2], [b_rs2])
                rsqrt_inplace(rs2, b_rs2, [128, 5])
                for ti in range(ntile):
                    tile = t_lo + ti
                    xt, b_xt = xts[tile % 2]
                    P.dma("sp", ch_x[tile % 2], xt, xcur[tile], reads=[b_xcur[tile]], writes=[b_xt])
                    for half in range(2):
                        expand(3, tile, half, (5, 6))
                        sl = slice(half * 1024, (half + 1) * 1024)
                        pa = ps[:, 5:7, :].rearrange("p a b -> p (a b)")
                        stt("dve", tmpl, f_st[:, ti, sl], rs2[:, ti:ti + 1], pa, ALU.mult, ALU.mult,
                            [b_fst, b_rs2, b_ps[5], b_ps[6]], [b_tmpl])
                        tt("pool", xt[:, sl], xt[:, sl], tmpl, ALU.add, [b_xt, b_tmpl], [b_xt])
                    if l == n_layers - 1:
                        P.dma("sp", ch_x[tile % 2], y_o[tile], xt, reads=[b_xt])
                    else:
                        P.dma("sp", ch_x[tile % 2], xcur[tile], xt, reads=[b_xt], writes=[b_xcur[tile]])

          except _Stop:
            stopped[0] = True
        if stop_after is not None:
            dbg = dout("dbg", [128, 65536])
            dbgx = dout("dbgx", [NT, 128, D])
            allb = list({id(b): b for b in b_R.values()}.values())
            for i in range(4):
                P.dma("pool", mch(), dbg[:, i * 16384:(i + 1) * 16384], arena[:, i * 16384:(i + 1) * 16384], reads=allb)
            for i in range(NT):
                P.dma("sp", mch(), dbgx[i], xcur_u[0][i], reads=[b_xcur_u[0][i]])
        for ch in range(len(P.dcnt)):
            if P.dcnt[ch] > 0:
                P._need("sp", (ch, P.dcnt[ch]))
        if not dry:
            import os
            if os.environ.get("KDBG_LABELS"):
                import json
                json.dump(P.pe_labels, open(os.environ["KDBG_LABELS"], "w"))
            P.emit()
    return nc, rec


def _tables(half):
    hh = np.arange(8, dtype=np.float64)
    g = 1.0 - np.exp2(-5.0 - hh)
    inv = 10000.0 ** (-np.arange(64, dtype=np.float32) / np.float32(64))
    p = np.arange(128)
    cos = np.zeros((128, NT, 64), np.float32); sin = np.zeros((128, NT, 64), np.float32)
    for t in range(NT):
        pos = (half * 1024 + t * 128 + p) if t < NT - 1 else (16384 + p % 8)
        ang = pos.astype(np.float32)[:, None] * inv[None, :].astype(np.float32)
        cos[:, t] = np.cos(ang); sin[:, t] = np.sin(ang)
    dec = np.zeros((128, 2, 2, 8), np.float32)
    for kind in range(2):
        i = p if kind == 0 else p % 8
        dec[:, kind, 0, :] = g[None, :] ** (i[:, None] + 1.0)
        dec[:, kind, 1, :] = g[None, :] ** (-(i[:, None] + 1.0)) * (128.0 ** -0.5)
    gt = np.zeros((128, 2, 8), np.float32)
    gt[:, 0, :] = g ** 128.0
    gt[:, 1, :] = g ** 8.0
    jj = p[:, None]; ii = p[None, :]
    maskr = np.zeros((128, 2, 128), np.float32)
    maskr[:, 0, :] = (jj <= ii)
    maskr[:, 1, :] = (jj <= ii) & (jj // 8 == ii // 8)
    maskg = np.zeros((128, 2, 128), np.float32)
    maskg[:, 0, :] = (jj >= ii)
    maskg[:, 1, :] = (jj // 8 == ii // 8) & (ii % 8 <= jj % 8)
    E = np.zeros((NS, NT, 128), np.float32)
    E[0, 0:NT - 1, :] = 1.0
    for s in range(16):
        E[1 + s, NT - 1, 8 * s:8 * s + 8] = 1.0
    rowm = np.zeros((128, 16), np.float32)
    for s in range(16):
        rowm[8 * s:8 * s + 8, s] = 1.0
    return dict(tab_cos=cos, tab_sin=sin, tab_dec=dec, tab_g=gt, tab_maskr=maskr, tab_maskg=maskg,
                tab_E=E, tab_rowmask=rowm, ident=np.eye(128, dtype=np.float32))


_WNAMES = ["w_ada", "b_ada", "norm_pre1", "norm_post1", "norm_pre2", "norm_post2", "w_in", "sgu_w_s", "sgu_b_s",
           "sgu_ln_g", "sgu_ln_b", "ret_gn_g", "w_branch_a", "w_branch_b", "w_out", "ffn_w_gate", "ffn_w_up",
           "ffn_conv_w", "ffn_conv_b", "ffn_w_down"]


def core_inputs(c, inp):
    b = c % 4
    m = {}
    xp_all = np.asarray(inp["x_prompt"]); xs_all = np.asarray(inp["x_sample"])
    for u in range(2):
        s0 = 32 * b + 16 * u
        xs = xs_all[s0:s0 + 16].reshape(1, 128, D)
        xp = xp_all[b, u * 1024:(u + 1) * 1024].reshape(8, 128, D)
        m["xin%d" % u] = np.ascontiguousarray(np.concatenate([xp, xs], 0))
        m["cin%d" % u] = np.ascontiguousarray(np.concatenate([np.asarray(inp["c_prompt"])[b:b + 1],
                                                              np.asarray(inp["c_sample"])[s0:s0 + 16]], 0))
        m["sret%d" % u] = np.ascontiguousarray(np.asarray(inp["state_ret"])[:, s0:s0 + 16])
        m["sconv%d" % u] = np.ascontiguousarray(np.asarray(inp["state_conv"])[:, s0:s0 + 16].reshape(DEPTH, 32, FF))
        tb = _tables(u)
        m["tab_cos%d" % u] = tb["tab_cos"]; m["tab_sin%d" % u] = tb["tab_sin"]
    for n in _WNAMES:
        m[n] = np.asarray(inp[n])
    tb = _tables(0)
    for k in ("tab_dec", "tab_g", "tab_maskr", "tab_maskg", "tab_E", "tab_rowmask", "ident"):
        m[k] = tb[k]
    return m


_CACHE = {}


def get_nc():
    if "nc" not in _CACHE:
        _, rec = build(plan=None, dry=True)
        nc, _ = build(plan=rec)
        _CACHE["nc"] = nc
    return _CACHE["nc"]


def kernel(**inputs):
    nc = get_nc()
    active = [0, 1, 4, 5]
    base = [core_inputs(b, inputs) for b in range(4)]
    zero = {k: np.zeros_like(v) for k, v in base[0].items()}
    in_maps = [zero] * 8
    in_maps = list(in_maps)
    for b, c in enumerate(active):
        in_maps[c] = base[b]
    res = run_bass_kernel_spmd(nc, in_maps, core_ids=list(range(8)))
    r = res.results
    B, S = 4, 2048
    y_prompt = np.zeros((B, S, D), np.float32)
    y_sample = np.zeros((128, 8, D), np.float32)
    ret_p = np.zeros((DEPTH, B, H, 128, 128), np.float32)
    ret_s = np.zeros((DEPTH, 128, H, 128, 128), np.float32)
    conv_p = np.zeros((DEPTH, B, 2, FF), np.float32)
    conv_s = np.zeros((DEPTH, 128, 2, FF), np.float32)
    v_s = np.zeros((DEPTH, 128, 8, 1024), np.float32)
    for b, c in enumerate(active):
        for u in range(2):
            s0 = 32 * b + 16 * u
            y = r[c]["y%d" % u]
            y_prompt[b, u * 1024:(u + 1) * 1024] = y[0:8].reshape(1024, D)
            y_sample[s0:s0 + 16] = y[8].reshape(16, 8, D)
            ret_s[:, s0:s0 + 16] = r[c]["rs_s%d" % u]
            conv_s[:, s0:s0 + 16] = r[c]["cs%d" % u][:, 0:32].reshape(DEPTH, 16, 2, FF)
            v_s[:, s0:s0 + 16] = r[c]["vs_s%d" % u].reshape(DEPTH, 16, 8, 1024)
        ret_p[:, b] = r[c]["rs_p"]
        conv_p[:, b] = r[c]["cs1"][:, 32:34]
    return (y_prompt, y_sample, ret_p, ret_s, conv_p, conv_s, v_s)
```

```python
import contextlib
import numpy as np
import concourse.bass as bass
import concourse.mybir as mybir
from concourse.bass_utils import run_bass_kernel_spmd

F32 = mybir.dt.float32
BF16 = mybir.dt.bfloat16
ALU = mybir.AluOpType
AF = mybir.ActivationFunctionType
AX = mybir.AxisListType

D = 2048
KB = 16
NT = 9
T = NT * 128
H = 8
FF = 5632
FB = 44
NS = 17
DEPTH = 2
EPS = 1e-6
PW = 256
NSLOT = 3
OFF_U, OFF_V, OFF_Q, OFF_K, OFF_VR, OFF_GR, OFF_GA, OFF_GB = 0, 1024, 2048, 3072, 4096, 5120, 6144, 8192
TG = [(0, 384), (384, 384), (768, 384)]


class _Stop(Exception):
    pass


class Buf:
    __slots__ = ("name", "w", "r", "excl")

    def __init__(self, name, inherit=(), excl=False):
        self.name = name
        self.w = None
        self.r = {}
        self.excl = excl
        for b in inherit:
            if b.w is not None:
                k, v = b.w
                if self.r.get(k, 0) < v:
                    self.r[k] = v
            for k, v in b.r.items():
                if self.r.get(k, 0) < v:
                    self.r[k] = v


class Prog:
    ENG = ("pe", "act", "dve", "pool", "sp")

    def __init__(self, nc, dry=False):
        self.nc = nc
        self.dry = dry
        self.ops = {e: [] for e in self.ENG}
        self.cnt = {e: 0 for e in self.ENG}
        self.seen = {e: {} for e in self.ENG}
        self.pending = {e: False for e in self.ENG}
        self.label = "init"
        self.pe_labels = []
        self.dsem = []
        self.dcnt = []
        if not dry:
            self.sems = {e: nc.alloc_semaphore("s_" + e) for e in self.ENG}

    def chan(self, name):
        self.dsem.append(None if self.dry else self.nc.alloc_semaphore("d_" + name))
        self.dcnt.append(0)
        return len(self.dsem) - 1

    def _need(self, eng, ev):
        key, val = ev
        if self.seen[eng].get(key, 0) >= val:
            return
        self.seen[eng][key] = val
        if self.dry:
            return
        sem = self.sems[key] if isinstance(key, str) else self.dsem[key]
        self.ops[eng].append(("wait", sem, val))

    def _deps(self, eng, reads, writes):
        for b in reads:
            if b.w is not None:
                if not (b.w[0] == eng and eng == "pe"):
                    self._need(eng, b.w)
            if b.excl:
                for k, v in b.r.items():
                    if k != eng:
                        self._need(eng, (k, v))
        for b in writes:
            if b.w is not None and b.w[0] != eng:
                self._need(eng, b.w)
            for k, v in b.r.items():
                if k != eng:
                    self._need(eng, (k, v))

    def _mark(self, ev, reads, writes):
        k, v = ev
        for b in reads:
            if b.r.get(k, 0) < v:
                b.r[k] = v
        for b in writes:
            b.w = ev
            b.r = {}

    def op(self, eng, fn, reads=(), writes=(), sig=True):
        if eng == "pe":
            self.pe_labels.append(self.label)
        self._deps(eng, reads, writes)
        if sig:
            self.cnt[eng] += 1
            ev = (eng, self.cnt[eng])
            if not self.dry:
                self.ops[eng].append(("ins", fn, self.sems[eng], 1))
            self.pending[eng] = False
        else:
            ev = (eng, self.cnt[eng] + 1)
            if not self.dry:
                self.ops[eng].append(("ins", fn, None, 0))
            self.pending[eng] = True
        self._mark(ev, reads, writes)

    def dma(self, q, ch, out, in_, reads=(), writes=(), serial=True):
        if serial and self.dcnt[ch] > 0:
            self._need(q, (ch, self.dcnt[ch]))
        self._deps(q, reads, writes)
        self.dcnt[ch] += 16
        ev = (ch, self.dcnt[ch])
        if not self.dry:
            self.ops[q].append(("ins", lambda e: e.dma_start(out=out, in_=in_), self.dsem[ch], 16))
        self._mark(ev, reads, writes)

    def emit(self):
        nc = self.nc
        for e in self.ENG:
            assert not self.pending[e], e
        with nc.Block() as block:
            def run(engname):
                def f(eng):
                    for o in self.ops[engname]:
                        if o[0] == "wait":
                            eng.wait_ge(o[1], o[2])
                        else:
                            ins = o[1](eng)
                            if o[2] is not None:
                                ins.then_inc(o[2], o[3])
                return f
            block.tensor(run("pe"))
            block.scalar(run("act"))
            block.vector(run("dve"))
            block.gpsimd(run("pool"))
            block.sync(run("sp"))


def _bc(ap, axis, shape):
    return ap.unsqueeze(axis).broadcast_to(list(shape))


def build(plan=None, n_layers=DEPTH, dry=False, stop_after=None, n_units=2):
    nc = bass.Bass("TRN2", target_bir_lowering=False)
    P = Prog(nc, dry=dry)
    rec = []

    def din(name, shape):
        return nc.dram_tensor(name, list(shape), F32, kind="ExternalInput").ap()

    def dout(name, shape):
        return nc.dram_tensor(name, list(shape), F32, kind="ExternalOutput").ap()

    NU = n_units
    xin_u = [din("xin%d" % u, [NT, 128, D]) for u in range(NU)]
    cin_u = [din("cin%d" % u, [NS, D]) for u in range(NU)]
    sret_u = [din("sret%d" % u, [DEPTH, 16, H, 128, 128]) for u in range(NU)]
    sconv_u = [din("sconv%d" % u, [DEPTH, 32, FF]) for u in range(NU)]
    w_ada = din("w_ada", [DEPTH, D, 6 * D]); b_ada = din("b_ada", [DEPTH, 6 * D])
    norms = [din(n, [DEPTH, D]) for n in ("norm_pre1", "norm_post1", "norm_pre2", "norm_post2")]
    w_in = din("w_in", [DEPTH, D, 10240])
    sgu_w_s = din("sgu_w_s", [DEPTH, 8, 128, 128]); sgu_b_s = din("sgu_b_s", [DEPTH, 8, 128])
    sgu_ln_g = din("sgu_ln_g", [DEPTH, 1024]); sgu_ln_b = din("sgu_ln_b", [DEPTH, 1024])
    ret_gn_g = din("ret_gn_g", [DEPTH, 1024])
    w_ba = din("w_branch_a", [DEPTH, 1024, D]); w_bb = din("w_branch_b", [DEPTH, 1024, D])
    w_out = din("w_out", [DEPTH, D, D])
    w_gate = din("ffn_w_gate", [DEPTH, D, FF]); w_up = din("ffn_w_up", [DEPTH, D, FF])
    conv_w = din("ffn_conv_w", [DEPTH, 3, FF]); conv_b = din("ffn_conv_b", [DEPTH, FF])
    w_down = din("ffn_w_down", [DEPTH, FF, D])
    t_cos_u = [din("tab_cos%d" % u, [128, NT, 64]) for u in range(NU)]
    t_sin_u = [din("tab_sin%d" % u, [128, NT, 64]) for u in range(NU)]
    t_dec = din("tab_dec", [128, 2, 2, 8]); t_g = din("tab_g", [128, 2, 8])
    t_maskr = din("tab_maskr", [128, 2, 128]); t_maskg = din("tab_maskg", [128, 2, 128])
    t_E = din("tab_E", [NS, NT, 128]); t_rowm = din("tab_rowmask", [128, 16])
    t_ident = din("ident", [128, 128])
    smid = nc.dram_tensor("smid", [DEPTH, 128, H, 128], F32, kind="Internal").ap()
    halo = nc.dram_tensor("halo", [DEPTH, 2, FF], F32, kind="Internal").ap()
    b_smid = [Buf("smid%d" % i) for i in range(DEPTH)]
    b_halo = [Buf("halo%d" % i) for i in range(DEPTH)]

    y_u = [dout("y%d" % u, [NT, 128, D]) for u in range(NU)]
    rs_p = dout("rs_p", [DEPTH, H, 128, 128])
    rs_s_u = [dout("rs_s%d" % u, [DEPTH, 16, H, 128, 128]) for u in range(NU)]
    cs_u = [dout("cs%d" % u, [DEPTH, 34, FF]) for u in range(NU)]
    vs_u = [dout("vs_s%d" % u, [DEPTH, 128, 1024]) for u in range(NU)]
    xcur_u = [nc.dram_tensor("xcur%d" % u, [NT, 128, D], F32, kind="Internal").ap() for u in range(NU)]
    b_xcur_u = [[Buf("xcur%d_%d" % (u, i)) for i in range(NT)] for u in range(NU)]

    es = contextlib.ExitStack()
    with es:
        def sb(name, shape, dt=F32):
            return es.enter_context(nc.sbuf_tensor(name, list(shape), dt))

        ident_f = sb("ident_f", [128, 128]); ident_b = sb("ident_b", [128, 128], BF16)
        ones_b = sb("ones_b", [128, 128], BF16)
        mhalf = sb("mhalf", [128, 16])
        epsb = sb("epsb", [128, 1])
        Etab = sb("Etab", [NS, NT, 128], BF16)
        dec = sb("dec", [128, 2, 2, 8]); gtab = sb("gtab", [128, 2, 8])
        maskr = sb("maskr", [128, 2, 128]); maskg = sb("maskg", [128, 2, 128])
        rowm = sb("rowm", [128, 16])
        csT_u = [sb("csT%d" % u, [128, KB, NS], BF16) for u in range(NU)]
        gnT = sb("gnT", [128, 8])
        MOD = [sb("mod%d" % i, [NS, D], BF16) for i in range(4)]
        b_const = Buf("const"); b_csT_u = [Buf("csT%d" % u) for u in range(NU)]; b_gnT = Buf("gnT")
        b_MOD = [Buf("mod%d" % i) for i in range(4)]
        arena = sb("arena", [128, 65536], BF16)
        b_R = {k: Buf(k) for k in ("R1", "R2a", "R2b", "R3a", "R3b", "R4")}
        R1 = arena[:, 0:18432]
        R2a = arena[:, 18432:27648]; R2b = arena[:, 27648:36864]
        R3a = arena[:, 36864:46080]; R3b = arena[:, 46080:55296]
        R4 = arena[:, 55296:65536]
        wslots = [sb("wslot%d" % i, [128, KB, PW], BF16) for i in range(NSLOT)]
        b_ws = [Buf("ws%d" % i) for i in range(NSLOT)]
        SCR = 33920
        scr = sb("scr", [128, SCR], mybir.dt.uint8)
        ps = es.enter_context(nc.psum_tensor("ps", [128, 7, 512], F32))
        ptb = es.enter_context(nc.psum_tensor("ptb", [128, 8, 128], BF16))
        b_ps = [Buf("ps%d" % i, excl=True) for i in range(7)]
        b_ptb = Buf("ptb", excl=True)

        ch_w = [P.chan("w%d" % i) for i in range(NSLOT)]
        ch_c = P.chan("const")
        ch_x = [P.chan("x0"), P.chan("x1")]
        ch_o = P.chan("out")
        ch_ms = [P.chan("misc%d" % i) for i in range(12)]
        mrot = {"i": 0}

        def mch():
            mrot["i"] = (mrot["i"] + 1) % len(ch_ms)
            return ch_ms[mrot["i"]]
        ch_s = [P.chan("s0"), P.chan("s1"), P.chan("s2")]
        ch_sb = [P.chan("sb0"), P.chan("sb1")]

        scr_state = {"bufs": [], "off": 0}

        def scratch_reset():
            old = scr_state["bufs"]
            scr_state["bufs"] = []
            scr_state["off"] = 0
            scr_state["inherit"] = old

        scr_state["inherit"] = []

        def scratch_mark():
            return (scr_state["off"], len(scr_state["bufs"]))

        def scratch_reset_to(mark):
            off, nb = mark
            scr_state["inherit"] = list(scr_state["inherit"]) + scr_state["bufs"][nb:]
            scr_state["bufs"] = scr_state["bufs"][:nb]
            scr_state["off"] = off

        def salloc(name, shape, dt=F32):
            esz = 4 if dt == F32 else 2
            n = 1
            for s in shape[1:]:
                n *= s
            nbytes = (n * esz + 31) // 32 * 32
            off = scr_state["off"]
            assert off + nbytes <= SCR, (name, off, nbytes)
            scr_state["off"] = off + nbytes
            flat = scr[0:shape[0], off:off + n * esz].bitcast(dt)
            if len(shape) == 2:
                ap = flat
            elif len(shape) == 3:
                ap = flat.rearrange("p (a b) -> p a b", a=shape[1])
            else:
                ap = flat.rearrange("p (a b c) -> p a b c", a=shape[1], b=shape[2])
            b = Buf(name, inherit=scr_state["inherit"])
            scr_state["bufs"].append(b)
            return ap, b

        def region(name_old_list, name):
            nb = Buf(name, inherit=[b_R[k] for k in name_old_list])
            for k in name_old_list:
                b_R[k] = nb
            return nb

        def mm(out, lhsT, rhs, start, stop, reads, writes, sig):
            P.op("pe", lambda e: e.matmul(out=out, lhsT=lhsT, rhs=rhs, start=start, stop=stop),
                 reads, writes, sig)

        def tr(out, in_, ident, reads, writes, sig):
            P.op("pe", lambda e: e.transpose(out=out, in_=in_, identity=ident), reads, writes, sig)

        def act(out, in_, func, reads, writes, **kw):
            P.op("act", lambda e: e.activation(out=out, in_=in_, func=func, **kw), reads, writes)

        def tt(eng, out, in0, in1, op, reads, writes):
            P.op(eng, lambda e: e.tensor_tensor(out=out, in0=in0, in1=in1, op=op), reads, writes)

        def ts(eng, out, in0, s1, s2, op0, op1, reads, writes):
            if s2 is None:
                P.op(eng, lambda e: e.tensor_scalar(out=out, in0=in0, scalar1=s1, scalar2=None, op0=op0),
                     reads, writes)
            else:
                P.op(eng, lambda e: e.tensor_scalar(out=out, in0=in0, scalar1=s1, scalar2=s2, op0=op0, op1=op1),
                     reads, writes)

        def stt(eng, out, in0, scalar, in1, op0, op1, reads, writes):
            P.op(eng, lambda e: e.scalar_tensor_tensor(out=out, in0=in0, scalar=scalar, in1=in1, op0=op0, op1=op1),
                 reads, writes)

        def cp(eng, out, in_, reads, writes):
            if eng == "act":
                P.op("act", lambda e: e.copy(out=out, in_=in_), reads, writes)
            else:
                P.op(eng, lambda e: e.tensor_copy(out=out, in_=in_), reads, writes)

        def memset(eng, ap, val, writes):
            P.op(eng, lambda e: e.memset(ap, val), (), writes)

        def rsqrt_inplace(ap, buf, shape, in1=None, b_in1=None):
            ts("dve", ap, ap, EPS, None, ALU.add, None, [buf], [buf])
            if in1 is None:
                assert shape[1] <= 16
                in1 = mhalf[0:shape[0], 0:shape[1]]
                b_in1 = b_const
            tt("pool", ap, ap, in1, ALU.pow, [buf, b_in1], [buf])

        wstate = {"i": 0, "issued": 0}

        def wview(slot, ncols):
            if ncols <= PW:
                return wslots[slot]
            return wslots[slot][:].rearrange("p a b -> p (a b)").rearrange("p (a b) -> p a b", b=ncols)

        def w_issue(idx):
            src, K, ncols = plan[idx]
            slot = idx % NSLOT
            P.dma("pool", ch_w[slot], wview(slot, ncols)[:, 0:K // 128, 0:ncols],
                  src.rearrange("(kb p) c -> p kb c", p=128), writes=[b_ws[slot]])

        def wnext(src, K, ncols=PW):
            i = wstate["i"]
            wstate["i"] = i + 1
            if plan is None:
                rec.append((src, K, ncols))
                return wview(i % NSLOT, ncols), b_ws[i % NSLOT]
            assert plan[i][1] == K and plan[i][2] == ncols
            while wstate["issued"] < min(len(plan), i + NSLOT):
                w_issue(wstate["issued"])
                wstate["issued"] += 1
            return wview(i % NSLOT, ncols), b_ws[i % NSLOT]

        def load_consts():
            for dst, src in ((ident_f, t_ident), (dec, t_dec), (gtab, t_g), (maskr, t_maskr),
                             (maskg, t_maskg), (rowm, t_rowm)):
                P.dma("sp", mch(), dst[:], src, writes=[Buf("c")] if False else [b_const])
            P.dma("pool", ch_c, Etab[:], t_E, writes=[b_const])
            cp("dve", ident_b[:], ident_f[:], [b_const], [b_const])
            memset("dve", ones_b[:], 1.0 / 128.0, [b_const])
            memset("dve", mhalf[:], -0.5, [b_const])
            memset("dve", epsb[:], EPS, [b_const])

        def compute_csT(cin, csT, b_csT):
            scratch_reset()
            c_sb, b_c = salloc("c_sb", [NS, D])
            sg_sb, b_sg = salloc("sg_sb", [NS, D])
            P.dma("sp", mch(), c_sb, cin, writes=[b_c])
            act(sg_sb, c_sb, AF.Silu, [b_c], [b_sg])
            pv = ps[:, 0, 0:KB * NS].rearrange("p (a b) -> p a b", a=KB)
            for kb in range(KB):
                tr(pv[:, kb, :], sg_sb[:, kb * 128:(kb + 1) * 128], ident_f[0:NS, 0:NS],
                   [b_sg, b_const], [b_ps[0]], sig=(kb == KB - 1))
            cp("dve", csT[:], pv, [b_ps[0]], [b_csT])

        def mod_scratch():
            bq, b_bq = salloc("bq", [NS, PW]); nq, b_nq = salloc("nq", [NS, PW]); tmp, b_tmp = salloc("mtmp", [NS, PW])
            return (bq, b_bq, nq, b_nq, tmp, b_tmp)

        def mod_steps(l, m, slot, u, bank, sc):
            bq, b_bq, nq, b_nq, tmp, b_tmp = sc
            csT, b_csT = csT_u[u], b_csT_u[u]
            kind = m % 3
            for q in range(D // PW):
                prev_label = P.label
                P.label = "mod"
                c0 = m * D + q * PW
                P.dma("sp", mch(), bq, b_ada[l, c0:c0 + PW].partition_broadcast(NS), writes=[b_bq])
                if kind != 0:
                    nsrc = norms[{1: 0, 2: 1, 4: 2, 5: 3}[m]]
                    P.dma("sp", mch(), nq, nsrc[l, q * PW:(q + 1) * PW].partition_broadcast(NS), writes=[b_nq])
                wp, b_wp = wnext(w_ada[l, :, c0:c0 + PW], D)
                o = ps[0:NS, bank, 0:PW]
                for kb in range(KB):
                    mm(o, csT[:, kb, :], wp[:, kb, :], kb == 0, kb == KB - 1,
                       [b_csT, b_wp], [b_ps[bank]], sig=(kb == KB - 1))
                dst = MOD[slot][:, q * PW:(q + 1) * PW]
                if kind == 0:
                    tt("dve", dst, o, bq, ALU.add, [b_ps[bank], b_bq], [b_MOD[slot]])
                else:
                    tt("dve", tmp, o, bq, ALU.add, [b_ps[bank], b_bq], [b_tmp])
                    if kind == 1:
                        stt("dve", dst, tmp, 1.0, nq, ALU.add, ALU.mult, [b_tmp, b_nq], [b_MOD[slot]])
                    else:
                        tt("dve", dst, tmp, nq, ALU.mult, [b_tmp, b_nq], [b_MOD[slot]])
                P.label = prev_label
                yield

        def mod_piece(l, m, slot, u):
            scratch_reset()
            sc = mod_scratch()
            for _ in mod_steps(l, m, slot, u, 0, sc):
                pass

        def expand(slot, tile, half, banks):
            for j in range(2):
                c0 = half * 1024 + j * 512
                mm(ps[:, banks[j], :], Etab[:, tile, :], MOD[slot][:, c0:c0 + 512], True, True,
                   [b_const, b_MOD[slot]], [b_ps[banks[j]]], sig=True)

        def prenorm(tile, xt, b_xt, slotA, slotB, dstT, b_dst, sc):
            junk, b_junk, ss, b_ss, tmp, b_tmp, htm, b_htm = sc
            act(junk, xt, AF.Square, [b_xt], [b_junk, b_ss], scale=float(D ** -0.5), accum_out=ss)
            rsqrt_inplace(ss, b_ss, [128, 1])
            for half in range(2):
                expand(slotA, tile, half, (0, 1))
                expand(slotB, tile, half, (2, 3))
                sl = slice(half * 1024, (half + 1) * 1024)
                pa = ps[:, 0:2, :].rearrange("p a b -> p (a b)")
                pb = ps[:, 2:4, :].rearrange("p a b -> p (a b)")
                stt("dve", tmp, xt[:, sl], ss, pa, ALU.mult, ALU.mult, [b_xt, b_ss, b_ps[0], b_ps[1]], [b_tmp])
                tt("dve", htm[:, sl], tmp, pb, ALU.add, [b_tmp, b_ps[2], b_ps[3]], [b_htm])
            for g in range(2):
                for j in range(8):
                    kb = g * 8 + j
                    tr(ptb[:, j, :], htm[:, kb * 128:(kb + 1) * 128], ident_b[:], [b_htm, b_const], [b_ptb],
                       sig=(j == 7))
                cp("act", dstT[:, g * 8:(g + 1) * 8, tile * 128:(tile + 1) * 128], ptb[:], [b_ptb], [b_dst])

        def prenorm_scratch():
            junk, b_junk = salloc("junk", [128, D], BF16)
            ss, b_ss = salloc("ss", [128, 1])
            tmp, b_tmp = salloc("ptmp", [128, 1024])
            htm, b_htm = salloc("htm", [128, D], BF16)
            return (junk, b_junk, ss, b_ss, tmp, b_tmp, htm, b_htm)

        def proj_tm(srcT, b_src, tile, wp, b_wp, nkb, bank, ncols=PW):
            o = ps[:, bank, 0:ncols]
            for kb in range(nkb):
                mm(o, srcT[:, kb, tile * 128:(tile + 1) * 128], wp[:, kb, 0:ncols], kb == 0, kb == nkb - 1,
                   [b_src, b_wp], [b_ps[bank]], sig=(kb == nkb - 1))
            return o

        def proj_fm(srcT, b_src, wp, b_wp, blk, nkb, banks, groups=TG):
            outs = [ps[:, banks[gi], 0:n] for gi, (t0, n) in enumerate(groups)]
            for kb in range(nkb):
                for gi, (t0, n) in enumerate(groups):
                    last = (kb == nkb - 1)
                    mm(outs[gi], wp[:, kb, blk * 128:(blk + 1) * 128], srcT[:, kb, t0:t0 + n], kb == 0, last,
                       [b_src, b_wp], [b_ps[banks[gi]]], sig=last)
            return outs

        load_consts()
        rr = {"b": 0}

        def rot(n, mod=7):
            b = rr["b"]
            rr["b"] = (b + n) % mod
            return b

        def chk(name):
            if stop_after == name:
                raise _Stop()

        for u in range(NU):
            compute_csT(cin_u[u], csT_u[u], b_csT_u[u])
        stopped = [False]
        for l in range(n_layers):
         for u in range(NU):
          if stopped[0]:
              break
          try:
            xin, cin, sret, sconv = xin_u[u], cin_u[u], sret_u[u], sconv_u[u]
            t_cos, t_sin = t_cos_u[u], t_sin_u[u]
            y_o, rs_s, cs_o, vs_o, xcur, b_xcur = y_u[u], rs_s_u[u], cs_u[u], vs_u[u], xcur_u[u], b_xcur_u[u]
            xsrc = xin if l == 0 else xcur
            b_xsrc = [Buf("xin%d" % i) for i in range(NT)] if l == 0 else b_xcur
            P.label = "L%d.U%d.A" % (l, u)
            if l == 0 and u == 0:
                mod_piece(l, 1, 0, u)
                mod_piece(l, 0, 1, u)
            scratch_reset()
            rows, b_rows = salloc("rows", [38, 512])
            P.dma("sp", mch(), rows[0:8, 0:128], ret_gn_g[l].rearrange("(h v) -> h v", h=8), writes=[b_rows])
            tr(ps[:, 4, 0:8], rows[0:8, 0:128], ident_f[0:8, 0:8], [b_rows, b_const], [b_ps[4]], sig=True)
            cp("dve", gnT[:], ps[:, 4, 0:8], [b_ps[4]], [b_gnT])

            P.label = "L%d.U%d.B" % (l, u)
            scratch_reset()
            hT = R1.rearrange("p (a b) -> p a b", a=KB)
            b_hT = region(["R1"], "hT")
            psc = prenorm_scratch()
            xts = [salloc("xt%d" % i, [128, D]) for i in range(2)]
            for tile in range(NT):
                xt, b_xt = xts[tile % 2]
                P.dma("sp", ch_x[tile % 2], xt, xsrc[tile], reads=[b_xsrc[tile]], writes=[b_xt])
                prenorm(tile, xt, b_xt, 0, 1, hT, b_hT, psc)
            chk("B")

            P.label = "L%d.U%d.C" % (l, u)
            scratch_reset()
            cosT, b_cos = salloc("cosT", [128, NT, 64])
            sinT, b_sin = salloc("sinT", [128, NT, 64])
            P.dma("sp", mch(), cosT, t_cos, writes=[b_cos])
            P.dma("sp", mch(), sinT, t_sin, writes=[b_sin])
            ra, b_ra = salloc("ra", [128, 2, 64]); rb, b_rb = salloc("rb", [128, 2, 64])
            rt, b_rt = salloc("rt", [128, 2, 128])
            qtms = [salloc("qtm%d" % i, [128, 2, 128], BF16) for i in range(2)]
            k_tm = R2a.rearrange("p (a b) -> p a b", a=NT); b_ktm0 = region(["R2a"], "k_tm")
            b_ktm = [[Buf("k_tm_%d_%d" % (t_, p_), inherit=[b_ktm0]) for p_ in range(4)] for t_ in range(NT)]
            kT = R2b.rearrange("p (a b) -> p a b", a=H); b_kT = region(["R2b"], "kT")
            v_tm = R3a.rearrange("p (a b) -> p a b", a=NT); b_vtm = region(["R3a"], "v_tm")
            qT = R3b.rearrange("p (a b) -> p a b", a=H); b_qT = region(["R3b"], "qT")

            def rope(o, tile, qk, h0, dst, b_dstbuf):
                kind = 1 if tile == NT - 1 else 0
                pv = o.rearrange("p (h d) -> p h d", h=2)
                x1 = pv[:, :, 0:64]; x2 = pv[:, :, 64:128]
                cs = _bc(cosT[:, tile, :], 1, [128, 2, 64]); sn = _bc(sinT[:, tile, :], 1, [128, 2, 64])
                tt("dve", ra, x1, cs, ALU.mult, [pbuf[0], b_cos], [b_ra])
                tt("dve", rb, x2, sn, ALU.mult, [pbuf[0], b_sin], [b_rb])
                tt("dve", rt[:, :, 0:64], ra, rb, ALU.subtract, [b_ra, b_rb], [b_rt])
                tt("dve", ra, x1, sn, ALU.mult, [pbuf[0], b_sin], [b_ra])
                tt("dve", rb, x2, cs, ALU.mult, [pbuf[0], b_cos], [b_rb])
                tt("dve", rt[:, :, 64:128], ra, rb, ALU.add, [b_ra, b_rb], [b_rt])
                dd = _bc(dec[:, kind, qk, h0:h0 + 2], 2, [128, 2, 128])
                tt("dve", dst, rt, dd, ALU.mult, [b_rt, b_const], [b_dstbuf])

            pbuf = [None]
            pend = [None]

            def flush():
                if pend[0] is not None:
                    pend[0]()
                    pend[0] = None

            for p in range(4):
                wp, b_wp = wnext(w_in[l, :, OFF_K + p * PW: OFF_K + (p + 1) * PW], D)
                for tile in range(NT):
                    bank = rot(1)
                    o = proj_tm(hT, b_hT, tile, wp, b_wp, KB, bank)
                    pbuf[0] = b_ps[bank]
                    dst = k_tm[:, tile, p * PW:(p + 1) * PW].rearrange("p (h d) -> p h d", h=2)
                    rope(o, tile, 1, 2 * p, dst, b_ktm[tile][p])
                    flush()

                    def post(p=p, tile=tile):
                        for j in range(2):
                            tr(ptb[:, j, :], k_tm[:, tile, p * PW + j * 128: p * PW + (j + 1) * 128], ident_b[:],
                               [b_ktm[tile][p], b_const], [b_ptb], sig=(j == 1))
                        cp("act", kT[:, 2 * p:2 * p + 2, tile * 128:(tile + 1) * 128], ptb[:, 0:2, :], [b_ptb], [b_kT])
                    pend[0] = post
            flush()
            P.label = "L%d.U%d.Cv" % (l, u)
            for p in range(4):
                wp, b_wp = wnext(w_in[l, :, OFF_VR + p * PW: OFF_VR + (p + 1) * PW], D)
                for tile in range(NT):
                    bank = rot(1)
                    o = proj_tm(hT, b_hT, tile, wp, b_wp, KB, bank)
                    cp("act", v_tm[:, tile, p * PW:(p + 1) * PW], o, [b_ps[bank]], [b_vtm])
            P.label = "L%d.U%d.E" % (l, u)
            qi = 0
            for p in range(4):
                wp, b_wp = wnext(w_in[l, :, OFF_Q + p * PW: OFF_Q + (p + 1) * PW], D)
                for tile in range(NT):
                    bank = rot(1)
                    o = proj_tm(hT, b_hT, tile, wp, b_wp, KB, bank)
                    pbuf[0] = b_ps[bank]
                    qt_i, b_qt_i = qtms[qi % 2]
                    qi += 1
                    rope(o, tile, 0, 2 * p, qt_i, b_qt_i)
                    flush()

                    def post(p=p, tile=tile, qt_i=qt_i, b_qt_i=b_qt_i):
                        for j in range(2):
                            tr(ptb[:, j, :], qt_i[:, j, :], ident_b[:], [b_qt_i, b_const], [b_ptb], sig=(j == 1))
                        cp("act", qT[:, 2 * p:2 * p + 2, tile * 128:(tile + 1) * 128], ptb[:, 0:2, :], [b_ptb], [b_qT])
                    pend[0] = post
            flush()
            P.label = "L%d.U%d.F" % (l, u)
            grT = R4[:, 0:H * T].rearrange("p (a b) -> p a b", a=H); b_grT = region(["R4"], "grT")
            for p in range(4):
                wp, b_wp = wnext(w_in[l, :, OFF_GR + p * PW: OFF_GR + (p + 1) * PW], D)
                for blk in range(2):
                    banks = (0, 1, 2) if blk == 0 else (3, 4, 5)
                    outs = proj_fm(hT, b_hT, wp, b_wp, blk, KB, banks)
                    for gi, (t0, n) in enumerate(TG):
                        act(grT[:, 2 * p + blk, t0:t0 + n], outs[gi], AF.Silu, [b_ps[banks[gi]]], [b_grT])
            chk("F")

            P.label = "L%d.U%d.G" % (l, u)
            scratch_reset()
            S, b_S = salloc("S", [128, H, 128]); S_bf, b_Sbf = salloc("S_bf", [128, H, 128], BF16)
            kmk, b_kmk = salloc("kmk", [128, 1024], BF16)
            gmark = None
            import itertools
            msc = mod_scratch()
            gsteps = itertools.chain(mod_steps(l, 2, 2, u, 6, msc),
                                     mod_steps(l, 4, 0, u, 6, msc),
                                     mod_steps(l, 3, 1, u, 6, msc),
                                     mod_steps(l, 5, 3, u, 6, msc))
            gmark = scratch_mark()
            s_sb, b_ssb = salloc("s_sb", [128, H, 128], BF16)
            o_f, b_of = salloc("o_f", [128, H, 128]); o_bf, b_obf = salloc("o_bf", [128, H, 128], BF16)
            osq, b_osq = salloc("osq", [128, H, 128], BF16)
            m2, b_m2 = salloc("m2", [128, H, 128])
            s0b = [salloc("s0b%d" % i, [128, 16, 128], BF16) for i in range(2)]
            if u == 0:
                memset("dve", S, 0.0, [b_S])
            else:
                P.dma("sp", mch(), S, smid[l], reads=[b_smid[l]], writes=[b_S])
            cp("act", S_bf, S, [b_S], [b_Sbf])
            for tile in range(NT):
                kind = 1 if tile == NT - 1 else 0
                tok = slice(tile * 128, (tile + 1) * 128)
                for hb in range(2):
                    for h4 in range(4):
                        h = hb * 4 + h4
                        mm(ps[:, hb, h4 * 128:(h4 + 1) * 128], kT[:, h, tok], qT[:, h, tok], True, True,
                           [b_kT, b_qT], [b_ps[hb]], sig=(h4 == 3))
                    tt("dve", s_sb[:, hb * 4:(hb + 1) * 4, :], ps[:, hb, :].rearrange("p (a b) -> p a b", a=4),
                       _bc(maskr[:, kind, :], 1, [128, 4, 128]), ALU.mult, [b_ps[hb], b_const], [b_ssb])
                chk("Ga")
                po = ps[:, 2:4, :].rearrange("p a (h i) -> p (a h) i", h=4)
                for h in range(H):
                    bo = b_ps[2 + h // 4]
                    mm(po[:, h, :], v_tm[:, tile, h * 128:(h + 1) * 128], s_sb[:, h, :], True, False,
                       [b_vtm, b_ssb], [bo], sig=False)
                    if kind == 0:
                        mm(po[:, h, :], S_bf[:, h, :], qT[:, h, tok], False, True, [b_Sbf, b_qT], [bo], sig=True)
                    else:
                        sbt, b_sbt = s0b[h % 2]
                        P.dma("pool", ch_sb[h % 2], sbt, sret[l, :, h, :, :].rearrange("s d v -> d s v"),
                              writes=[b_sbt])
                        for s in range(16):
                            mm(po[:, h, 8 * s:8 * s + 8], sbt[:, s, :], qT[:, h, tile * 128 + 8 * s: tile * 128 + 8 * s + 8],
                               False, s == 15, [b_sbt, b_qT], [bo], sig=(s == 15))
                chk("Gb1")
                pof = ps[:, 2:4, :].rearrange("p a b -> p (a b)")
                o_f2 = o_f.rearrange("p a b -> p (a b)"); o_bf2 = o_bf.rearrange("p a b -> p (a b)")
                osq2 = osq.rearrange("p a b -> p (a b)"); m22 = m2.rearrange("p a b -> p (a b)")
                cp("act", o_f2, pof, [b_ps[2], b_ps[3]], [b_of])
                chk("Gb2")
                cp("dve", o_bf2, pof, [b_ps[2], b_ps[3]], [b_obf])
                chk("Gb3")
                act(osq2, pof, AF.Square, [b_ps[2], b_ps[3]], [b_osq])
                chk("Gb")
                if kind == 0:
                    pd = ps[:, 4:6, :].rearrange("p a (h i) -> p (a h) i", h=4)
                    for h in range(H):
                        mm(pd[:, h, :], k_tm[:, tile, h * 128:(h + 1) * 128], v_tm[:, tile, h * 128:(h + 1) * 128],
                           True, True, [b_ktm[tile][h // 2], b_vtm], [b_ps[4 + h // 4]], sig=(h % 4 == 3))
                    pdf = ps[:, 4:6, :].rearrange("p a b -> p (a b)")
                    S2 = S.rearrange("p a b -> p (a b)")
                    tt("dve", S2, S2, pdf, ALU.add, [b_S, b_ps[4], b_ps[5]], [b_S])
                    tt("dve", S, S, _bc(gtab[:, 0, :], 2, [128, H, 128]), ALU.mult, [b_S, b_const], [b_S])
                    cp("act", S_bf, S, [b_S], [b_Sbf])
                    if tile == NT - 2:
                        if u == 0 and NU > 1:
                            P.dma("sp", mch(), smid[l], S, reads=[b_S], writes=[b_smid[l]])
                        else:
                            P.dma("sp", mch(), rs_p[l].rearrange("h d v -> d h v"), S, reads=[b_S])
                chk("Gc")
                for hb in range(2):
                    mm(ps[:, hb, :], ones_b[:], o_bf2[:, hb * 512:(hb + 1) * 512], True, True, [b_const, b_obf],
                       [b_ps[hb]], sig=True)
                for hb in range(2):
                    mm(ps[:, 4 + hb, :], ones_b[:], osq2[:, hb * 512:(hb + 1) * 512], True, True, [b_const, b_osq],
                       [b_ps[4 + hb]], sig=True)
                pm = ps[:, 0:2, :].rearrange("p a b -> p (a b)")
                pq = ps[:, 4:6, :].rearrange("p a b -> p (a b)")
                act(m22, pm, AF.Square, [b_ps[0], b_ps[1]], [b_m2])
                tt("dve", m22, pq, m22, ALU.subtract, [b_ps[4], b_ps[5], b_m2], [b_m2])
                act(m22, m22, AF.Sqrt, [b_m2], [b_m2], bias=epsb[:, 0:1], scale=1.0)
                P.op("dve", lambda e: e.reciprocal(out=m22, in_=m22), [b_m2], [b_m2])
                tt("dve", o_f2, o_f2, pm, ALU.subtract, [b_of, b_ps[0], b_ps[1]], [b_of])
                tt("dve", o_f2, o_f2, m22, ALU.mult, [b_of, b_m2], [b_of])
                tt("dve", o_f, o_f, _bc(gnT[:], 2, [128, H, 128]), ALU.mult, [b_of, b_gnT], [b_of])
                tt("dve", grT[:, :, tok], o_f, grT[:, :, tok], ALU.mult, [b_of, b_grT], [b_grT])
                for _ in range(4):
                    next(gsteps, None)
                if tile == 0:
                    chk("Gd")
                if tile == 7:
                    chk("Ge")
                if kind == 1:
                    chk("Gf")
                    P.label = "L%d.U%d.Gs" % (l, u)
                    scratch_reset_to(gmark)
                    s0f = [salloc("s0f%d" % i, [128, H, 128]) for i in range(3)]
                    for s in range(16):
                        sft, b_sft = s0f[s % 3]
                        P.dma("sp", ch_s[s % 3], sft, sret[l, s].rearrange("h d v -> d h v"), writes=[b_sft])
                        ts("dve", kmk, k_tm[:, tile, :], rowm[:, s:s + 1], None, ALU.mult, None,
                           b_ktm[tile] + [b_const], [b_kmk])
                        pd = ps[:, 4:6, :].rearrange("p a (h i) -> p (a h) i", h=4)
                        for h in range(H):
                            mm(pd[:, h, :], kmk[:, h * 128:(h + 1) * 128], v_tm[:, tile, h * 128:(h + 1) * 128],
                               True, True, [b_kmk, b_vtm], [b_ps[4 + h // 4]], sig=(h % 4 == 3))
                        pdf = ps[:, 4:6, :].rearrange("p a b -> p (a b)")
                        sf2 = sft.rearrange("p a b -> p (a b)")
                        tt("dve", sf2, sf2, pdf, ALU.add, [b_sft, b_ps[4], b_ps[5]], [b_sft])
                        tt("dve", sft, sft, _bc(gtab[:, 1, :], 2, [128, H, 128]), ALU.mult, [b_sft, b_const], [b_sft])
                        P.dma("sp", ch_s[s % 3], rs_s[l, s].rearrange("h d v -> d h v"), sft, reads=[b_sft])
            for _ in gsteps:
                pass
            y_bT = grT; b_ybT = b_grT
            b_R["R2a"] = Buf("k_tm_done", inherit=[b for row in b_ktm for b in row])
            chk("G")

            P.label = "L%d.U%d.H" % (l, u)
            scratch_reset()
            lnG, b_lnG = salloc("lnG", [128, 1024]); lnB, b_lnB = salloc("lnB", [128, 1024])
            WspT, b_Wsp = salloc("WspT", [128, 2, 8, 128], BF16)
            bsP, b_bsP = salloc("bsP", [128, 8, 128]); bs8, b_bs8 = salloc("bs8", [128, 8, 8])
            hmark = scratch_mark()
            wtmp, b_wtmp = salloc("wtmp", [128, 8, 128]); wmk, b_wmk = salloc("wmk", [128, 8, 128], BF16)
            rep, b_rep = salloc("rep", [128, 8, 8])
            P.dma("sp", mch(), lnG, sgu_ln_g[l, :].partition_broadcast(128), writes=[b_lnG])
            P.dma("sp", mch(), lnB, sgu_ln_b[l, :].partition_broadcast(128), writes=[b_lnB])
            P.dma("sp", mch(), bsP, sgu_b_s[l].partition_broadcast(128), writes=[b_bsP])
            P.dma("sp", mch(), bs8, sgu_b_s[l, :, 0:8].partition_broadcast(128), writes=[b_bs8])
            P.dma("sp", mch(), wtmp, sgu_w_s[l].rearrange("g t s -> t g s"), writes=[b_wtmp])
            for s in range(16):
                P.dma("sp", mch(), rep[8 * s:8 * s + 8, :, :], sgu_w_s[l, :, 0:8, 0:8].rearrange("g t s -> t g s"),
                      writes=[b_rep])
            tt("dve", wmk, wtmp, _bc(maskg[:, 0, :], 1, [128, 8, 128]), ALU.mult, [b_wtmp, b_const], [b_wmk])
            for g in range(8):
                tr(ptb[:, g, :], wmk[:, g, :], ident_b[:], [b_wmk, b_const], [b_ptb], sig=(g == 7))
            cp("act", WspT[:, 0, :, :], ptb[:], [b_ptb], [b_Wsp])
            wmk4 = wmk.rearrange("p g (a b) -> p g a b", a=16)
            in0 = rep.unsqueeze(2).broadcast_to([128, 8, 16, 8])
            in1 = maskg[:, 1, :].rearrange("p (a b) -> p a b", a=16).unsqueeze(1).broadcast_to([128, 8, 16, 8])
            tt("dve", wmk4, in0, in1, ALU.mult, [b_rep, b_const], [b_wmk])
            for g in range(8):
                tr(ptb[:, g, :], wmk[:, g, :], ident_b[:], [b_wmk, b_const], [b_ptb], sig=(g == 7))
            cp("act", WspT[:, 1, :, :], ptb[:], [b_ptb], [b_Wsp])
            scratch_reset_to(hmark)
            vn, b_vn = salloc("vn", [128, 1024])
            sums, b_sums = salloc("sums", [128, NT, 4]); sqs, b_sqs = salloc("sqs", [128, NT, 4])
            mean, b_mean = salloc("mean", [128, NT]); rstd, b_rstd = salloc("rstdv", [128, NT])
            jk, b_jk = salloc("jk", [128, PW], BF16)
            ug, b_ug = salloc("ug", [128, T], BF16)
            ztmp, b_ztmp = salloc("ztmp", [128, T])
            vs_tm = R2a.rearrange("p (a b) -> p a b", a=NT); b_vs = region(["R2a"], "vs_tm")
            y_aT = R2b.rearrange("p (a b) -> p a b", a=H); b_yaT = region(["R2b"], "y_aT")
            for p in range(4):
                wp, b_wp = wnext(w_in[l, :, OFF_V + p * PW: OFF_V + (p + 1) * PW], D)
                for tile in range(NT):
                    bank = rot(1)
                    o = proj_tm(hT, b_hT, tile, wp, b_wp, KB, bank)
                    dstv = vs_tm[:, tile, p * PW:(p + 1) * PW]
                    act(dstv, o, AF.Gelu_apprx_tanh, [b_ps[bank]], [b_vs, b_sums], accum_out=sums[:, tile, p:p + 1])
                    act(jk, dstv, AF.Square, [b_vs], [b_jk, b_sqs], accum_out=sqs[:, tile, p:p + 1])
            P.op("dve", lambda e: e.reduce_sum(out=mean, in_=sums, axis=AX.X), [b_sums], [b_mean])
            P.op("dve", lambda e: e.reduce_sum(out=rstd, in_=sqs, axis=AX.X), [b_sqs], [b_rstd])
            ts("dve", mean, mean, 1.0 / 1024, None, ALU.mult, None, [b_mean], [b_mean])
            ts("dve", rstd, rstd, 1.0 / 1024, None, ALU.mult, None, [b_rstd], [b_rstd])
            msq, b_msq = salloc("msq", [128, NT])
            tt("dve", msq, mean, mean, ALU.mult, [b_mean], [b_msq])
            tt("dve", rstd, rstd, msq, ALU.subtract, [b_rstd, b_msq], [b_rstd])
            rsqrt_inplace(rstd, b_rstd, [128, NT])
            for tile in range(NT):
                ts("dve", vn, vs_tm[:, tile, :], mean[:, tile:tile + 1], rstd[:, tile:tile + 1], ALU.subtract, ALU.mult,
                   [b_vs, b_mean, b_rstd], [b_vn])
                tt("dve", vn, vn, lnG, ALU.mult, [b_vn, b_lnG], [b_vn])
                tt("dve", vn, vn, lnB, ALU.add, [b_vn, b_lnB], [b_vn])
                cp("act", vs_tm[:, tile, :], vn, [b_vn], [b_vs])
                if tile == NT - 1:
                    P.dma("sp", mch(), vs_o[l], vn, reads=[b_vn])
            for p in range(4):
                wp, b_wp = wnext(w_in[l, :, OFF_U + p * PW: OFF_U + (p + 1) * PW], D)
                for blk in range(2):
                    g = 2 * p + blk
                    outs = proj_fm(hT, b_hT, wp, b_wp, blk, KB, (0, 1, 2))
                    for gi, (t0, n) in enumerate(TG):
                        act(ug[:, t0:t0 + n], outs[gi], AF.Gelu_apprx_tanh, [b_ps[gi]], [b_ug])
                    for tile in range(NT):
                        kind = 1 if tile == NT - 1 else 0
                        bank = 3 + tile // 4
                        c0 = (tile % 4) * 128
                        mm(ps[:, bank, c0:c0 + 128], vs_tm[:, tile, g * 128:(g + 1) * 128], WspT[:, kind, g, :],
                           True, True, [b_vs, b_Wsp], [b_ps[bank]], sig=(tile % 4 == 3 or tile == NT - 1))
                    for bq in range(2):
                        tt("dve", ztmp[:, bq * 512:(bq + 1) * 512].rearrange("p (a b) -> p a b", a=4),
                           ps[:, 3 + bq, :].rearrange("p (a b) -> p a b", a=4),
                           _bc(bsP[:, g, :], 1, [128, 4, 128]), ALU.add, [b_ps[3 + bq], b_bsP], [b_ztmp])
                    tt("dve", ztmp[:, 1024:1152].rearrange("p (a b) -> p a b", a=16),
                       ps[:, 5, 0:128].rearrange("p (a b) -> p a b", a=16),
                       _bc(bs8[:, g, :], 1, [128, 16, 8]), ALU.add, [b_ps[5], b_bs8], [b_ztmp])
                    tt("dve", y_aT[:, g, :], ztmp, ug, ALU.mult, [b_ztmp, b_ug], [b_yaT])
            chk("H")

            P.label = "L%d.U%d.I" % (l, u)
            scratch_reset()
            sg1, b_sg1 = salloc("sg1", [128, 2, T], BF16)
            sg2, b_sg2 = salloc("sg2", [128, 2, T], BF16)
            mergedT = arena[:, 36864:55296].rearrange("p (a b) -> p a b", a=KB)
            b_mg = region(["R3a", "R3b"], "mergedT")
            for p in range(8):
                wp, b_wp = wnext(w_in[l, :, OFF_GA + p * PW: OFF_GA + (p + 1) * PW], D)
                for blk in range(2):
                    banks = (0, 1, 2) if blk == 0 else (3, 4, 5)
                    outs = proj_fm(hT, b_hT, wp, b_wp, blk, KB, banks)
                    for gi, (t0, n) in enumerate(TG):
                        act(sg1[:, blk, t0:t0 + n], outs[gi], AF.Sigmoid, [b_ps[banks[gi]]], [b_sg1])
                wp, b_wp = wnext(w_ba[l, :, p * PW:(p + 1) * PW], 1024)
                for blk in range(2):
                    banks = (0, 1, 2) if blk == 0 else (3, 4, 5)
                    outs = proj_fm(y_aT, b_yaT, wp, b_wp, blk, 8, banks)
                    for gi, (t0, n) in enumerate(TG):
                        tt("dve", sg1[:, blk, t0:t0 + n], sg1[:, blk, t0:t0 + n], outs[gi], ALU.mult,
                           [b_sg1, b_ps[banks[gi]]], [b_sg1])
                wp, b_wp = wnext(w_in[l, :, OFF_GB + p * PW: OFF_GB + (p + 1) * PW], D)
                for blk in range(2):
                    banks = (0, 1, 2) if blk == 0 else (3, 4, 5)
                    outs = proj_fm(hT, b_hT, wp, b_wp, blk, KB, banks)
                    for gi, (t0, n) in enumerate(TG):
                        act(sg2[:, blk, t0:t0 + n], outs[gi], AF.Sigmoid, [b_ps[banks[gi]]], [b_sg2])
                wp, b_wp = wnext(w_bb[l, :, p * PW:(p + 1) * PW], 1024)
                for blk in range(2):
                    banks = (0, 1, 2) if blk == 0 else (3, 4, 5)
                    outs = proj_fm(y_bT, b_ybT, wp, b_wp, blk, 8, banks)
                    for gi, (t0, n) in enumerate(TG):
                        tt("dve", sg2[:, blk, t0:t0 + n], sg2[:, blk, t0:t0 + n], outs[gi], ALU.mult,
                           [b_sg2, b_ps[banks[gi]]], [b_sg2])
                    tt("pool", mergedT[:, 2 * p + blk, :], sg1[:, blk, :], sg2[:, blk, :], ALU.add,
                       [b_sg1, b_sg2], [b_mg])
            chk("I")

            P.label = "L%d.U%d.J" % (l, u)
            scratch_reset()
            t_st = arena[:, 18432:36864].rearrange("p (a b) -> p a b", a=NT)
            b_tst = region(["R2a", "R2b"], "t_store")
            h2T = R1.rearrange("p (a b) -> p a b", a=KB)
            b_h2T = region(["R1"], "h2T")
            ssq, b_ssq = salloc("ssq", [128, NT, 8]); rs1, b_rs1 = salloc("rs1", [128, NT])
            jk, b_jk = salloc("jk2", [128, PW], BF16)
            psc = prenorm_scratch()
            xts = [salloc("xtj%d" % i, [128, D]) for i in range(2)]
            tmpj, b_tmpj = psc[4], psc[5]
            for p in range(8):
                wp, b_wp = wnext(w_out[l, :, p * PW:(p + 1) * PW], D)
                for tile in range(NT):
                    bank = 4 + rot(1) % 3
                    o = proj_tm(mergedT, b_mg, tile, wp, b_wp, KB, bank)
                    cp("dve", t_st[:, tile, p * PW:(p + 1) * PW], o, [b_ps[bank]], [b_tst])
                    act(jk, o, AF.Square, [b_ps[bank]], [b_jk, b_ssq], accum_out=ssq[:, tile, p:p + 1])
            P.op("dve", lambda e: e.reduce_sum(out=rs1, in_=ssq, axis=AX.X), [b_ssq], [b_rs1])
            ts("dve", rs1, rs1, 1.0 / D, None, ALU.mult, None, [b_rs1], [b_rs1])
            rsqrt_inplace(rs1, b_rs1, [128, NT])
            for tile in range(NT):
                xt, b_xt = xts[tile % 2]
                P.dma("sp", ch_x[tile % 2], xt, xsrc[tile], reads=[b_xsrc[tile]], writes=[b_xt])
                for half in range(2):
                    expand(2, tile, half, (0, 1))
                    sl = slice(half * 1024, (half + 1) * 1024)
                    pa = ps[:, 0:2, :].rearrange("p a b -> p (a b)")
                    stt("dve", tmpj, t_st[:, tile, sl], rs1[:, tile:tile + 1], pa, ALU.mult, ALU.mult,
                        [b_tst, b_rs1, b_ps[0], b_ps[1]], [b_tmpj])
                    tt("pool", xt[:, sl], xt[:, sl], tmpj, ALU.add, [b_xt, b_tmpj], [b_xt])
                P.dma("sp", ch_x[tile % 2], xcur[tile], xt, reads=[b_xt], writes=[b_xcur[tile]])
                prenorm(tile, xt, b_xt, 0, 1, h2T, b_h2T, psc)
            chk("J")

            P.label = "L%d.U%d.L" % (l, u)
            scratch_reset()
            ctab, b_ctab = salloc("ctab", [128, FB, 38])
            gsave, b_gsave = salloc("gsave", [128, FB, 2])
            csave, b_csave = salloc("csave", [128, FB, 34])
            lsc = mod_scratch()
            nxt = (l, u + 1) if u + 1 < NU else ((l + 1, 0) if l + 1 < n_layers else None)
            if nxt is not None:
                import itertools
                lsteps = itertools.chain(mod_steps(nxt[0], 1, 0, nxt[1], 6, lsc), mod_steps(nxt[0], 0, 1, nxt[1], 6, lsc))
            else:
                lsteps = iter(())
            lmark = scratch_mark()
            rows2 = [salloc("rows%d" % i, [38, 512]) for i in range(2)]
            b_rparts = [[Buf("rp%d_%d" % (i, j), inherit=[rows2[i][1]]) for j in range(4)] for i in range(2)]
            for cchunk in range(FF // 512):
                c0 = cchunk * 512
                rows = rows2[cchunk % 2][0]
                brp = b_rparts[cchunk % 2]
                P.dma("sp", mch(), rows[0:32, :], sconv[l, :, c0:c0 + 512], writes=[brp[0]])
                if u == 0:
                    memset("dve", rows[32:34, :], 0.0, [brp[1]])
                else:
                    P.dma("sp", mch(), rows[32:34, :], halo[l, :, c0:c0 + 512], reads=[b_halo[l]], writes=[brp[1]])
                P.dma("sp", mch(), rows[34:37, :], conv_w[l, :, c0:c0 + 512], writes=[brp[2]])
                P.dma("sp", mch(), rows[37:38, :], conv_b[l:l + 1, c0:c0 + 512], writes=[brp[3]])
                pv = ps[:, 4 + cchunk % 2, 0:4 * 38].rearrange("p (a b) -> p a b", a=4)
                for j in range(4):
                    tr(pv[:, j, :], rows[:, j * 128:(j + 1) * 128], ident_f[0:38, 0:38], brp + [b_const],
                       [b_ps[4 + cchunk % 2]], sig=(j == 3))
                cp("dve", ctab[:, cchunk * 4:(cchunk + 1) * 4, :], pv, [b_ps[4 + cchunk % 2]], [b_ctab])
            yT = arena[:, 18432:18432 + FB * 640].rearrange("p (a b) -> p a b", a=FB)
            b_yT = region(["R2a", "R2b", "R3a", "R3b"], "yT")
            f_st = R4.rearrange("p (a b) -> p a b", a=5)
            b_fst = region(["R4"], "f_store")
            batches = [(0, 5), (5, 9)]
            for bi, (t_lo, t_hi) in enumerate(batches):
                ntile = t_hi - t_lo
                tok0 = t_lo * 128
                ntok = ntile * 128
                groups = [(tok0, 320), (tok0 + 320, 320)] if bi == 0 else [(tok0, 512)]
                ng = len(groups)
                P.label = "L%d.U%d.Lgu%d" % (l, u, bi)
                scratch_reset_to(lmark)
                gsets = [(salloc("gext%d" % i, [128, 642]), salloc("gexs%d" % i, [128, 16, 10]),
                          salloc("acc%d" % i, [128, 640]), salloc("ge%d" % i, [128, 640], BF16),
                          salloc("upsb%d" % i, [128, 640], BF16)) for i in range(2)]
                brot = {"i": 0}

                def take_banks():
                    i = brot["i"]
                    brot["i"] = i + 1
                    if ng == 2:
                        b0 = 2 * (i % 3)
                        return (b0, b0 + 1)
                    return (i % 6,)

                for p in range(FB // 2):
                    next(lsteps, None)
                    wg, b_wg = wnext(w_gate[l, :, p * PW:(p + 1) * PW], D)
                    for blk in range(2):
                        fb = 2 * p + blk
                        (gext, b_gext), (gexs, b_gexs), (acc, b_acc), (ge, b_ge), (upsb, b_upsb) = gsets[fb % 2]
                        gb = take_banks()
                        og = proj_fm(h2T, b_h2T, wg, b_wg, blk, KB, gb, groups)
                        w0 = ctab[:, fb, 34:35]; w1 = ctab[:, fb, 35:36]; w2 = ctab[:, fb, 36:37]; cb = ctab[:, fb, 37:38]
                        if bi == 0:
                            cp("dve", gext[:, 0:2], ctab[:, fb, 32:34], [b_ctab], [b_gext])
                            for gi in range(ng):
                                cp("act", gext[:, 2 + gi * 320: 2 + (gi + 1) * 320], og[gi], [b_ps[gb[gi]]], [b_gext])
                            cp("dve", gsave[:, fb, :], gext[:, 640:642], [b_gext], [b_gsave])
                            npr = 640
                        else:
                            cp("dve", gext[:, 0:2], gsave[:, fb, :], [b_gsave], [b_gext])
                            cp("act", gext[:, 2:386], og[0][:, 0:384], [b_ps[gb[0]]], [b_gext])
                            cp("dve", gexs[:, :, 0:2], ctab[:, fb, 0:32].rearrange("p (a b) -> p a b", a=16),
                               [b_ctab], [b_gexs])
                            cp("act", gexs[:, :, 2:10], og[0][:, 384:512].rearrange("p (a b) -> p a b", a=16),
                               [b_ps[gb[0]]], [b_gexs])
                            cp("dve", csave[:, fb, 32:34], gext[:, 384:386], [b_gext], [b_csave])
                            cp("dve", csave[:, fb, 0:32].rearrange("p (a b) -> p a b", a=16), gexs[:, :, 8:10],
                               [b_gexs], [b_csave])
                            npr = 384
                        act(acc[:, 0:npr], gext[:, 2:2 + npr], AF.Identity, [b_gext, b_ctab], [b_acc], scale=w2, bias=cb)
                        stt("dve", acc[:, 0:npr], gext[:, 1:1 + npr], w1, acc[:, 0:npr], ALU.mult, ALU.add,
                            [b_gext, b_ctab, b_acc], [b_acc])
                        stt("dve", acc[:, 0:npr], gext[:, 0:npr], w0, acc[:, 0:npr], ALU.mult, ALU.add,
                            [b_gext, b_ctab, b_acc], [b_acc])
                        if bi == 1:
                            a3 = acc[:, 384:512].rearrange("p (a b) -> p a b", a=16)
                            act(a3, gexs[:, :, 2:10], AF.Identity, [b_gexs, b_ctab], [b_acc], scale=w2, bias=cb)
                            stt("dve", a3, gexs[:, :, 1:9], w1, a3, ALU.mult, ALU.add, [b_gexs, b_ctab, b_acc], [b_acc])
                            stt("dve", a3, gexs[:, :, 0:8], w0, a3, ALU.mult, ALU.add, [b_gexs, b_ctab, b_acc], [b_acc])
                        act(ge[:, 0:ntok], acc[:, 0:ntok], AF.Gelu_apprx_tanh, [b_acc], [b_ge])
                    wu, b_wu = wnext(w_up[l, :, p * PW:(p + 1) * PW], D)
                    for blk in range(2):
                        fb = 2 * p + blk
                        (gext, b_gext), (gexs, b_gexs), (acc, b_acc), (ge, b_ge), (upsb, b_upsb) = gsets[fb % 2]
                        ub = take_banks()
                        ou = proj_fm(h2T, b_h2T, wu, b_wu, blk, KB, ub, groups)
                        for gi, (t0, n) in enumerate(groups):
                            tt("dve", yT[:, fb, t0 - tok0:t0 - tok0 + n], ge[:, t0 - tok0:t0 - tok0 + n], ou[gi], ALU.mult,
                               [b_ge, b_ps[ub[gi]]], [b_yT])
                scratch_reset_to(lmark)
                fsq, b_fsq = salloc("fsq", [128, 5, 4]); rs2, b_rs2 = salloc("rs2", [128, 5])
                jk, b_jk = salloc("jk3", [128, 512], BF16)
                tmpl, b_tmpl = salloc("tmpl", [128, 1024])
                crow, b_crow = salloc("crow", [34, 512])
                xts = [salloc("xtl0", [128, D])] * 2
                if bi == 1:
                    for cchunk in range(FF // 512):
                        for j in range(4):
                            tr(ps[0:34, 6, j * 128:(j + 1) * 128], csave[:, cchunk * 4 + j, :], ident_f[:],
                               [b_csave, b_const], [b_ps[6]], sig=(j == 3))
                        cp("act", crow, ps[0:34, 6, :], [b_ps[6]], [b_crow])
                        P.dma("sp", ch_o, cs_o[l, :, cchunk * 512:(cchunk + 1) * 512], crow, reads=[b_crow])
                        if u == 0 and NU > 1:
                            P.dma("sp", mch(), halo[l, :, cchunk * 512:(cchunk + 1) * 512], crow[32:34, :],
                                  reads=[b_crow], writes=[b_halo[l]])
                P.label = "L%d.U%d.Ldn%d" % (l, u, bi)
                kgs = [(k0, min(8, FB - k0)) for k0 in range(0, FB, 8)]
                for q in range(D // 512):
                    for (k0, nk) in kgs:
                        wp, b_wp = wnext(w_down[l, k0 * 128:(k0 + nk) * 128, q * 512:(q + 1) * 512], nk * 128, 512)
                        for kk in range(nk):
                            kb = k0 + kk
                            for ti in range(ntile):
                                last = (kb == FB - 1)
                                mm(ps[:, ti, :], yT[:, kb, ti * 128:(ti + 1) * 128], wp[:, kk, :], kb == 0, last,
                                   [b_yT, b_wp], [b_ps[ti]], sig=(last or (kk == nk - 1 and ti == ntile - 1)))
                    for ti in range(ntile):
                        cp("dve", f_st[:, ti, q * 512:(q + 1) * 512], ps[:, ti, :], [b_ps[ti]], [b_fst])
                        act(jk, ps[:, ti, :], AF.Square, [b_ps[ti]], [b_jk, b_fsq], accum_out=fsq[:, ti, q:q + 1])
                P.label = "L%d.U%d.Lrs%d" % (l, u, bi)
                P.op("dve", lambda e: e.reduce_sum(out=rs2, in_=fsq, axis=AX.X), [b_fsq], [b_rs2])
                ts("dve", rs2, rs2, 1.0 / D, None, ALU.mult, None, [b_rs2], [b_rs2])
                rsqrt_inplace(rs2, b_rs2, [128, 5])
                for ti in range(ntile):
                    tile = t_lo + ti
                    xt, b_xt = xts[tile % 2]
                    P.dma("sp", ch_x[tile % 2], xt, xcur[tile], reads=[b_xcur[tile]], writes=[b_xt])
                    for half in range(2):
                        expand(3, tile, half, (5, 6))
                        sl = slice(half * 1024, (half + 1) * 1024)
                        pa = ps[:, 5:7, :].rearrange("p a b -> p (a b)")
                        stt("dve", tmpl, f_st[:, ti, sl], rs2[:, ti:ti + 1], pa, ALU.mult, ALU.mult,
                            [b_fst, b_rs2, b_ps[5], b_ps[6]], [b_tmpl])
                        tt("pool", xt[:, sl], xt[:, sl], tmpl, ALU.add, [b_xt, b_tmpl], [b_xt])
                    if l == n_layers - 1:
                        P.dma("sp", ch_x[tile % 2], y_o[tile], xt, reads=[b_xt])
                    else:
                        P.dma("sp", ch_x[tile % 2], xcur[tile], xt, reads=[b_xt], writes=[b_xcur[tile]])

          except _Stop:
            stopped[0] = True
        if stop_after is not None:
            dbg = dout("dbg", [128, 65536])
            dbgx = dout("dbgx", [NT, 128, D])
            allb = list({id(b): b for b in b_R.values()}.values())
            for i in range(4):
                P.dma("pool", mch(), dbg[:, i * 16384:(i + 1) * 16384], arena[:, i * 16384:(i + 1) * 16384], reads=allb)
            for i in range(NT):
                P.dma("sp", mch(), dbgx[i], xcur_u[0][i], reads=[b_xcur_u[0][i]])
        for ch in range(len(P.dcnt)):
            if P.dcnt[ch] > 0:
                P._need("sp", (ch, P.dcnt[ch]))
        if not dry:
            import os
            if os.environ.get("KDBG_LABELS"):
                import json
                json.dump(P.pe_labels, open(os.environ["KDBG_LABELS"], "w"))
            P.emit()
    return nc, rec


def _tables(half):
    hh = np.arange(8, dtype=np.float64)
    g = 1.0 - np.exp2(-5.0 - hh)
    inv = 10000.0 ** (-np.arange(64, dtype=np.float32) / np.float32(64))
    p = np.arange(128)
    cos = np.zeros((128, NT, 64), np.float32); sin = np.zeros((128, NT, 64), np.float32)
    for t in range(NT):
        pos = (half * 1024 + t * 128 + p) if t < NT - 1 else (16384 + p % 8)
        ang = pos.astype(np.float32)[:, None] * inv[None, :].astype(np.float32)
        cos[:, t] = np.cos(ang); sin[:, t] = np.sin(ang)
    dec = np.zeros((128, 2, 2, 8), np.float32)
    for kind in range(2):
        i = p if kind == 0 else p % 8
        dec[:, kind, 0, :] = g[None, :] ** (i[:, None] + 1.0)
        dec[:, kind, 1, :] = g[None, :] ** (-(i[:, None] + 1.0)) * (128.0 ** -0.5)
    gt = np.zeros((128, 2, 8), np.float32)
    gt[:, 0, :] = g ** 128.0
    gt[:, 1, :] = g ** 8.0
    jj = p[:, None]; ii = p[None, :]
    maskr = np.zeros((128, 2, 128), np.float32)
    maskr[:, 0, :] = (jj <= ii)
    maskr[:, 1, :] = (jj <= ii) & (jj // 8 == ii // 8)
    maskg = np.zeros((128, 2, 128), np.float32)
    maskg[:, 0, :] = (jj >= ii)
    maskg[:, 1, :] = (jj // 8 == ii // 8) & (ii % 8 <= jj % 8)
    E = np.zeros((NS, NT, 128), np.float32)
    E[0, 0:NT - 1, :] = 1.0
    for s in range(16):
        E[1 + s, NT - 1, 8 * s:8 * s + 8] = 1.0
    rowm = np.zeros((128, 16), np.float32)
    for s in range(16):
        rowm[8 * s:8 * s + 8, s] = 1.0
    return dict(tab_cos=cos, tab_sin=sin, tab_dec=dec, tab_g=gt, tab_maskr=maskr, tab_maskg=maskg,
                tab_E=E, tab_rowmask=rowm, ident=np.eye(128, dtype=np.float32))


_WNAMES = ["w_ada", "b_ada", "norm_pre1", "norm_post1", "norm_pre2", "norm_post2", "w_in", "sgu_w_s", "sgu_b_s",
           "sgu_ln_g", "sgu_ln_b", "ret_gn_g", "w_branch_a", "w_branch_b", "w_out", "ffn_w_gate", "ffn_w_up",
           "ffn_conv_w", "ffn_conv_b", "ffn_w_down"]


def core_inputs(c, inp):
    b = c % 4
    m = {}
    xp_all = np.asarray(inp["x_prompt"]); xs_all = np.asarray(inp["x_sample"])
    for u in range(2):
        s0 = 32 * b + 16 * u
        xs = xs_all[s0:s0 + 16].reshape(1, 128, D)
        xp = xp_all[b, u * 1024:(u + 1) * 1024].reshape(8, 128, D)
        m["xin%d" % u] = np.ascontiguousarray(np.concatenate([xp, xs], 0))
        m["cin%d" % u] = np.ascontiguousarray(np.concatenate([np.asarray(inp["c_prompt"])[b:b + 1],
                                                              np.asarray(inp["c_sample"])[s0:s0 + 16]], 0))
        m["sret%d" % u] = np.ascontiguousarray(np.asarray(inp["state_ret"])[:, s0:s0 + 16])
        m["sconv%d" % u] = np.ascontiguousarray(np.asarray(inp["state_conv"])[:, s0:s0 + 16].reshape(DEPTH, 32, FF))
        tb = _tables(u)
        m["tab_cos%d" % u] = tb["tab_cos"]; m["tab_sin%d" % u] = tb["tab_sin"]
    for n in _WNAMES:
        m[n] = np.asarray(inp[n])
    tb = _tables(0)
    for k in ("tab_dec", "tab_g", "tab_maskr", "tab_maskg", "tab_E", "tab_rowmask", "ident"):
        m[k] = tb[k]
    return m


_CACHE = {}


def get_nc():
    if "nc" not in _CACHE:
        _, rec = build(plan=None, dry=True)
        nc, _ = build(plan=rec)
        _CACHE["nc"] = nc
    return _CACHE["nc"]


def kernel(**inputs):
    nc = get_nc()
    active = [0, 1, 4, 5]
    base = [core_inputs(b, inputs) for b in range(4)]
    zero = {k: np.zeros_like(v) for k, v in base[0].items()}
    in_maps = [zero] * 8
    in_maps = list(in_maps)
    for b, c in enumerate(active):
        in_maps[c] = base[b]
    res = run_bass_kernel_spmd(nc, in_maps, core_ids=list(range(8)))
    r = res.results
    B, S = 4, 2048
    y_prompt = np.zeros((B, S, D), np.float32)
    y_sample = np.zeros((128, 8, D), np.float32)
    ret_p = np.zeros((DEPTH, B, H, 128, 128), np.float32)
    ret_s = np.zeros((DEPTH, 128, H, 128, 128), np.float32)
    conv_p = np.zeros((DEPTH, B, 2, FF), np.float32)
    conv_s = np.zeros((DEPTH, 128, 2, FF), np.float32)
    v_s = np.zeros((DEPTH, 128, 8, 1024), np.float32)
    for b, c in enumerate(active):
        for u in range(2):
            s0 = 32 * b + 16 * u
            y = r[c]["y%d" % u]
            y_prompt[b, u * 1024:(u + 1) * 1024] = y[0:8].reshape(1024, D)
            y_sample[s0:s0 + 16] = y[8].reshape(16, 8, D)
            ret_s[:, s0:s0 + 16] = r[c]["rs_s%d" % u]
            conv_s[:, s0:s0 + 16] = r[c]["cs%d" % u][:, 0:32].reshape(DEPTH, 16, 2, FF)
            v_s[:, s0:s0 + 16] = r[c]["vs_s%d" % u].reshape(DEPTH, 16, 8, 1024)
        ret_p[:, b] = r[c]["rs_p"]
        conv_p[:, b] = r[c]["cs1"][:, 32:34]
    return (y_prompt, y_sample, ret_p, ret_s, conv_p, conv_s, v_s)
```

```python
import contextlib
import numpy as np
import concourse.bass as bass
import concourse.mybir as mybir
from concourse.bass_utils import run_bass_kernel_spmd

F32 = mybir.dt.float32
BF16 = mybir.dt.bfloat16
ALU = mybir.AluOpType
AF = mybir.ActivationFunctionType
AX = mybir.AxisListType

D = 2048
KB = 16
NT = 9
T = NT * 128
H = 8
FF = 5632
FB = 44
NS = 17
DEPTH = 2
EPS = 1e-6
PW = 256
NSLOT = 3
OFF_U, OFF_V, OFF_Q, OFF_K, OFF_VR, OFF_GR, OFF_GA, OFF_GB = 0, 1024, 2048, 3072, 4096, 5120, 6144, 8192
TG = [(0, 384), (384, 384), (768, 384)]


class _Stop(Exception):
    pass


class Buf:
    __slots__ = ("name", "w", "r", "excl")

    def __init__(self, name, inherit=(), excl=False):
        self.name = name
        self.w = None
        self.r = {}
        self.excl = excl
        for b in inherit:
            if b.w is not None:
                k, v = b.w
                if self.r.get(k, 0) < v:
                    self.r[k] = v
            for k, v in b.r.items():
                if self.r.get(k, 0) < v:
                    self.r[k] = v


class Prog:
    ENG = ("pe", "act", "dve", "pool", "sp")

    def __init__(self, nc, dry=False):
        self.nc = nc
        self.dry = dry
        self.ops = {e: [] for e in self.ENG}
        self.cnt = {e: 0 for e in self.ENG}
        self.seen = {e: {} for e in self.ENG}
        self.pending = {e: False for e in self.ENG}
        self.label = "init"
        self.pe_labels = []
        self.dsem = []
        self.dcnt = []
        if not dry:
            self.sems = {e: nc.alloc_semaphore("s_" + e) for e in self.ENG}

    def chan(self, name):
        self.dsem.append(None if self.dry else self.nc.alloc_semaphore("d_" + name))
        self.dcnt.append(0)
        return len(self.dsem) - 1

    def _need(self, eng, ev):
        key, val = ev
        if self.seen[eng].get(key, 0) >= val:
            return
        self.seen[eng][key] = val
        if self.dry:
            return
        sem = self.sems[key] if isinstance(key, str) else self.dsem[key]
        self.ops[eng].append(("wait", sem, val))

    def _deps(self, eng, reads, writes):
        for b in reads:
            if b.w is not None:
                if not (b.w[0] == eng and eng == "pe"):
                    self._need(eng, b.w)
            if b.excl:
                for k, v in b.r.items():
                    if k != eng:
                        self._need(eng, (k, v))
        for b in writes:
            if b.w is not None and b.w[0] != eng:
                self._need(eng, b.w)
            for k, v in b.r.items():
                if k != eng:
                    self._need(eng, (k, v))

    def _mark(self, ev, reads, writes):
        k, v = ev
        for b in reads:
            if b.r.get(k, 0) < v:
                b.r[k] = v
        for b in writes:
            b.w = ev
            b.r = {}

    def op(self, eng, fn, reads=(), writes=(), sig=True):
        if eng == "pe":
            self.pe_labels.append(self.label)
        self._deps(eng, reads, writes)
        if sig:
            self.cnt[eng] += 1
            ev = (eng, self.cnt[eng])
            if not self.dry:
                self.ops[eng].append(("ins", fn, self.sems[eng], 1))
            self.pending[eng] = False
        else:
            ev = (eng, self.cnt[eng] + 1)
            if not self.dry:
                self.ops[eng].append(("ins", fn, None, 0))
            self.pending[eng] = True
        self._mark(ev, reads, writes)

    def dma(self, q, ch, out, in_, reads=(), writes=(), serial=True):
        if serial and self.dcnt[ch] > 0:
            self._need(q, (ch, self.dcnt[ch]))
        self._deps(q, reads, writes)
        self.dcnt[ch] += 16
        ev = (ch, self.dcnt[ch])
        if not self.dry:
            self.ops[q].append(("ins", lambda e: e.dma_start(out=out, in_=in_), self.dsem[ch], 16))
        self._mark(ev, reads, writes)

    def emit(self):
        nc = self.nc
        for e in self.ENG:
            assert not self.pending[e], e
        with nc.Block() as block:
            def run(engname):
                def f(eng):
                    for o in self.ops[engname]:
                        if o[0] == "wait":
                            eng.wait_ge(o[1], o[2])
                        else:
                            ins = o[1](eng)
                            if o[2] is not None:
                                ins.then_inc(o[2], o[3])
                return f
            block.tensor(run("pe"))
            block.scalar(run("act"))
            block.vector(run("dve"))
            block.gpsimd(run("pool"))
            block.sync(run("sp"))


def _bc(ap, axis, shape):
    return ap.unsqueeze(axis).broadcast_to(list(shape))


def build(plan=None, n_layers=DEPTH, dry=False, stop_after=None, n_units=2):
    nc = bass.Bass("TRN2", target_bir_lowering=False)
    P = Prog(nc, dry=dry)
    rec = []

    def din(name, shape):
        return nc.dram_tensor(name, list(shape), F32, kind="ExternalInput").ap()

    def dout(name, shape):
        return nc.dram_tensor(name, list(shape), F32, kind="ExternalOutput").ap()

    NU = n_units
    xin_u = [din("xin%d" % u, [NT, 128, D]) for u in range(NU)]
    cin_u = [din("cin%d" % u, [NS, D]) for u in range(NU)]
    sret_u = [din("sret%d" % u, [DEPTH, 16, H, 128, 128]) for u in range(NU)]
    sconv_u = [din("sconv%d" % u, [DEPTH, 32, FF]) for u in range(NU)]
    w_ada = din("w_ada", [DEPTH, D, 6 * D]); b_ada = din("b_ada", [DEPTH, 6 * D])
    norms = [din(n, [DEPTH, D]) for n in ("norm_pre1", "norm_post1", "norm_pre2", "norm_post2")]
    w_in = din("w_in", [DEPTH, D, 10240])
    sgu_w_s = din("sgu_w_s", [DEPTH, 8, 128, 128]); sgu_b_s = din("sgu_b_s", [DEPTH, 8, 128])
    sgu_ln_g = din("sgu_ln_g", [DEPTH, 1024]); sgu_ln_b = din("sgu_ln_b", [DEPTH, 1024])
    ret_gn_g = din("ret_gn_g", [DEPTH, 1024])
    w_ba = din("w_branch_a", [DEPTH, 1024, D]); w_bb = din("w_branch_b", [DEPTH, 1024, D])
    w_out = din("w_out", [DEPTH, D, D])
    w_gate = din("ffn_w_gate", [DEPTH, D, FF]); w_up = din("ffn_w_up", [DEPTH, D, FF])
    conv_w = din("ffn_conv_w", [DEPTH, 3, FF]); conv_b = din("ffn_conv_b", [DEPTH, FF])
    w_down = din("ffn_w_down", [DEPTH, FF, D])
    t_cos_u = [din("tab_cos%d" % u, [128, NT, 64]) for u in range(NU)]
    t_sin_u = [din("tab_sin%d" % u, [128, NT, 64]) for u in range(NU)]
    t_dec = din("tab_dec", [128, 2, 2, 8]); t_g = din("tab_g", [128, 2, 8])
    t_maskr = din("tab_maskr", [128, 2, 128]); t_maskg = din("tab_maskg", [128, 2, 128])
    t_E = din("tab_E", [NS, NT, 128]); t_rowm = din("tab_rowmask", [128, 16])
    t_ident = din("ident", [128, 128])
    smid = nc.dram_tensor("smid", [DEPTH, 128, H, 128], F32, kind="Internal").ap()
    halo = nc.dram_tensor("halo", [DEPTH, 2, FF], F32, kind="Internal").ap()
    b_smid = [Buf("smid%d" % i) for i in range(DEPTH)]
    b_halo = [Buf("halo%d" % i) for i in range(DEPTH)]

    y_u = [dout("y%d" % u, [NT, 128, D]) for u in range(NU)]
    rs_p = dout("rs_p", [DEPTH, H, 128, 128])
    rs_s_u = [dout("rs_s%d" % u, [DEPTH, 16, H, 128, 128]) for u in range(NU)]
    cs_u = [dout("cs%d" % u, [DEPTH, 34, FF]) for u in range(NU)]
    vs_u = [dout("vs_s%d" % u, [DEPTH, 128, 1024]) for u in range(NU)]
    xcur_u = [nc.dram_tensor("xcur%d" % u, [NT, 128, D], F32, kind="Internal").ap() for u in range(NU)]
    b_xcur_u = [[Buf("xcur%d_%d" % (u, i)) for i in range(NT)] for u in range(NU)]

    es = contextlib.ExitStack()
    with es:
        def sb(name, shape, dt=F32):
            return es.enter_context(nc.sbuf_tensor(name, list(shape), dt))

        ident_f = sb("ident_f", [128, 128]); ident_b = sb("ident_b", [128, 128], BF16)
        ones_b = sb("ones_b", [128, 128], BF16)
        mhalf = sb("mhalf", [128, 16])
        epsb = sb("epsb", [128, 1])
        Etab = sb("Etab", [NS, NT, 128], BF16)
        dec = sb("dec", [128, 2, 2, 8]); gtab = sb("gtab", [128, 2, 8])
        maskr = sb("maskr", [128, 2, 128]); maskg = sb("maskg", [128, 2, 128])
        rowm = sb("rowm", [128, 16])
        csT_u = [sb("csT%d" % u, [128, KB, NS], BF16) for u in range(NU)]
        gnT = sb("gnT", [128, 8])
        MOD = [sb("mod%d" % i, [NS, D], BF16) for i in range(4)]
        b_const = Buf("const"); b_csT_u = [Buf("csT%d" % u) for u in range(NU)]; b_gnT = Buf("gnT")
        b_MOD = [Buf("mod%d" % i) for i in range(4)]
        arena = sb("arena", [128, 65536], BF16)
        b_R = {k: Buf(k) for k in ("R1", "R2a", "R2b", "R3a", "R3b", "R4")}
        R1 = arena[:, 0:18432]
        R2a = arena[:, 18432:27648]; R2b = arena[:, 27648:36864]
        R3a = arena[:, 36864:46080]; R3b = arena[:, 46080:55296]
        R4 = arena[:, 55296:65536]
        wslots = [sb("wslot%d" % i, [128, KB, PW], BF16) for i in range(NSLOT)]
        b_ws = [Buf("ws%d" % i) for i in range(NSLOT)]
        SCR = 33920
        scr = sb("scr", [128, SCR], mybir.dt.uint8)
        ps = es.enter_context(nc.psum_tensor("ps", [128, 7, 512], F32))
        ptb = es.enter_context(nc.psum_tensor("ptb", [128, 8, 128], BF16))
        b_ps = [Buf("ps%d" % i, excl=True) for i in range(7)]
        b_ptb = Buf("ptb", excl=True)

        ch_w = [P.chan("w%d" % i) for i in range(NSLOT)]
        ch_c = P.chan("const")
        ch_x = [P.chan("x0"), P.chan("x1")]
        ch_o = P.chan("out")
        ch_ms = [P.chan("misc%d" % i) for i in range(12)]
        mrot = {"i": 0}

        def mch():
            mrot["i"] = (mrot["i"] + 1) % len(ch_ms)
            return ch_ms[mrot["i"]]
        ch_s = [P.chan("s0"), P.chan("s1"), P.chan("s2")]
        ch_sb = [P.chan("sb0"), P.chan("sb1")]

        scr_state = {"bufs": [], "off": 0}

        def scratch_reset():
            old = scr_state["bufs"]
            scr_state["bufs"] = []
            scr_state["off"] = 0
            scr_state["inherit"] = old

        scr_state["inherit"] = []

        def scratch_mark():
            return (scr_state["off"], len(scr_state["bufs"]))

        def scratch_reset_to(mark):
            off, nb = mark
            scr_state["inherit"] = list(scr_state["inherit"]) + scr_state["bufs"][nb:]
            scr_state["bufs"] = scr_state["bufs"][:nb]
            scr_state["off"] = off

        def salloc(name, shape, dt=F32):
            esz = 4 if dt == F32 else 2
            n = 1
            for s in shape[1:]:
                n *= s
            nbytes = (n * esz + 31) // 32 * 32
            off = scr_state["off"]
            assert off + nbytes <= SCR, (name, off, nbytes)
            scr_state["off"] = off + nbytes
            flat = scr[0:shape[0], off:off + n * esz].bitcast(dt)
            if len(shape) == 2:
                ap = flat
            elif len(shape) == 3:
                ap = flat.rearrange("p (a b) -> p a b", a=shape[1])
            else:
                ap = flat.rearrange("p (a b c) -> p a b c", a=shape[1], b=shape[2])
            b = Buf(name, inherit=scr_state["inherit"])
            scr_state["bufs"].append(b)
            return ap, b

        def region(name_old_list, name):
            nb = Buf(name, inherit=[b_R[k] for k in name_old_list])
            for k in name_old_list:
                b_R[k] = nb
            return nb

        def mm(out, lhsT, rhs, start, stop, reads, writes, sig):
            P.op("pe", lambda e: e.matmul(out=out, lhsT=lhsT, rhs=rhs, start=start, stop=stop),
                 reads, writes, sig)

        def tr(out, in_, ident, reads, writes, sig):
            P.op("pe", lambda e: e.transpose(out=out, in_=in_, identity=ident), reads, writes, sig)

        def act(out, in_, func, reads, writes, **kw):
            P.op("act", lambda e: e.activation(out=out, in_=in_, func=func, **kw), reads, writes)

        def tt(eng, out, in0, in1, op, reads, writes):
            P.op(eng, lambda e: e.tensor_tensor(out=out, in0=in0, in1=in1, op=op), reads, writes)

        def ts(eng, out, in0, s1, s2, op0, op1, reads, writes):
            if s2 is None:
                P.op(eng, lambda e: e.tensor_scalar(out=out, in0=in0, scalar1=s1, scalar2=None, op0=op0),
                     reads, writes)
            else:
                P.op(eng, lambda e: e.tensor_scalar(out=out, in0=in0, scalar1=s1, scalar2=s2, op0=op0, op1=op1),
                     reads, writes)

        def stt(eng, out, in0, scalar, in1, op0, op1, reads, writes):
            P.op(eng, lambda e: e.scalar_tensor_tensor(out=out, in0=in0, scalar=scalar, in1=in1, op0=op0, op1=op1),
                 reads, writes)

        def cp(eng, out, in_, reads, writes):
            if eng == "act":
                P.op("act", lambda e: e.copy(out=out, in_=in_), reads, writes)
            else:
                P.op(eng, lambda e: e.tensor_copy(out=out, in_=in_), reads, writes)

        def memset(eng, ap, val, writes):
            P.op(eng, lambda e: e.memset(ap, val), (), writes)

        def rsqrt_inplace(ap, buf, shape, in1=None, b_in1=None):
            ts("dve", ap, ap, EPS, None, ALU.add, None, [buf], [buf])
            if in1 is None:
                assert shape[1] <= 16
                in1 = mhalf[0:shape[0], 0:shape[1]]
                b_in1 = b_const
            tt("pool", ap, ap, in1, ALU.pow, [buf, b_in1], [buf])

        wstate = {"i": 0, "issued": 0}

        def wview(slot, ncols):
            if ncols <= PW:
                return wslots[slot]
            return wslots[slot][:].rearrange("p a b -> p (a b)").rearrange("p (a b) -> p a b", b=ncols)

        def w_issue(idx):
            src, K, ncols = plan[idx]
            slot = idx % NSLOT
            P.dma("pool", ch_w[slot], wview(slot, ncols)[:, 0:K // 128, 0:ncols],
                  src.rearrange("(kb p) c -> p kb c", p=128), writes=[b_ws[slot]])

        def wnext(src, K, ncols=PW):
            i = wstate["i"]
            wstate["i"] = i + 1
            if plan is None:
                rec.append((src, K, ncols))
                return wview(i % NSLOT, ncols), b_ws[i % NSLOT]
            assert plan[i][1] == K and plan[i][2] == ncols
            while wstate["issued"] < min(len(plan), i + NSLOT):
                w_issue(wstate["issued"])
                wstate["issued"] += 1
            return wview(i % NSLOT, ncols), b_ws[i % NSLOT]

        def load_consts():
            for dst, src in ((ident_f, t_ident), (dec, t_dec), (gtab, t_g), (maskr, t_maskr),
                             (maskg, t_maskg), (rowm, t_rowm)):
                P.dma("sp", mch(), dst[:], src, writes=[Buf("c")] if False else [b_const])
            P.dma("pool", ch_c, Etab[:], t_E, writes=[b_const])
            cp("dve", ident_b[:], ident_f[:], [b_const], [b_const])
            memset("dve", ones_b[:], 1.0 / 128.0, [b_const])
            memset("dve", mhalf[:], -0.5, [b_const])
            memset("dve", epsb[:], EPS, [b_const])

        def compute_csT(cin, csT, b_csT):
            scratch_reset()
            c_sb, b_c = salloc("c_sb", [NS, D])
            sg_sb, b_sg = salloc("sg_sb", [NS, D])
            P.dma("sp", mch(), c_sb, cin, writes=[b_c])
            act(sg_sb, c_sb, AF.Silu, [b_c], [b_sg])
            pv = ps[:, 0, 0:KB * NS].rearrange("p (a b) -> p a b", a=KB)
            for kb in range(KB):
                tr(pv[:, kb, :], sg_sb[:, kb * 128:(kb + 1) * 128], ident_f[0:NS, 0:NS],
                   [b_sg, b_const], [b_ps[0]], sig=(kb == KB - 1))
            cp("dve", csT[:], pv, [b_ps[0]], [b_csT])

        def mod_scratch():
            bq, b_bq = salloc("bq", [NS, PW]); nq, b_nq = salloc("nq", [NS, PW]); tmp, b_tmp = salloc("mtmp", [NS, PW])
            return (bq, b_bq, nq, b_nq, tmp, b_tmp)

        def mod_steps(l, m, slot, u, bank, sc):
            bq, b_bq, nq, b_nq, tmp, b_tmp = sc
            csT, b_csT = csT_u[u], b_csT_u[u]
            kind = m % 3
            for q in range(D // PW):
                prev_label = P.label
                P.label = "mod"
                c0 = m * D + q * PW
                P.dma("sp", mch(), bq, b_ada[l, c0:c0 + PW].partition_broadcast(NS), writes=[b_bq])
                if kind != 0:
                    nsrc = norms[{1: 0, 2: 1, 4: 2, 5: 3}[m]]
                    P.dma("sp", mch(), nq, nsrc[l, q * PW:(q + 1) * PW].partition_broadcast(NS), writes=[b_nq])
                wp, b_wp = wnext(w_ada[l, :, c0:c0 + PW], D)
                o = ps[0:NS, bank, 0:PW]
                for kb in range(KB):
                    mm(o, csT[:, kb, :], wp[:, kb, :], kb == 0, kb == KB - 1,
                       [b_csT, b_wp], [b_ps[bank]], sig=(kb == KB - 1))
                dst = MOD[slot][:, q * PW:(q + 1) * PW]
                if kind == 0:
                    tt("dve", dst, o, bq, ALU.add, [b_ps[bank], b_bq], [b_MOD[slot]])
                else:
                    tt("dve", tmp, o, bq, ALU.add, [b_ps[bank], b_bq], [b_tmp])
                    if kind == 1:
                        stt("dve", dst, tmp, 1.0, nq, ALU.add, ALU.mult, [b_tmp, b_nq], [b_MOD[slot]])
                    else:
                        tt("dve", dst, tmp, nq, ALU.mult, [b_tmp, b_nq], [b_MOD[slot]])
                P.label = prev_label
                yield

        def mod_piece(l, m, slot, u):
            scratch_reset()
            sc = mod_scratch()
            for _ in mod_steps(l, m, slot, u, 0, sc):
                pass

        def expand(slot, tile, half, banks):
            for j in range(2):
                c0 = half * 1024 + j * 512
                mm(ps[:, banks[j], :], Etab[:, tile, :], MOD[slot][:, c0:c0 + 512], True, True,
                   [b_const, b_MOD[slot]], [b_ps[banks[j]]], sig=True)

        def prenorm(tile, xt, b_xt, slotA, slotB, dstT, b_dst, sc):
            junk, b_junk, ss, b_ss, tmp, b_tmp, htm, b_htm = sc
            act(junk, xt, AF.Square, [b_xt], [b_junk, b_ss], scale=float(D ** -0.5), accum_out=ss)
            rsqrt_inplace(ss, b_ss, [128, 1])
            for half in range(2):
                expand(slotA, tile, half, (0, 1))
                expand(slotB, tile, half, (2, 3))
                sl = slice(half * 1024, (half + 1) * 1024)
                pa = ps[:, 0:2, :].rearrange("p a b -> p (a b)")
                pb = ps[:, 2:4, :].rearrange("p a b -> p (a b)")
                stt("dve", tmp, xt[:, sl], ss, pa, ALU.mult, ALU.mult, [b_xt, b_ss, b_ps[0], b_ps[1]], [b_tmp])
                tt("dve", htm[:, sl], tmp, pb, ALU.add, [b_tmp, b_ps[2], b_ps[3]], [b_htm])
            for g in range(2):
                for j in range(8):
                    kb = g * 8 + j
                    tr(ptb[:, j, :], htm[:, kb * 128:(kb + 1) * 128], ident_b[:], [b_htm, b_const], [b_ptb],
                       sig=(j == 7))
                cp("act", dstT[:, g * 8:(g + 1) * 8, tile * 128:(tile + 1) * 128], ptb[:], [b_ptb], [b_dst])

        def prenorm_scratch():
            junk, b_junk = salloc("junk", [128, D], BF16)
            ss, b_ss = salloc("ss", [128, 1])
            tmp, b_tmp = salloc("ptmp", [128, 1024])
            htm, b_htm = salloc("htm", [128, D], BF16)
            return (junk, b_junk, ss, b_ss, tmp, b_tmp, htm, b_htm)

        def proj_tm(srcT, b_src, tile, wp, b_wp, nkb, bank, ncols=PW):
            o = ps[:, bank, 0:ncols]
            for kb in range(nkb):
                mm(o, srcT[:, kb, tile * 128:(tile + 1) * 128], wp[:, kb, 0:ncols], kb == 0, kb == nkb - 1,
                   [b_src, b_wp], [b_ps[bank]], sig=(kb == nkb - 1))
            return o

        def proj_fm(srcT, b_src, wp, b_wp, blk, nkb, banks, groups=TG):
            outs = [ps[:, banks[gi], 0:n] for gi, (t0, n) in enumerate(groups)]
            for kb in range(nkb):
                for gi, (t0, n) in enumerate(groups):
                    last = (kb == nkb - 1)
                    mm(outs[gi], wp[:, kb, blk * 128:(blk + 1) * 128], srcT[:, kb, t0:t0 + n], kb == 0, last,
                       [b_src, b_wp], [b_ps[banks[gi]]], sig=last)
            return outs

        load_consts()
        rr = {"b": 0}

        def rot(n, mod=7):
            b = rr["b"]
            rr["b"] = (b + n) % mod
            return b

        def chk(name):
            if stop_after == name:
                raise _Stop()

        for u in range(NU):
            compute_csT(cin_u[u], csT_u[u], b_csT_u[u])
        stopped = [False]
        for l in range(n_layers):
         for u in range(NU):
          if stopped[0]:
              break
          try:
            xin, cin, sret, sconv = xin_u[u], cin_u[u], sret_u[u], sconv_u[u]
            t_cos, t_sin = t_cos_u[u], t_sin_u[u]
            y_o, rs_s, cs_o, vs_o, xcur, b_xcur = y_u[u], rs_s_u[u], cs_u[u], vs_u[u], xcur_u[u], b_xcur_u[u]
            xsrc = xin if l == 0 else xcur
            b_xsrc = [Buf("xin%d" % i) for i in range(NT)] if l == 0 else b_xcur
            P.label = "L%d.U%d.A" % (l, u)
            if l == 0 and u == 0:
                mod_piece(l, 1, 0, u)
                mod_piece(l, 0, 1, u)
            scratch_reset()
            rows, b_rows = salloc("rows", [38, 512])
            P.dma("sp", mch(), rows[0:8, 0:128], ret_gn_g[l].rearrange("(h v) -> h v", h=8), writes=[b_rows])
            tr(ps[:, 4, 0:8], rows[0:8, 0:128], ident_f[0:8, 0:8], [b_rows, b_const], [b_ps[4]], sig=True)
            cp("dve", gnT[:], ps[:, 4, 0:8], [b_ps[4]], [b_gnT])

            P.label = "L%d.U%d.B" % (l, u)
            scratch_reset()
            hT = R1.rearrange("p (a b) -> p a b", a=KB)
            b_hT = region(["R1"], "hT")
            psc = prenorm_scratch()
            xts = [salloc("xt%d" % i, [128, D]) for i in range(2)]
            for tile in range(NT):
                xt, b_xt = xts[tile % 2]
                P.dma("sp", ch_x[tile % 2], xt, xsrc[tile], reads=[b_xsrc[tile]], writes=[b_xt])
                prenorm(tile, xt, b_xt, 0, 1, hT, b_hT, psc)
            chk("B")

            P.label = "L%d.U%d.C" % (l, u)
            scratch_reset()
            cosT, b_cos = salloc("cosT", [128, NT, 64])
            sinT, b_sin = salloc("sinT", [128, NT, 64])
            P.dma("sp", mch(), cosT, t_cos, writes=[b_cos])
            P.dma("sp", mch(), sinT, t_sin, writes=[b_sin])
            ra, b_ra = salloc("ra", [128, 2, 64]); rb, b_rb = salloc("rb", [128, 2, 64])
            rt, b_rt = salloc("rt", [128, 2, 128])
            qtms = [salloc("qtm%d" % i, [128, 2, 128], BF16) for i in range(2)]
            k_tm = R2a.rearrange("p (a b) -> p a b", a=NT); b_ktm0 = region(["R2a"], "k_tm")
            b_ktm = [[Buf("k_tm_%d_%d" % (t_, p_), inherit=[b_ktm0]) for p_ in range(4)] for t_ in range(NT)]
            kT = R2b.rearrange("p (a b) -> p a b", a=H); b_kT = region(["R2b"], "kT")
            v_tm = R3a.rearrange("p (a b) -> p a b", a=NT); b_vtm = region(["R3a"], "v_tm")
            qT = R3b.rearrange("p (a b) -> p a b", a=H); b_qT = region(["R3b"], "qT")

            def rope(o, tile, qk, h0, dst, b_dstbuf):
                kind = 1 if tile == NT - 1 else 0
                pv = o.rearrange("p (h d) -> p h d", h=2)
                x1 = pv[:, :, 0:64]; x2 = pv[:, :, 64:128]
                cs = _bc(cosT[:, tile, :], 1, [128, 2, 64]); sn = _bc(sinT[:, tile, :], 1, [128, 2, 64])
                tt("dve", ra, x1, cs, ALU.mult, [pbuf[0], b_cos], [b_ra])
                tt("dve", rb, x2, sn, ALU.mult, [pbuf[0], b_sin], [b_rb])
                tt("dve", rt[:, :, 0:64], ra, rb, ALU.subtract, [b_ra, b_rb], [b_rt])
                tt("dve", ra, x1, sn, ALU.mult, [pbuf[0], b_sin], [b_ra])
                tt("dve", rb, x2, cs, ALU.mult, [pbuf[0], b_cos], [b_rb])
                tt("dve", rt[:, :, 64:128], ra, rb, ALU.add, [b_ra, b_rb], [b_rt])
                dd = _bc(dec[:, kind, qk, h0:h0 + 2], 2, [128, 2, 128])
                tt("dve", dst, rt, dd, ALU.mult, [b_rt, b_const], [b_dstbuf])

            pbuf = [None]
            pend = [None]

            def flush():
                if pend[0] is not None:
                    pend[0]()
                    pend[0] = None

            for p in range(4):
                wp, b_wp = wnext(w_in[l, :, OFF_K + p * PW: OFF_K + (p + 1) * PW], D)
                for tile in range(NT):
                    bank = rot(1)
                    o = proj_tm(hT, b_hT, tile, wp, b_wp, KB, bank)
                    pbuf[0] = b_ps[bank]
                    dst = k_tm[:, tile, p * PW:(p + 1) * PW].rearrange("p (h d) -> p h d", h=2)
                    rope(o, tile, 1, 2 * p, dst, b_ktm[tile][p])
                    flush()

                    def post(p=p, tile=tile):
                        for j in range(2):
                            tr(ptb[:, j, :], k_tm[:, tile, p * PW + j * 128: p * PW + (j + 1) * 128], ident_b[:],
                               [b_ktm[tile][p], b_const], [b_ptb], sig=(j == 1))
                        cp("act", kT[:, 2 * p:2 * p + 2, tile * 128:(tile + 1) * 128], ptb[:, 0:2, :], [b_ptb], [b_kT])
                    pend[0] = post
            flush()
            P.label = "L%d.U%d.Cv" % (l, u)
            for p in range(4):
                wp, b_wp = wnext(w_in[l, :, OFF_VR + p * PW: OFF_VR + (p + 1) * PW], D)
                for tile in range(NT):
                    bank = rot(1)
                    o = proj_tm(hT, b_hT, tile, wp, b_wp, KB, bank)
                    cp("act", v_tm[:, tile, p * PW:(p + 1) * PW], o, [b_ps[bank]], [b_vtm])
            P.label = "L%d.U%d.E" % (l, u)
            qi = 0
            for p in range(4):
                wp, b_wp = wnext(w_in[l, :, OFF_Q + p * PW: OFF_Q + (p + 1) * PW], D)
                for tile in range(NT):
                    bank = rot(1)
                    o = proj_tm(hT, b_hT, tile, wp, b_wp, KB, bank)
                    pbuf[0] = b_ps[bank]
                    qt_i, b_qt_i = qtms[qi % 2]
                    qi += 1
                    rope(o, tile, 0, 2 * p, qt_i, b_qt_i)
                    flush()

                    def post(p=p, tile=tile, qt_i=qt_i, b_qt_i=b_qt_i):
                        for j in range(2):
                            tr(ptb[:, j, :], qt_i[:, j, :], ident_b[:], [b_qt_i, b_const], [b_ptb], sig=(j == 1))
                        cp("act", qT[:, 2 * p:2 * p + 2, tile * 128:(tile + 1) * 128], ptb[:, 0:2, :], [b_ptb], [b_qT])
                    pend[0] = post
            flush()
            P.label = "L%d.U%d.F" % (l, u)
            grT = R4[:, 0:H * T].rearrange("p (a b) -> p a b", a=H); b_grT = region(["R4"], "grT")
            for p in range(4):
                wp, b_wp = wnext(w_in[l, :, OFF_GR + p * PW: OFF_GR + (p + 1) * PW], D)
                for blk in range(2):
                    banks = (0, 1, 2) if blk == 0 else (3, 4, 5)
                    outs = proj_fm(hT, b_hT, wp, b_wp, blk, KB, banks)
                    for gi, (t0, n) in enumerate(TG):
                        act(grT[:, 2 * p + blk, t0:t0 + n], outs[gi], AF.Silu, [b_ps[banks[gi]]], [b_grT])
            chk("F")

            P.label = "L%d.U%d.G" % (l, u)
            scratch_reset()
            S, b_S = salloc("S", [128, H, 128]); S_bf, b_Sbf = salloc("S_bf", [128, H, 128], BF16)
            kmk, b_kmk = salloc("kmk", [128, 1024], BF16)
            gmark = None
            import itertools
            msc = mod_scratch()
            gsteps = itertools.chain(mod_steps(l, 2, 2, u, 6, msc),
                                     mod_steps(l, 4, 0, u, 6, msc))
            gmark = scratch_mark()
            s_sb, b_ssb = salloc("s_sb", [128, H, 128], BF16)
            o_f, b_of = salloc("o_f", [128, H, 128]); o_bf, b_obf = salloc("o_bf", [128, H, 128], BF16)
            osq, b_osq = salloc("osq", [128, H, 128], BF16)
            m2, b_m2 = salloc("m2", [128, H, 128])
            s0b = [salloc("s0b%d" % i, [128, 16, 128], BF16) for i in range(2)]
            if u == 0:
                memset("dve", S, 0.0, [b_S])
            else:
                P.dma("sp", mch(), S, smid[l], reads=[b_smid[l]], writes=[b_S])
            cp("act", S_bf, S, [b_S], [b_Sbf])
            for tile in range(NT):
                kind = 1 if tile == NT - 1 else 0
                tok = slice(tile * 128, (tile + 1) * 128)
                for hb in range(2):
                    for h4 in range(4):
                        h = hb * 4 + h4
                        mm(ps[:, hb, h4 * 128:(h4 + 1) * 128], kT[:, h, tok], qT[:, h, tok], True, True,
                           [b_kT, b_qT], [b_ps[hb]], sig=(h4 == 3))
                    tt("dve", s_sb[:, hb * 4:(hb + 1) * 4, :], ps[:, hb, :].rearrange("p (a b) -> p a b", a=4),
                       _bc(maskr[:, kind, :], 1, [128, 4, 128]), ALU.mult, [b_ps[hb], b_const], [b_ssb])
                chk("Ga")
                po = ps[:, 2:4, :].rearrange("p a (h i) -> p (a h) i", h=4)
                for h in range(H):
                    bo = b_ps[2 + h // 4]
                    mm(po[:, h, :], v_tm[:, tile, h * 128:(h + 1) * 128], s_sb[:, h, :], True, False,
                       [b_vtm, b_ssb], [bo], sig=False)
                    if kind == 0:
                        mm(po[:, h, :], S_bf[:, h, :], qT[:, h, tok], False, True, [b_Sbf, b_qT], [bo], sig=True)
                    else:
                        sbt, b_sbt = s0b[h % 2]
                        P.dma("pool", ch_sb[h % 2], sbt, sret[l, :, h, :, :].rearrange("s d v -> d s v"),
                              writes=[b_sbt])
                        for s in range(16):
                            mm(po[:, h, 8 * s:8 * s + 8], sbt[:, s, :], qT[:, h, tile * 128 + 8 * s: tile * 128 + 8 * s + 8],
                               False, s == 15, [b_sbt, b_qT], [bo], sig=(s == 15))
                chk("Gb1")
                pof = ps[:, 2:4, :].rearrange("p a b -> p (a b)")
                o_f2 = o_f.rearrange("p a b -> p (a b)"); o_bf2 = o_bf.rearrange("p a b -> p (a b)")
                osq2 = osq.rearrange("p a b -> p (a b)"); m22 = m2.rearrange("p a b -> p (a b)")
                cp("act", o_f2, pof, [b_ps[2], b_ps[3]], [b_of])
                chk("Gb2")
                cp("dve", o_bf2, pof, [b_ps[2], b_ps[3]], [b_obf])
                chk("Gb3")
                act(osq2, pof, AF.Square, [b_ps[2], b_ps[3]], [b_osq])
                chk("Gb")
                if kind == 0:
                    pd = ps[:, 4:6, :].rearrange("p a (h i) -> p (a h) i", h=4)
                    for h in range(H):
                        mm(pd[:, h, :], k_tm[:, tile, h * 128:(h + 1) * 128], v_tm[:, tile, h * 128:(h + 1) * 128],
                           True, True, [b_ktm[tile][h // 2], b_vtm], [b_ps[4 + h // 4]], sig=(h % 4 == 3))
                    pdf = ps[:, 4:6, :].rearrange("p a b -> p (a b)")
                    S2 = S.rearrange("p a b -> p (a b)")
                    tt("dve", S2, S2, pdf, ALU.add, [b_S, b_ps[4], b_ps[5]], [b_S])
                    tt("dve", S, S, _bc(gtab[:, 0, :], 2, [128, H, 128]), ALU.mult, [b_S, b_const], [b_S])
                    cp("act", S_bf, S, [b_S], [b_Sbf])
                    if tile == NT - 2:
                        if u == 0 and NU > 1:
                            P.dma("sp", mch(), smid[l], S, reads=[b_S], writes=[b_smid[l]])
                        else:
                            P.dma("sp", mch(), rs_p[l].rearrange("h d v -> d h v"), S, reads=[b_S])
                chk("Gc")
                for hb in range(2):
                    mm(ps[:, hb, :], ones_b[:], o_bf2[:, hb * 512:(hb + 1) * 512], True, True, [b_const, b_obf],
                       [b_ps[hb]], sig=True)
                for hb in range(2):
                    mm(ps[:, 4 + hb, :], ones_b[:], osq2[:, hb * 512:(hb + 1) * 512], True, True, [b_const, b_osq],
                       [b_ps[4 + hb]], sig=True)
                pm = ps[:, 0:2, :].rearrange("p a b -> p (a b)")
                pq = ps[:, 4:6, :].rearrange("p a b -> p (a b)")
                act(m22, pm, AF.Square, [b_ps[0], b_ps[1]], [b_m2])
                tt("dve", m22, pq, m22, ALU.subtract, [b_ps[4], b_ps[5], b_m2], [b_m2])
                act(m22, m22, AF.Sqrt, [b_m2], [b_m2], bias=epsb[:, 0:1], scale=1.0)
                P.op("dve", lambda e: e.reciprocal(out=m22, in_=m22), [b_m2], [b_m2])
                tt("dve", o_f2, o_f2, pm, ALU.subtract, [b_of, b_ps[0], b_ps[1]], [b_of])
                tt("dve", o_f2, o_f2, m22, ALU.mult, [b_of, b_m2], [b_of])
                tt("dve", o_f, o_f, _bc(gnT[:], 2, [128, H, 128]), ALU.mult, [b_of, b_gnT], [b_of])
                tt("dve", grT[:, :, tok], o_f, grT[:, :, tok], ALU.mult, [b_of, b_grT], [b_grT])
                for _ in range(2):
                    next(gsteps, None)
                if tile == 0:
                    chk("Gd")
                if tile == 7:
                    chk("Ge")
                if kind == 1:
                    chk("Gf")
                    P.label = "L%d.U%d.Gs" % (l, u)
                    scratch_reset_to(gmark)
                    s0f = [salloc("s0f%d" % i, [128, H, 128]) for i in range(3)]
                    kmks = [(kmk, b_kmk), salloc("kmk1", [128, 1024], BF16)]
                    for s in range(16):
                        sft, b_sft = s0f[s % 3]
                        kmk_s, b_kmk_s = kmks[s % 2]
                        bp = [(4, 5), (0, 1), (2, 3)][s % 3]
                        P.dma("sp", ch_s[s % 3], sft, sret[l, s].rearrange("h d v -> d h v"), writes=[b_sft])
                        ts("dve", kmk_s, k_tm[:, tile, :], rowm[:, s:s + 1], None, ALU.mult, None,
                           b_ktm[tile] + [b_const], [b_kmk_s])
                        pd = ps[:, bp[0]:bp[0] + 2, :].rearrange("p a (h i) -> p (a h) i", h=4)
                        for h in range(H):
                            mm(pd[:, h, :], kmk_s[:, h * 128:(h + 1) * 128], v_tm[:, tile, h * 128:(h + 1) * 128],
                               True, True, [b_kmk_s, b_vtm], [b_ps[bp[h // 4]]], sig=(h % 4 == 3))
                        pdf = ps[:, bp[0]:bp[0] + 2, :].rearrange("p a b -> p (a b)")
                        sf2 = sft.rearrange("p a b -> p (a b)")
                        tt("dve", sf2, sf2, pdf, ALU.add, [b_sft, b_ps[bp[0]], b_ps[bp[1]]], [b_sft])
                        tt("dve", sft, sft, _bc(gtab[:, 1, :], 2, [128, H, 128]), ALU.mult, [b_sft, b_const], [b_sft])
                        P.dma("sp", ch_s[s % 3], rs_s[l, s].rearrange("h d v -> d h v"), sft, reads=[b_sft])
            for _ in gsteps:
                pass
            y_bT = grT; b_ybT = b_grT
            b_R["R2a"] = Buf("k_tm_done", inherit=[b for row in b_ktm for b in row])
            chk("G")

            P.label = "L%d.U%d.H" % (l, u)
            scratch_reset()
            lnG, b_lnG = salloc("lnG", [128, 1024]); lnB, b_lnB = salloc("lnB", [128, 1024])
            WspT, b_Wsp = salloc("WspT", [128, 2, 8, 128], BF16)
            bsP, b_bsP = salloc("bsP", [128, 8, 128]); bs8, b_bs8 = salloc("bs8", [128, 8, 8])
            hmark = scratch_mark()
            wtmp, b_wtmp = salloc("wtmp", [128, 8, 128]); wmk, b_wmk = salloc("wmk", [128, 8, 128], BF16)
            rep, b_rep = salloc("rep", [128, 8, 8])
            P.dma("sp", mch(), lnG, sgu_ln_g[l, :].partition_broadcast(128), writes=[b_lnG])
            P.dma("sp", mch(), lnB, sgu_ln_b[l, :].partition_broadcast(128), writes=[b_lnB])
            P.dma("sp", mch(), bsP, sgu_b_s[l].partition_broadcast(128), writes=[b_bsP])
            P.dma("sp", mch(), bs8, sgu_b_s[l, :, 0:8].partition_broadcast(128), writes=[b_bs8])
            P.dma("sp", mch(), wtmp, sgu_w_s[l].rearrange("g t s -> t g s"), writes=[b_wtmp])
            for s in range(16):
                P.dma("sp", mch(), rep[8 * s:8 * s + 8, :, :], sgu_w_s[l, :, 0:8, 0:8].rearrange("g t s -> t g s"),
                      writes=[b_rep])
            tt("dve", wmk, wtmp, _bc(maskg[:, 0, :], 1, [128, 8, 128]), ALU.mult, [b_wtmp, b_const], [b_wmk])
            for g in range(8):
                tr(ptb[:, g, :], wmk[:, g, :], ident_b[:], [b_wmk, b_const], [b_ptb], sig=(g == 7))
            cp("act", WspT[:, 0, :, :], ptb[:], [b_ptb], [b_Wsp])
            wmk4 = wmk.rearrange("p g (a b) -> p g a b", a=16)
            in0 = rep.unsqueeze(2).broadcast_to([128, 8, 16, 8])
            in1 = maskg[:, 1, :].rearrange("p (a b) -> p a b", a=16).unsqueeze(1).broadcast_to([128, 8, 16, 8])
            tt("dve", wmk4, in0, in1, ALU.mult, [b_rep, b_const], [b_wmk])
            for g in range(8):
                tr(ptb[:, g, :], wmk[:, g, :], ident_b[:], [b_wmk, b_const], [b_ptb], sig=(g == 7))
            cp("act", WspT[:, 1, :, :], ptb[:], [b_ptb], [b_Wsp])
            scratch_reset_to(hmark)
            vn, b_vn = salloc("vn", [128, 1024])
            sums, b_sums = salloc("sums", [128, NT, 4]); sqs, b_sqs = salloc("sqs", [128, NT, 4])
            mean, b_mean = salloc("mean", [128, NT]); rstd, b_rstd = salloc("rstdv", [128, NT])
            jk, b_jk = salloc("jk", [128, PW], BF16)
            ug, b_ug = salloc("ug", [128, T], BF16)
            ztmp, b_ztmp = salloc("ztmp", [128, T])
            vs_tm = R2a.rearrange("p (a b) -> p a b", a=NT); b_vs = region(["R2a"], "vs_tm")
            y_aT = R2b.rearrange("p (a b) -> p a b", a=H); b_yaT = region(["R2b"], "y_aT")
            for p in range(4):
                wp, b_wp = wnext(w_in[l, :, OFF_V + p * PW: OFF_V + (p + 1) * PW], D)
                for tile in range(NT):
                    bank = rot(1)
                    o = proj_tm(hT, b_hT, tile, wp, b_wp, KB, bank)
                    dstv = vs_tm[:, tile, p * PW:(p + 1) * PW]
                    act(dstv, o, AF.Gelu_apprx_tanh, [b_ps[bank]], [b_vs, b_sums], accum_out=sums[:, tile, p:p + 1])
                    act(jk, dstv, AF.Square, [b_vs], [b_jk, b_sqs], accum_out=sqs[:, tile, p:p + 1])
            P.op("dve", lambda e: e.reduce_sum(out=mean, in_=sums, axis=AX.X), [b_sums], [b_mean])
            P.op("dve", lambda e: e.reduce_sum(out=rstd, in_=sqs, axis=AX.X), [b_sqs], [b_rstd])
            ts("dve", mean, mean, 1.0 / 1024, None, ALU.mult, None, [b_mean], [b_mean])
            ts("dve", rstd, rstd, 1.0 / 1024, None, ALU.mult, None, [b_rstd], [b_rstd])
            msq, b_msq = salloc("msq", [128, NT])
            tt("dve", msq, mean, mean, ALU.mult, [b_mean], [b_msq])
            tt("dve", rstd, rstd, msq, ALU.subtract, [b_rstd, b_msq], [b_rstd])
            rsqrt_inplace(rstd, b_rstd, [128, NT])
            for tile in range(NT):
                ts("dve", vn, vs_tm[:, tile, :], mean[:, tile:tile + 1], rstd[:, tile:tile + 1], ALU.subtract, ALU.mult,
                   [b_vs, b_mean, b_rstd], [b_vn])
                tt("dve", vn, vn, lnG, ALU.mult, [b_vn, b_lnG], [b_vn])
                tt("dve", vn, vn, lnB, ALU.add, [b_vn, b_lnB], [b_vn])
                cp("act", vs_tm[:, tile, :], vn, [b_vn], [b_vs])
                if tile == NT - 1:
                    P.dma("sp", mch(), vs_o[l], vn, reads=[b_vn])
            for p in range(4):
                wp, b_wp = wnext(w_in[l, :, OFF_U + p * PW: OFF_U + (p + 1) * PW], D)
                for blk in range(2):
                    g = 2 * p + blk
                    outs = proj_fm(hT, b_hT, wp, b_wp, blk, KB, (0, 1, 2))
                    for gi, (t0, n) in enumerate(TG):
                        act(ug[:, t0:t0 + n], outs[gi], AF.Gelu_apprx_tanh, [b_ps[gi]], [b_ug])
                    for tile in range(NT):
                        kind = 1 if tile == NT - 1 else 0
                        bank = 3 + tile // 4
                        c0 = (tile % 4) * 128
                        mm(ps[:, bank, c0:c0 + 128], vs_tm[:, tile, g * 128:(g + 1) * 128], WspT[:, kind, g, :],
                           True, True, [b_vs, b_Wsp], [b_ps[bank]], sig=(tile % 4 == 3 or tile == NT - 1))
                    for bq in range(2):
                        tt("dve", ztmp[:, bq * 512:(bq + 1) * 512].rearrange("p (a b) -> p a b", a=4),
                           ps[:, 3 + bq, :].rearrange("p (a b) -> p a b", a=4),
                           _bc(bsP[:, g, :], 1, [128, 4, 128]), ALU.add, [b_ps[3 + bq], b_bsP], [b_ztmp])
                    tt("dve", ztmp[:, 1024:1152].rearrange("p (a b) -> p a b", a=16),
                       ps[:, 5, 0:128].rearrange("p (a b) -> p a b", a=16),
                       _bc(bs8[:, g, :], 1, [128, 16, 8]), ALU.add, [b_ps[5], b_bs8], [b_ztmp])
                    tt("dve", y_aT[:, g, :], ztmp, ug, ALU.mult, [b_ztmp, b_ug], [b_yaT])
            chk("H")

            P.label = "L%d.U%d.I" % (l, u)
            scratch_reset()
            sg1, b_sg1 = salloc("sg1", [128, 2, T], BF16)
            sg2, b_sg2 = salloc("sg2", [128, 2, T], BF16)
            import itertools
            isc = mod_scratch()
            isteps = itertools.chain(mod_steps(l, 3, 1, u, 6, isc),
                                     mod_steps(l, 5, 3, u, 6, isc))
            mergedT = arena[:, 36864:55296].rearrange("p (a b) -> p a b", a=KB)
            b_mg = region(["R3a", "R3b"], "mergedT")
            for p in range(8):
                next(isteps, None)
                next(isteps, None)
                wp, b_wp = wnext(w_in[l, :, OFF_GA + p * PW: OFF_GA + (p + 1) * PW], D)
                for blk in range(2):
                    banks = (0, 1, 2) if blk == 0 else (3, 4, 5)
                    outs = proj_fm(hT, b_hT, wp, b_wp, blk, KB, banks)
                    for gi, (t0, n) in enumerate(TG):
                        act(sg1[:, blk, t0:t0 + n], outs[gi], AF.Sigmoid, [b_ps[banks[gi]]], [b_sg1])
                wp, b_wp = wnext(w_ba[l, :, p * PW:(p + 1) * PW], 1024)
                for blk in range(2):
                    banks = (0, 1, 2) if blk == 0 else (3, 4, 5)
                    outs = proj_fm(y_aT, b_yaT, wp, b_wp, blk, 8, banks)
                    for gi, (t0, n) in enumerate(TG):
                        tt("dve", sg1[:, blk, t0:t0 + n], sg1[:, blk, t0:t0 + n], outs[gi], ALU.mult,
                           [b_sg1, b_ps[banks[gi]]], [b_sg1])
                wp, b_wp = wnext(w_in[l, :, OFF_GB + p * PW: OFF_GB + (p + 1) * PW], D)
                for blk in range(2):
                    banks = (0, 1, 2) if blk == 0 else (3, 4, 5)
                    outs = proj_fm(hT, b_hT, wp, b_wp, blk, KB, banks)
                    for gi, (t0, n) in enumerate(TG):
                        act(sg2[:, blk, t0:t0 + n], outs[gi], AF.Sigmoid, [b_ps[banks[gi]]], [b_sg2])
                wp, b_wp = wnext(w_bb[l, :, p * PW:(p + 1) * PW], 1024)
                for blk in range(2):
                    banks = (0, 1, 2) if blk == 0 else (3, 4, 5)
                    outs = proj_fm(y_bT, b_ybT, wp, b_wp, blk, 8, banks)
                    for gi, (t0, n) in enumerate(TG):
                        tt("dve", sg2[:, blk, t0:t0 + n], sg2[:, blk, t0:t0 + n], outs[gi], ALU.mult,
                           [b_sg2, b_ps[banks[gi]]], [b_sg2])
                    tt("pool", mergedT[:, 2 * p + blk, :], sg1[:, blk, :], sg2[:, blk, :], ALU.add,
                       [b_sg1, b_sg2], [b_mg])
            for _ in isteps:
                pass
            chk("I")

            P.label = "L%d.U%d.J" % (l, u)
            scratch_reset()
            t_st = arena[:, 18432:36864].rearrange("p (a b) -> p a b", a=NT)
            b_tst = region(["R2a", "R2b"], "t_store")
            h2T = R1.rearrange("p (a b) -> p a b", a=KB)
            b_h2T = region(["R1"], "h2T")
            ssq, b_ssq = salloc("ssq", [128, NT, 8]); rs1, b_rs1 = salloc("rs1", [128, NT])
            jk, b_jk = salloc("jk2", [128, PW], BF16)
            psc = prenorm_scratch()
            xts = [salloc("xtj%d" % i, [128, D]) for i in range(2)]
            tmpj, b_tmpj = psc[4], psc[5]
            for p in range(8):
                wp, b_wp = wnext(w_out[l, :, p * PW:(p + 1) * PW], D)
                for tile in range(NT):
                    bank = 4 + rot(1) % 3
                    o = proj_tm(mergedT, b_mg, tile, wp, b_wp, KB, bank)
                    cp("dve", t_st[:, tile, p * PW:(p + 1) * PW], o, [b_ps[bank]], [b_tst])
                    act(jk, o, AF.Square, [b_ps[bank]], [b_jk, b_ssq], accum_out=ssq[:, tile, p:p + 1])
            P.op("dve", lambda e: e.reduce_sum(out=rs1, in_=ssq, axis=AX.X), [b_ssq], [b_rs1])
            ts("dve", rs1, rs1, 1.0 / D, None, ALU.mult, None, [b_rs1], [b_rs1])
            rsqrt_inplace(rs1, b_rs1, [128, NT])
            for tile in range(NT):
                xt, b_xt = xts[tile % 2]
                P.dma("sp", ch_x[tile % 2], xt, xsrc[tile], reads=[b_xsrc[tile]], writes=[b_xt])
                for half in range(2):
                    expand(2, tile, half, (0, 1))
                    sl = slice(half * 1024, (half + 1) * 1024)
                    pa = ps[:, 0:2, :].rearrange("p a b -> p (a b)")
                    stt("dve", tmpj, t_st[:, tile, sl], rs1[:, tile:tile + 1], pa, ALU.mult, ALU.mult,
                        [b_tst, b_rs1, b_ps[0], b_ps[1]], [b_tmpj])
                    tt("pool", xt[:, sl], xt[:, sl], tmpj, ALU.add, [b_xt, b_tmpj], [b_xt])
                P.dma("sp", ch_x[tile % 2], xcur[tile], xt, reads=[b_xt], writes=[b_xcur[tile]])
                prenorm(tile, xt, b_xt, 0, 1, h2T, b_h2T, psc)
            chk("J")

            P.label = "L%d.U%d.L" % (l, u)
            scratch_reset()
            ctab, b_ctab = salloc("ctab", [128, FB, 38])
            gsave, b_gsave = salloc("gsave", [128, FB, 2])
            csave, b_csave = salloc("csave", [128, FB, 34])
            lsc = mod_scratch()
            nxt = (l, u + 1) if u + 1 < NU else ((l + 1, 0) if l + 1 < n_layers else None)
            if nxt is not None:
                import itertools
                lsteps = itertools.chain(mod_steps(nxt[0], 1, 0, nxt[1], 6, lsc), mod_steps(nxt[0], 0, 1, nxt[1], 6, lsc))
            else:
                lsteps = iter(())
            lmark = scratch_mark()
            rows2 = [salloc("rows%d" % i, [38, 512]) for i in range(2)]
            b_rparts = [[Buf("rp%d_%d" % (i, j), inherit=[rows2[i][1]]) for j in range(4)] for i in range(2)]
            for cchunk in range(FF // 512):
                c0 = cchunk * 512
                rows = rows2[cchunk % 2][0]
                brp = b_rparts[cchunk % 2]
                P.dma("sp", mch(), rows[0:32, :], sconv[l, :, c0:c0 + 512], writes=[brp[0]])
                if u == 0:
                    memset("dve", rows[32:34, :], 0.0, [brp[1]])
                else:
                    P.dma("sp", mch(), rows[32:34, :], halo[l, :, c0:c0 + 512], reads=[b_halo[l]], writes=[brp[1]])
                P.dma("sp", mch(), rows[34:37, :], conv_w[l, :, c0:c0 + 512], writes=[brp[2]])
                P.dma("sp", mch(), rows[37:38, :], conv_b[l:l + 1, c0:c0 + 512], writes=[brp[3]])
                pv = ps[:, 4 + cchunk % 2, 0:4 * 38].rearrange("p (a b) -> p a b", a=4)
                for j in range(4):
                    tr(pv[:, j, :], rows[:, j * 128:(j + 1) * 128], ident_f[0:38, 0:38], brp + [b_const],
                       [b_ps[4 + cchunk % 2]], sig=(j == 3))
                cp("dve", ctab[:, cchunk * 4:(cchunk + 1) * 4, :], pv, [b_ps[4 + cchunk % 2]], [b_ctab])
            yT = arena[:, 18432:18432 + FB * 640].rearrange("p (a b) -> p a b", a=FB)
            b_yT = region(["R2a", "R2b", "R3a", "R3b"], "yT")
            f_st = R4.rearrange("p (a b) -> p a b", a=5)
            b_fst = region(["R4"], "f_store")
            batches = [(0, 5), (5, 9)]
            for bi, (t_lo, t_hi) in enumerate(batches):
                ntile = t_hi - t_lo
                tok0 = t_lo * 128
                ntok = ntile * 128
                groups = [(tok0, 320), (tok0 + 320, 320)] if bi == 0 else [(tok0, 512)]
                ng = len(groups)
                P.label = "L%d.U%d.Lgu%d" % (l, u, bi)
                scratch_reset_to(lmark)
                gsets = [(salloc("gext%d" % i, [128, 642]), salloc("gexs%d" % i, [128, 16, 10]),
                          salloc("acc%d" % i, [128, 640]), salloc("ge%d" % i, [128, 640], BF16),
                          salloc("upsb%d" % i, [128, 640], BF16)) for i in range(2)]
                brot = {"i": 0}

                def take_banks():
                    i = brot["i"]
                    brot["i"] = i + 1
                    if ng == 2:
                        b0 = 2 * (i % 3)
                        return (b0, b0 + 1)
                    return (i % 6,)

                for p in range(FB // 2):
                    next(lsteps, None)
                    wg, b_wg = wnext(w_gate[l, :, p * PW:(p + 1) * PW], D)
                    for blk in range(2):
                        fb = 2 * p + blk
                        (gext, b_gext), (gexs, b_gexs), (acc, b_acc), (ge, b_ge), (upsb, b_upsb) = gsets[fb % 2]
                        gb = take_banks()
                        og = proj_fm(h2T, b_h2T, wg, b_wg, blk, KB, gb, groups)
                        w0 = ctab[:, fb, 34:35]; w1 = ctab[:, fb, 35:36]; w2 = ctab[:, fb, 36:37]; cb = ctab[:, fb, 37:38]
                        if bi == 0:
                            cp("dve", gext[:, 0:2], ctab[:, fb, 32:34], [b_ctab], [b_gext])
                            for gi in range(ng):
                                cp("act", gext[:, 2 + gi * 320: 2 + (gi + 1) * 320], og[gi], [b_ps[gb[gi]]], [b_gext])
                            cp("dve", gsave[:, fb, :], gext[:, 640:642], [b_gext], [b_gsave])
                            npr = 640
                        else:
                            cp("dve", gext[:, 0:2], gsave[:, fb, :], [b_gsave], [b_gext])
                            cp("act", gext[:, 2:386], og[0][:, 0:384], [b_ps[gb[0]]], [b_gext])
                            cp("dve", gexs[:, :, 0:2], ctab[:, fb, 0:32].rearrange("p (a b) -> p a b", a=16),
                               [b_ctab], [b_gexs])
                            cp("act", gexs[:, :, 2:10], og[0][:, 384:512].rearrange("p (a b) -> p a b", a=16),
                               [b_ps[gb[0]]], [b_gexs])
                            cp("dve", csave[:, fb, 32:34], gext[:, 384:386], [b_gext], [b_csave])
                            cp("dve", csave[:, fb, 0:32].rearrange("p (a b) -> p a b", a=16), gexs[:, :, 8:10],
                               [b_gexs], [b_csave])
                            npr = 384
                        act(acc[:, 0:npr], gext[:, 2:2 + npr], AF.Identity, [b_gext, b_ctab], [b_acc], scale=w2, bias=cb)
                        stt("dve", acc[:, 0:npr], gext[:, 1:1 + npr], w1, acc[:, 0:npr], ALU.mult, ALU.add,
                            [b_gext, b_ctab, b_acc], [b_acc])
                        stt("dve", acc[:, 0:npr], gext[:, 0:npr], w0, acc[:, 0:npr], ALU.mult, ALU.add,
                            [b_gext, b_ctab, b_acc], [b_acc])
                        if bi == 1:
                            a3 = acc[:, 384:512].rearrange("p (a b) -> p a b", a=16)
                            act(a3, gexs[:, :, 2:10], AF.Identity, [b_gexs, b_ctab], [b_acc], scale=w2, bias=cb)
                            stt("dve", a3, gexs[:, :, 1:9], w1, a3, ALU.mult, ALU.add, [b_gexs, b_ctab, b_acc], [b_acc])
                            stt("dve", a3, gexs[:, :, 0:8], w0, a3, ALU.mult, ALU.add, [b_gexs, b_ctab, b_acc], [b_acc])
                        act(ge[:, 0:ntok], acc[:, 0:ntok], AF.Gelu_apprx_tanh, [b_acc], [b_ge])
                    wu, b_wu = wnext(w_up[l, :, p * PW:(p + 1) * PW], D)
                    for blk in range(2):
                        fb = 2 * p + blk
                        (gext, b_gext), (gexs, b_gexs), (acc, b_acc), (ge, b_ge), (upsb, b_upsb) = gsets[fb % 2]
                        ub = take_banks()
                        ou = proj_fm(h2T, b_h2T, wu, b_wu, blk, KB, ub, groups)
                        for gi, (t0, n) in enumerate(groups):
                            tt("dve", yT[:, fb, t0 - tok0:t0 - tok0 + n], ge[:, t0 - tok0:t0 - tok0 + n], ou[gi], ALU.mult,
                               [b_ge, b_ps[ub[gi]]], [b_yT])
                scratch_reset_to(lmark)
                fsq, b_fsq = salloc("fsq", [128, 5, 4]); rs2, b_rs2 = salloc("rs2", [128, 5])
                jk, b_jk = salloc("jk3", [128, 512], BF16)
                tmpl, b_tmpl = salloc("tmpl", [128, 1024])
                crow, b_crow = salloc("crow", [34, 512])
                xts = [salloc("xtl0", [128, D])] * 2
                if bi == 1:
                    for cchunk in range(FF // 512):
                        for j in range(4):
                            tr(ps[0:34, 6, j * 128:(j + 1) * 128], csave[:, cchunk * 4 + j, :], ident_f[:],
                               [b_csave, b_const], [b_ps[6]], sig=(j == 3))
                        cp("act", crow, ps[0:34, 6, :], [b_ps[6]], [b_crow])
                        P.dma("sp", ch_o, cs_o[l, :, cchunk * 512:(cchunk + 1) * 512], crow, reads=[b_crow])
                        if u == 0 and NU > 1:
                            P.dma("sp", mch(), halo[l, :, cchunk * 512:(cchunk + 1) * 512], crow[32:34, :],
                                  reads=[b_crow], writes=[b_halo[l]])
                P.label = "L%d.U%d.Ldn%d" % (l, u, bi)
                kgs = [(k0, min(8, FB - k0)) for k0 in range(0, FB, 8)]
                for q in range(D // 512):
                    for (k0, nk) in kgs:
                        wp, b_wp = wnext(w_down[l, k0 * 128:(k0 + nk) * 128, q * 512:(q + 1) * 512], nk * 128, 512)
                        for kk in range(nk):
                            kb = k0 + kk
                            for ti in range(ntile):
                                last = (kb == FB - 1)
                                mm(ps[:, ti, :], yT[:, kb, ti * 128:(ti + 1) * 128], wp[:, kk, :], kb == 0, last,
                                   [b_yT, b_wp], [b_ps[ti]], sig=(last or (kk == nk - 1 and ti == ntile - 1)))
                    for ti in range(ntile):
                        cp("dve", f_st[:, ti, q * 512:(q + 1) * 512], ps[:, ti, :], [b_ps[ti]], [b_fst])
                        act(jk, ps[:, ti, :], AF.Square, [b_ps[ti]], [b_jk, b_fsq], accum_out=fsq[:, ti, q:q + 1])
                P.label = "L%d.U%d.Lrs%d" % (l, u, bi)
                P.op("dve", lambda e: e.reduce_sum(out=rs2, in_=fsq, axis=AX.X), [b_fsq], [b_rs2])
                ts("dve", rs2, rs2, 1.0 / D, None, ALU.mult, None, [b_rs2], [b_rs2])
                rsqrt_inplace(rs2, b_rs2, [128, 5])
                for ti in range(ntile):
                    tile = t_lo + ti
                    xt, b_xt = xts[tile % 2]
                    P.dma("sp", ch_x[tile % 2], xt, xcur[tile], reads=[b_xcur[tile]], writes=[b_xt])
                    for half in range(2):
                        expand(3, tile, half, (5, 6))
                        sl = slice(half * 1024, (half + 1) * 1024)
                        pa = ps[:, 5:7, :].rearrange("p a b -> p (a b)")
                        stt("dve", tmpl, f_st[:, ti, sl], rs2[:, ti:ti + 1], pa, ALU.mult, ALU.mult,
                            [b_fst, b_rs2, b_ps[5], b_ps[6]], [b_tmpl])
                        tt("pool", xt[:, sl], xt[:, sl], tmpl, ALU.add, [b_xt, b_tmpl], [b_xt])
                    if l == n_layers - 1:
                        P.dma("sp", ch_x[tile % 2], y_o[tile], xt, reads=[b_xt])
                    else:
                        P.dma("sp", ch_x[tile % 2], xcur[tile], xt, reads=[b_xt], writes=[b_xcur[tile]])

          except _Stop:
            stopped[0] = True
        if stop_after is not None:
            dbg = dout("dbg", [128, 65536])
            dbgx = dout("dbgx", [NT, 128, D])
            allb = list({id(b): b for b in b_R.values()}.values())
            for i in range(4):
                P.dma("pool", mch(), dbg[:, i * 16384:(i + 1) * 16384], arena[:, i * 16384:(i + 1) * 16384], reads=allb)
            for i in range(NT):
                P.dma("sp", mch(), dbgx[i], xcur_u[0][i], reads=[b_xcur_u[0][i]])
        for ch in range(len(P.dcnt)):
            if P.dcnt[ch] > 0:
                P._need("sp", (ch, P.dcnt[ch]))
        if not dry:
            import os
            if os.environ.get("KDBG_LABELS"):
                import json
                json.dump(P.pe_labels, open(os.environ["KDBG_LABELS"], "w"))
            P.emit()
    return nc, rec


def _tables(half):
    hh = np.arange(8, dtype=np.float64)
    g = 1.0 - np.exp2(-5.0 - hh)
    inv = 10000.0 ** (-np.arange(64, dtype=np.float32) / np.float32(64))
    p = np.arange(128)
    cos = np.zeros((128, NT, 64), np.float32); sin = np.zeros((128, NT, 64), np.float32)
    for t in range(NT):
        pos = (half * 1024 + t * 128 + p) if t < NT - 1 else (16384 + p % 8)
        ang = pos.astype(np.float32)[:, None] * inv[None, :].astype(np.float32)
        cos[:, t] = np.cos(ang); sin[:, t] = np.sin(ang)
    dec = np.zeros((128, 2, 2, 8), np.float32)
    for kind in range(2):
        i = p if kind == 0 else p % 8
        dec[:, kind, 0, :] = g[None, :] ** (i[:, None] + 1.0)
        dec[:, kind, 1, :] = g[None, :] ** (-(i[:, None] + 1.0)) * (128.0 ** -0.5)
    gt = np.zeros((128, 2, 8), np.float32)
    gt[:, 0, :] = g ** 128.0
    gt[:, 1, :] = g ** 8.0
    jj = p[:, None]; ii = p[None, :]
    maskr = np.zeros((128, 2, 128), np.float32)
    maskr[:, 0, :] = (jj <= ii)
    maskr[:, 1, :] = (jj <= ii) & (jj // 8 == ii // 8)
    maskg = np.zeros((128, 2, 128), np.float32)
    maskg[:, 0, :] = (jj >= ii)
    maskg[:, 1, :] = (jj // 8 == ii // 8) & (ii % 8 <= jj % 8)
    E = np.zeros((NS, NT, 128), np.float32)
    E[0, 0:NT - 1, :] = 1.0
    for s in range(16):
        E[1 + s, NT - 1, 8 * s:8 * s + 8] = 1.0
    rowm = np.zeros((128, 16), np.float32)
    for s in range(16):
        rowm[8 * s:8 * s + 8, s] = 1.0
    return dict(tab_cos=cos, tab_sin=sin, tab_dec=dec, tab_g=gt, tab_maskr=maskr, tab_maskg=maskg,
                tab_E=E, tab_rowmask=rowm, ident=np.eye(128, dtype=np.float32))


_WNAMES = ["w_ada", "b_ada", "norm_pre1", "norm_post1", "norm_pre2", "norm_post2", "w_in", "sgu_w_s", "sgu_b_s",
           "sgu_ln_g", "sgu_ln_b", "ret_gn_g", "w_branch_a", "w_branch_b", "w_out", "ffn_w_gate", "ffn_w_up",
           "ffn_conv_w", "ffn_conv_b", "ffn_w_down"]


def core_inputs(c, inp):
    b = c % 4
    m = {}
    xp_all = np.asarray(inp["x_prompt"]); xs_all = np.asarray(inp["x_sample"])
    for u in range(2):
        s0 = 32 * b + 16 * u
        xs = xs_all[s0:s0 + 16].reshape(1, 128, D)
        xp = xp_all[b, u * 1024:(u + 1) * 1024].reshape(8, 128, D)
        m["xin%d" % u] = np.ascontiguousarray(np.concatenate([xp, xs], 0))
        m["cin%d" % u] = np.ascontiguousarray(np.concatenate([np.asarray(inp["c_prompt"])[b:b + 1],
                                                              np.asarray(inp["c_sample"])[s0:s0 + 16]], 0))
        m["sret%d" % u] = np.ascontiguousarray(np.asarray(inp["state_ret"])[:, s0:s0 + 16])
        m["sconv%d" % u] = np.ascontiguousarray(np.asarray(inp["state_conv"])[:, s0:s0 + 16].reshape(DEPTH, 32, FF))
        tb = _tables(u)
        m["tab_cos%d" % u] = tb["tab_cos"]; m["tab_sin%d" % u] = tb["tab_sin"]
    for n in _WNAMES:
        m[n] = np.asarray(inp[n])
    tb = _tables(0)
    for k in ("tab_dec", "tab_g", "tab_maskr", "tab_maskg", "tab_E", "tab_rowmask", "ident"):
        m[k] = tb[k]
    return m


_CACHE = {}


def get_nc():
    if "nc" not in _CACHE:
        _, rec = build(plan=None, dry=True)
        nc, _ = build(plan=rec)
        _CACHE["nc"] = nc
    return _CACHE["nc"]


def kernel(**inputs):
    nc = get_nc()
    active = [0, 1, 4, 5]
    base = [core_inputs(b, inputs) for b in range(4)]
    zero = {k: np.zeros_like(v) for k, v in base[0].items()}
    in_maps = [zero] * 8
    in_maps = list(in_maps)
    for b, c in enumerate(active):
        in_maps[c] = base[b]
    res = run_bass_kernel_spmd(nc, in_maps, core_ids=list(range(8)))
    r = res.results
    B, S = 4, 2048
    y_prompt = np.zeros((B, S, D), np.float32)
    y_sample = np.zeros((128, 8, D), np.float32)
    ret_p = np.zeros((DEPTH, B, H, 128, 128), np.float32)
    ret_s = np.zeros((DEPTH, 128, H, 128, 128), np.float32)
    conv_p = np.zeros((DEPTH, B, 2, FF), np.float32)
    conv_s = np.zeros((DEPTH, 128, 2, FF), np.float32)
    v_s = np.zeros((DEPTH, 128, 8, 1024), np.float32)
    for b, c in enumerate(active):
        for u in range(2):
            s0 = 32 * b + 16 * u
            y = r[c]["y%d" % u]
            y_prompt[b, u * 1024:(u + 1) * 1024] = y[0:8].reshape(1024, D)
            y_sample[s0:s0 + 16] = y[8].reshape(16, 8, D)
            ret_s[:, s0:s0 + 16] = r[c]["rs_s%d" % u]
            conv_s[:, s0:s0 + 16] = r[c]["cs%d" % u][:, 0:32].reshape(DEPTH, 16, 2, FF)
            v_s[:, s0:s0 + 16] = r[c]["vs_s%d" % u].reshape(DEPTH, 16, 8, 1024)
        ret_p[:, b] = r[c]["rs_p"]
        conv_p[:, b] = r[c]["cs1"][:, 32:34]
    return (y_prompt, y_sample, ret_p, ret_s, conv_p, conv_s, v_s)
```

```python
import contextlib
import numpy as np
import concourse.bass as bass
import concourse.mybir as mybir
from concourse.bass_utils import run_bass_kernel_spmd

F32 = mybir.dt.float32
BF16 = mybir.dt.bfloat16
ALU = mybir.AluOpType
AF = mybir.ActivationFunctionType
AX = mybir.AxisListType

D = 2048
KB = 16
NT = 9
T = NT * 128
H = 8
FF = 5632
FB = 44
NS = 17
DEPTH = 2
EPS = 1e-6
PW = 256
NSLOT = 3
OFF_U, OFF_V, OFF_Q, OFF_K, OFF_VR, OFF_GR, OFF_GA, OFF_GB = 0, 1024, 2048, 3072, 4096, 5120, 6144, 8192
TG = [(0, 384), (384, 384), (768, 384)]


class _Stop(Exception):
    pass


class Buf:
    __slots__ = ("name", "w", "r", "excl")

    def __init__(self, name, inherit=(), excl=False):
        self.name = name
        self.w = None
        self.r = {}
        self.excl = excl
        for b in inherit:
            if b.w is not None:
                k, v = b.w
                if self.r.get(k, 0) < v:
                    self.r[k] = v
            for k, v in b.r.items():
                if self.r.get(k, 0) < v:
                    self.r[k] = v


class Prog:
    ENG = ("pe", "act", "dve", "pool", "sp")

    def __init__(self, nc, dry=False):
        self.nc = nc
        self.dry = dry
        self.ops = {e: [] for e in self.ENG}
        self.cnt = {e: 0 for e in self.ENG}
        self.seen = {e: {} for e in self.ENG}
        self.pending = {e: False for e in self.ENG}
        self.label = "init"
        self.pe_labels = []
        self.dsem = []
        self.dcnt = []
        if not dry:
            self.sems = {e: nc.alloc_semaphore("s_" + e) for e in self.ENG}

    def chan(self, name):
        self.dsem.append(None if self.dry else self.nc.alloc_semaphore("d_" + name))
        self.dcnt.append(0)
        return len(self.dsem) - 1

    def _need(self, eng, ev):
        key, val = ev
        if self.seen[eng].get(key, 0) >= val:
            return
        self.seen[eng][key] = val
        if self.dry:
            return
        sem = self.sems[key] if isinstance(key, str) else self.dsem[key]
        self.ops[eng].append(("wait", sem, val))

    def _deps(self, eng, reads, writes):
        for b in reads:
            if b.w is not None:
                if not (b.w[0] == eng and eng == "pe"):
                    self._need(eng, b.w)
            if b.excl:
                for k, v in b.r.items():
                    if k != eng:
                        self._need(eng, (k, v))
        for b in writes:
            if b.w is not None and b.w[0] != eng:
                self._need(eng, b.w)
            for k, v in b.r.items():
                if k != eng:
                    self._need(eng, (k, v))

    def _mark(self, ev, reads, writes):
        k, v = ev
        for b in reads:
            if b.r.get(k, 0) < v:
                b.r[k] = v
        for b in writes:
            b.w = ev
            b.r = {}

    def op(self, eng, fn, reads=(), writes=(), sig=True):
        if eng == "pe":
            self.pe_labels.append(self.label)
        self._deps(eng, reads, writes)
        if sig:
            self.cnt[eng] += 1
            ev = (eng, self.cnt[eng])
            if not self.dry:
                self.ops[eng].append(("ins", fn, self.sems[eng], 1))
            self.pending[eng] = False
        else:
            ev = (eng, self.cnt[eng] + 1)
            if not self.dry:
                self.ops[eng].append(("ins", fn, None, 0))
            self.pending[eng] = True
        self._mark(ev, reads, writes)

    def dma(self, q, ch, out, in_, reads=(), writes=(), serial=True):
        if serial and self.dcnt[ch] > 0:
            self._need(q, (ch, self.dcnt[ch]))
        self._deps(q, reads, writes)
        self.dcnt[ch] += 16
        ev = (ch, self.dcnt[ch])
        if not self.dry:
            self.ops[q].append(("ins", lambda e: e.dma_start(out=out, in_=in_), self.dsem[ch], 16))
        self._mark(ev, reads, writes)

    def emit(self):
        nc = self.nc
        for e in self.ENG:
            assert not self.pending[e], e
        with nc.Block() as block:
            def run(engname):
                def f(eng):
                    for o in self.ops[engname]:
                        if o[0] == "wait":
                            eng.wait_ge(o[1], o[2])
                        else:
                            ins = o[1](eng)
                            if o[2] is not None:
                                ins.then_inc(o[2], o[3])
                return f
            block.tensor(run("pe"))
            block.scalar(run("act"))
            block.vector(run("dve"))
            block.gpsimd(run("pool"))
            block.sync(run("sp"))


def _bc(ap, axis, shape):
    return ap.unsqueeze(axis).broadcast_to(list(shape))


def build(plan=None, n_layers=DEPTH, dry=False, stop_after=None, n_units=2):
    nc = bass.Bass("TRN2", target_bir_lowering=False)
    P = Prog(nc, dry=dry)
    rec = []

    def din(name, shape):
        return nc.dram_tensor(name, list(shape), F32, kind="ExternalInput").ap()

    def dout(name, shape):
        return nc.dram_tensor(name, list(shape), F32, kind="ExternalOutput").ap()

    NU = n_units
    xin_u = [din("xin%d" % u, [NT, 128, D]) for u in range(NU)]
    cin_u = [din("cin%d" % u, [NS, D]) for u in range(NU)]
    sret_u = [din("sret%d" % u, [DEPTH, 16, H, 128, 128]) for u in range(NU)]
    sconv_u = [din("sconv%d" % u, [DEPTH, 32, FF]) for u in range(NU)]
    w_ada = din("w_ada", [DEPTH, D, 6 * D]); b_ada = din("b_ada", [DEPTH, 6 * D])
    norms = [din(n, [DEPTH, D]) for n in ("norm_pre1", "norm_post1", "norm_pre2", "norm_post2")]
    w_in = din("w_in", [DEPTH, D, 10240])
    sgu_w_s = din("sgu_w_s", [DEPTH, 8, 128, 128]); sgu_b_s = din("sgu_b_s", [DEPTH, 8, 128])
    sgu_ln_g = din("sgu_ln_g", [DEPTH, 1024]); sgu_ln_b = din("sgu_ln_b", [DEPTH, 1024])
    ret_gn_g = din("ret_gn_g", [DEPTH, 1024])
    w_ba = din("w_branch_a", [DEPTH, 1024, D]); w_bb = din("w_branch_b", [DEPTH, 1024, D])
    w_out = din("w_out", [DEPTH, D, D])
    w_gate = din("ffn_w_gate", [DEPTH, D, FF]); w_up = din("ffn_w_up", [DEPTH, D, FF])
    conv_w = din("ffn_conv_w", [DEPTH, 3, FF]); conv_b = din("ffn_conv_b", [DEPTH, FF])
    w_down = din("ffn_w_down", [DEPTH, FF, D])
    t_cos_u = [din("tab_cos%d" % u, [128, NT, 64]) for u in range(NU)]
    t_sin_u = [din("tab_sin%d" % u, [128, NT, 64]) for u in range(NU)]
    t_dec = din("tab_dec", [128, 2, 2, 8]); t_g = din("tab_g", [128, 2, 8])
    t_maskr = din("tab_maskr", [128, 2, 128]); t_maskg = din("tab_maskg", [128, 2, 128])
    t_E = din("tab_E", [NS, NT, 128]); t_rowm = din("tab_rowmask", [128, 16])
    t_ident = din("ident", [128, 128])
    smid = nc.dram_tensor("smid", [DEPTH, 128, H, 128], F32, kind="Internal").ap()
    halo = nc.dram_tensor("halo", [DEPTH, 2, FF], F32, kind="Internal").ap()
    b_smid = [Buf("smid%d" % i) for i in range(DEPTH)]
    b_halo = [Buf("halo%d" % i) for i in range(DEPTH)]

    y_u = [dout("y%d" % u, [NT, 128, D]) for u in range(NU)]
    rs_p = dout("rs_p", [DEPTH, H, 128, 128])
    rs_s_u = [dout("rs_s%d" % u, [DEPTH, 16, H, 128, 128]) for u in range(NU)]
    cs_u = [dout("cs%d" % u, [DEPTH, 34, FF]) for u in range(NU)]
    vs_u = [dout("vs_s%d" % u, [DEPTH, 128, 1024]) for u in range(NU)]
    xcur_u = [nc.dram_tensor("xcur%d" % u, [NT, 128, D], F32, kind="Internal").ap() for u in range(NU)]
    b_xcur_u = [[Buf("xcur%d_%d" % (u, i)) for i in range(NT)] for u in range(NU)]

    es = contextlib.ExitStack()
    with es:
        def sb(name, shape, dt=F32):
            return es.enter_context(nc.sbuf_tensor(name, list(shape), dt))

        ident_f = sb("ident_f", [128, 128]); ident_b = sb("ident_b", [128, 128], BF16)
        ones_b = sb("ones_b", [128, 128], BF16)
        mhalf = sb("mhalf", [128, 16])
        epsb = sb("epsb", [128, 1])
        Etab = sb("Etab", [NS, NT, 128], BF16)
        dec = sb("dec", [128, 2, 2, 8]); gtab = sb("gtab", [128, 2, 8])
        maskr = sb("maskr", [128, 2, 128]); maskg = sb("maskg", [128, 2, 128])
        rowm = sb("rowm", [128, 16])
        csT_u = [sb("csT%d" % u, [128, KB, NS], BF16) for u in range(NU)]
        gnT = sb("gnT", [128, 8])
        MOD = [sb("mod%d" % i, [NS, D], BF16) for i in range(4)]
        b_const = Buf("const"); b_csT_u = [Buf("csT%d" % u) for u in range(NU)]; b_gnT = Buf("gnT")
        b_MOD = [Buf("mod%d" % i) for i in range(4)]
        arena = sb("arena", [128, 65536], BF16)
        b_R = {k: Buf(k) for k in ("R1", "R2a", "R2b", "R3a", "R3b", "R4")}
        R1 = arena[:, 0:18432]
        R2a = arena[:, 18432:27648]; R2b = arena[:, 27648:36864]
        R3a = arena[:, 36864:46080]; R3b = arena[:, 46080:55296]
        R4 = arena[:, 55296:65536]
        wslots = [sb("wslot%d" % i, [128, KB, PW], BF16) for i in range(NSLOT)]
        b_ws = [Buf("ws%d" % i) for i in range(NSLOT)]
        SCR = 33920
        scr = sb("scr", [128, SCR], mybir.dt.uint8)
        ps = es.enter_context(nc.psum_tensor("ps", [128, 7, 512], F32))
        ptb = es.enter_context(nc.psum_tensor("ptb", [128, 8, 128], BF16))
        b_ps = [Buf("ps%d" % i, excl=True) for i in range(7)]
        b_ptb = Buf("ptb", excl=True)

        ch_w = [P.chan("w%d" % i) for i in range(NSLOT)]
        ch_c = P.chan("const")
        ch_x = [P.chan("x0"), P.chan("x1")]
        ch_o = P.chan("out")
        ch_ms = [P.chan("misc%d" % i) for i in range(12)]
        mrot = {"i": 0}

        def mch():
            mrot["i"] = (mrot["i"] + 1) % len(ch_ms)
            return ch_ms[mrot["i"]]
        ch_s = [P.chan("s0"), P.chan("s1"), P.chan("s2")]
        ch_sb = [P.chan("sb0"), P.chan("sb1")]

        scr_state = {"bufs": [], "off": 0}

        def scratch_reset():
            old = scr_state["bufs"]
            scr_state["bufs"] = []
            scr_state["off"] = 0
            scr_state["inherit"] = old

        scr_state["inherit"] = []

        def scratch_mark():
            return (scr_state["off"], len(scr_state["bufs"]))

        def scratch_reset_to(mark):
            off, nb = mark
            scr_state["inherit"] = list(scr_state["inherit"]) + scr_state["bufs"][nb:]
            scr_state["bufs"] = scr_state["bufs"][:nb]
            scr_state["off"] = off

        def salloc(name, shape, dt=F32):
            esz = 4 if dt == F32 else 2
            n = 1
            for s in shape[1:]:
                n *= s
            nbytes = (n * esz + 31) // 32 * 32
            off = scr_state["off"]
            assert off + nbytes <= SCR, (name, off, nbytes)
            scr_state["off"] = off + nbytes
            flat = scr[0:shape[0], off:off + n * esz].bitcast(dt)
            if len(shape) == 2:
                ap = flat
            elif len(shape) == 3:
                ap = flat.rearrange("p (a b) -> p a b", a=shape[1])
            else:
                ap = flat.rearrange("p (a b c) -> p a b c", a=shape[1], b=shape[2])
            b = Buf(name, inherit=scr_state["inherit"])
            scr_state["bufs"].append(b)
            return ap, b

        def region(name_old_list, name):
            nb = Buf(name, inherit=[b_R[k] for k in name_old_list])
            for k in name_old_list:
                b_R[k] = nb
            return nb

        def mm(out, lhsT, rhs, start, stop, reads, writes, sig):
            P.op("pe", lambda e: e.matmul(out=out, lhsT=lhsT, rhs=rhs, start=start, stop=stop),
                 reads, writes, sig)

        def tr(out, in_, ident, reads, writes, sig):
            P.op("pe", lambda e: e.transpose(out=out, in_=in_, identity=ident), reads, writes, sig)

        def act(out, in_, func, reads, writes, **kw):
            P.op("act", lambda e: e.activation(out=out, in_=in_, func=func, **kw), reads, writes)

        def tt(eng, out, in0, in1, op, reads, writes):
            P.op(eng, lambda e: e.tensor_tensor(out=out, in0=in0, in1=in1, op=op), reads, writes)

        def ts(eng, out, in0, s1, s2, op0, op1, reads, writes):
            if s2 is None:
                P.op(eng, lambda e: e.tensor_scalar(out=out, in0=in0, scalar1=s1, scalar2=None, op0=op0),
                     reads, writes)
            else:
                P.op(eng, lambda e: e.tensor_scalar(out=out, in0=in0, scalar1=s1, scalar2=s2, op0=op0, op1=op1),
                     reads, writes)

        def stt(eng, out, in0, scalar, in1, op0, op1, reads, writes):
            P.op(eng, lambda e: e.scalar_tensor_tensor(out=out, in0=in0, scalar=scalar, in1=in1, op0=op0, op1=op1),
                 reads, writes)

        def cp(eng, out, in_, reads, writes):
            if eng == "act":
                P.op("act", lambda e: e.copy(out=out, in_=in_), reads, writes)
            else:
                P.op(eng, lambda e: e.tensor_copy(out=out, in_=in_), reads, writes)

        def memset(eng, ap, val, writes):
            P.op(eng, lambda e: e.memset(ap, val), (), writes)

        def rsqrt_inplace(ap, buf, shape, in1=None, b_in1=None):
            ts("dve", ap, ap, EPS, None, ALU.add, None, [buf], [buf])
            if in1 is None:
                assert shape[1] <= 16
                in1 = mhalf[0:shape[0], 0:shape[1]]
                b_in1 = b_const
            tt("pool", ap, ap, in1, ALU.pow, [buf, b_in1], [buf])

        wstate = {"i": 0, "issued": 0}

        def wview(slot, ncols):
            if ncols <= PW:
                return wslots[slot]
            return wslots[slot][:].rearrange("p a b -> p (a b)").rearrange("p (a b) -> p a b", b=ncols)

        def w_issue(idx):
            src, K, ncols = plan[idx]
            slot = idx % NSLOT
            P.dma("pool", ch_w[slot], wview(slot, ncols)[:, 0:K // 128, 0:ncols],
                  src.rearrange("(kb p) c -> p kb c", p=128), writes=[b_ws[slot]])

        def wnext(src, K, ncols=PW):
            i = wstate["i"]
            wstate["i"] = i + 1
            if plan is None:
                rec.append((src, K, ncols))
                return wview(i % NSLOT, ncols), b_ws[i % NSLOT]
            assert plan[i][1] == K and plan[i][2] == ncols
            while wstate["issued"] < min(len(plan), i + NSLOT):
                w_issue(wstate["issued"])
                wstate["issued"] += 1
            return wview(i % NSLOT, ncols), b_ws[i % NSLOT]

        def load_consts():
            for dst, src in ((ident_f, t_ident), (dec, t_dec), (gtab, t_g), (maskr, t_maskr),
                             (maskg, t_maskg), (rowm, t_rowm)):
                P.dma("sp", mch(), dst[:], src, writes=[Buf("c")] if False else [b_const])
            P.dma("pool", ch_c, Etab[:], t_E, writes=[b_const])
            cp("dve", ident_b[:], ident_f[:], [b_const], [b_const])
            memset("dve", ones_b[:], 1.0 / 128.0, [b_const])
            memset("dve", mhalf[:], -0.5, [b_const])
            memset("dve", epsb[:], EPS, [b_const])

        def compute_csT(cin, csT, b_csT):
            scratch_reset()
            c_sb, b_c = salloc("c_sb", [NS, D])
            sg_sb, b_sg = salloc("sg_sb", [NS, D])
            P.dma("sp", mch(), c_sb, cin, writes=[b_c])
            act(sg_sb, c_sb, AF.Silu, [b_c], [b_sg])
            pv = ps[:, 0, 0:KB * NS].rearrange("p (a b) -> p a b", a=KB)
            for kb in range(KB):
                tr(pv[:, kb, :], sg_sb[:, kb * 128:(kb + 1) * 128], ident_f[0:NS, 0:NS],
                   [b_sg, b_const], [b_ps[0]], sig=(kb == KB - 1))
            cp("dve", csT[:], pv, [b_ps[0]], [b_csT])

        def mod_scratch():
            bq, b_bq = salloc("bq", [NS, PW]); nq, b_nq = salloc("nq", [NS, PW]); tmp, b_tmp = salloc("mtmp", [NS, PW])
            return (bq, b_bq, nq, b_nq, tmp, b_tmp)

        def mod_steps(l, m, slot, u, bank, sc):
            bq, b_bq, nq, b_nq, tmp, b_tmp = sc
            csT, b_csT = csT_u[u], b_csT_u[u]
            kind = m % 3
            for q in range(D // PW):
                prev_label = P.label
                P.label = "mod"
                c0 = m * D + q * PW
                P.dma("sp", mch(), bq, b_ada[l, c0:c0 + PW].partition_broadcast(NS), writes=[b_bq])
                if kind != 0:
                    nsrc = norms[{1: 0, 2: 1, 4: 2, 5: 3}[m]]
                    P.dma("sp", mch(), nq, nsrc[l, q * PW:(q + 1) * PW].partition_broadcast(NS), writes=[b_nq])
                wp, b_wp = wnext(w_ada[l, :, c0:c0 + PW], D)
                o = ps[0:NS, bank, 0:PW]
                for kb in range(KB):
                    mm(o, csT[:, kb, :], wp[:, kb, :], kb == 0, kb == KB - 1,
                       [b_csT, b_wp], [b_ps[bank]], sig=(kb == KB - 1))
                dst = MOD[slot][:, q * PW:(q + 1) * PW]
                if kind == 0:
                    tt("dve", dst, o, bq, ALU.add, [b_ps[bank], b_bq], [b_MOD[slot]])
                else:
                    tt("dve", tmp, o, bq, ALU.add, [b_ps[bank], b_bq], [b_tmp])
                    if kind == 1:
                        stt("dve", dst, tmp, 1.0, nq, ALU.add, ALU.mult, [b_tmp, b_nq], [b_MOD[slot]])
                    else:
                        tt("dve", dst, tmp, nq, ALU.mult, [b_tmp, b_nq], [b_MOD[slot]])
                P.label = prev_label
                yield

        def mod_piece(l, m, slot, u):
            scratch_reset()
            sc = mod_scratch()
            for _ in mod_steps(l, m, slot, u, 0, sc):
                pass

        def expand(slot, tile, half, banks):
            for j in range(2):
                c0 = half * 1024 + j * 512
                mm(ps[:, banks[j], :], Etab[:, tile, :], MOD[slot][:, c0:c0 + 512], True, True,
                   [b_const, b_MOD[slot]], [b_ps[banks[j]]], sig=True)

        def prenorm(tile, xt, b_xt, slotA, slotB, dstT, b_dst, sc):
            junk, b_junk, ss, b_ss, tmp, b_tmp, htm, b_htm = sc
            act(junk, xt, AF.Square, [b_xt], [b_junk, b_ss], scale=float(D ** -0.5), accum_out=ss)
            rsqrt_inplace(ss, b_ss, [128, 1])
            for half in range(2):
                expand(slotA, tile, half, (0, 1))
                expand(slotB, tile, half, (2, 3))
                sl = slice(half * 1024, (half + 1) * 1024)
                pa = ps[:, 0:2, :].rearrange("p a b -> p (a b)")
                pb = ps[:, 2:4, :].rearrange("p a b -> p (a b)")
                stt("dve", tmp, xt[:, sl], ss, pa, ALU.mult, ALU.mult, [b_xt, b_ss, b_ps[0], b_ps[1]], [b_tmp])
                tt("dve", htm[:, sl], tmp, pb, ALU.add, [b_tmp, b_ps[2], b_ps[3]], [b_htm])
            for g in range(2):
                for j in range(8):
                    kb = g * 8 + j
                    tr(ptb[:, j, :], htm[:, kb * 128:(kb + 1) * 128], ident_b[:], [b_htm, b_const], [b_ptb],
                       sig=(j == 7))
                cp("act", dstT[:, g * 8:(g + 1) * 8, tile * 128:(tile + 1) * 128], ptb[:], [b_ptb], [b_dst])

        def prenorm_scratch():
            junk, b_junk = salloc("junk", [128, D], BF16)
            ss, b_ss = salloc("ss", [128, 1])
            tmp, b_tmp = salloc("ptmp", [128, 1024])
            htm, b_htm = salloc("htm", [128, D], BF16)
            return (junk, b_junk, ss, b_ss, tmp, b_tmp, htm, b_htm)

        def proj_tm(srcT, b_src, tile, wp, b_wp, nkb, bank, ncols=PW):
            o = ps[:, bank, 0:ncols]
            for kb in range(nkb):
                mm(o, srcT[:, kb, tile * 128:(tile + 1) * 128], wp[:, kb, 0:ncols], kb == 0, kb == nkb - 1,
                   [b_src, b_wp], [b_ps[bank]], sig=(kb == nkb - 1))
            return o

        def proj_fm(srcT, b_src, wp, b_wp, blk, nkb, banks, groups=TG):
            outs = [ps[:, banks[gi], 0:n] for gi, (t0, n) in enumerate(groups)]
            for kb in range(nkb):
                for gi, (t0, n) in enumerate(groups):
                    last = (kb == nkb - 1)
                    mm(outs[gi], wp[:, kb, blk * 128:(blk + 1) * 128], srcT[:, kb, t0:t0 + n], kb == 0, last,
                       [b_src, b_wp], [b_ps[banks[gi]]], sig=last)
            return outs

        load_consts()
        rr = {"b": 0}

        def rot(n, mod=7):
            b = rr["b"]
            rr["b"] = (b + n) % mod
            return b

        def chk(name):
            if stop_after == name:
                raise _Stop()

        for u in range(NU):
            compute_csT(cin_u[u], csT_u[u], b_csT_u[u])
        stopped = [False]
        for l in range(n_layers):
         for u in range(NU):
          if stopped[0]:
              break
          try:
            xin, cin, sret, sconv = xin_u[u], cin_u[u], sret_u[u], sconv_u[u]
            t_cos, t_sin = t_cos_u[u], t_sin_u[u]
            y_o, rs_s, cs_o, vs_o, xcur, b_xcur = y_u[u], rs_s_u[u], cs_u[u], vs_u[u], xcur_u[u], b_xcur_u[u]
            xsrc = xin if l == 0 else xcur
            b_xsrc = [Buf("xin%d" % i) for i in range(NT)] if l == 0 else b_xcur
            P.label = "L%d.U%d.A" % (l, u)
            if l == 0 and u == 0:
                mod_piece(l, 1, 0, u)
                mod_piece(l, 0, 1, u)
            scratch_reset()
            rows, b_rows = salloc("rows", [38, 512])
            P.dma("sp", mch(), rows[0:8, 0:128], ret_gn_g[l].rearrange("(h v) -> h v", h=8), writes=[b_rows])
            tr(ps[:, 4, 0:8], rows[0:8, 0:128], ident_f[0:8, 0:8], [b_rows, b_const], [b_ps[4]], sig=True)
            cp("dve", gnT[:], ps[:, 4, 0:8], [b_ps[4]], [b_gnT])

            P.label = "L%d.U%d.B" % (l, u)
            scratch_reset()
            hT = R1.rearrange("p (a b) -> p a b", a=KB)
            b_hT = region(["R1"], "hT")
            psc = prenorm_scratch()
            xts = [salloc("xt%d" % i, [128, D]) for i in range(2)]
            for tile in range(NT):
                xt, b_xt = xts[tile % 2]
                P.dma("sp", ch_x[tile % 2], xt, xsrc[tile], reads=[b_xsrc[tile]], writes=[b_xt])
                prenorm(tile, xt, b_xt, 0, 1, hT, b_hT, psc)
            chk("B")

            P.label = "L%d.U%d.C" % (l, u)
            scratch_reset()
            cosT, b_cos = salloc("cosT", [128, NT, 64])
            sinT, b_sin = salloc("sinT", [128, NT, 64])
            P.dma("sp", mch(), cosT, t_cos, writes=[b_cos])
            P.dma("sp", mch(), sinT, t_sin, writes=[b_sin])
            ra, b_ra = salloc("ra", [128, 2, 64]); rb, b_rb = salloc("rb", [128, 2, 64])
            rt, b_rt = salloc("rt", [128, 2, 128])
            qtms = [salloc("qtm%d" % i, [128, 2, 128], BF16) for i in range(2)]
            k_tm = R2a.rearrange("p (a b) -> p a b", a=NT); b_ktm0 = region(["R2a"], "k_tm")
            b_ktm = [[Buf("k_tm_%d_%d" % (t_, p_), inherit=[b_ktm0]) for p_ in range(4)] for t_ in range(NT)]
            kT = R2b.rearrange("p (a b) -> p a b", a=H); b_kT = region(["R2b"], "kT")
            v_tm = R3a.rearrange("p (a b) -> p a b", a=NT); b_vtm = region(["R3a"], "v_tm")
            qT = R3b.rearrange("p (a b) -> p a b", a=H); b_qT = region(["R3b"], "qT")

            def rope(o, tile, qk, h0, dst, b_dstbuf):
                kind = 1 if tile == NT - 1 else 0
                pv = o.rearrange("p (h d) -> p h d", h=2)
                x1 = pv[:, :, 0:64]; x2 = pv[:, :, 64:128]
                cs = _bc(cosT[:, tile, :], 1, [128, 2, 64]); sn = _bc(sinT[:, tile, :], 1, [128, 2, 64])
                tt("dve", ra, x1, cs, ALU.mult, [pbuf[0], b_cos], [b_ra])
                tt("dve", rb, x2, sn, ALU.mult, [pbuf[0], b_sin], [b_rb])
                tt("dve", rt[:, :, 0:64], ra, rb, ALU.subtract, [b_ra, b_rb], [b_rt])
                tt("dve", ra, x1, sn, ALU.mult, [pbuf[0], b_sin], [b_ra])
                tt("dve", rb, x2, cs, ALU.mult, [pbuf[0], b_cos], [b_rb])
                tt("dve", rt[:, :, 64:128], ra, rb, ALU.add, [b_ra, b_rb], [b_rt])
                dd = _bc(dec[:, kind, qk, h0:h0 + 2], 2, [128, 2, 128])
                tt("dve", dst, rt, dd, ALU.mult, [b_rt, b_const], [b_dstbuf])

            pbuf = [None]
            pend = [None]

            def flush():
                if pend[0] is not None:
                    pend[0]()
                    pend[0] = None

            for p in range(4):
                wp, b_wp = wnext(w_in[l, :, OFF_K + p * PW: OFF_K + (p + 1) * PW], D)
                for tile in range(NT):
                    bank = rot(1)
                    o = proj_tm(hT, b_hT, tile, wp, b_wp, KB, bank)
                    pbuf[0] = b_ps[bank]
                    dst = k_tm[:, tile, p * PW:(p + 1) * PW].rearrange("p (h d) -> p h d", h=2)
                    rope(o, tile, 1, 2 * p, dst, b_ktm[tile][p])
                    flush()

                    def post(p=p, tile=tile):
                        for j in range(2):
                            tr(ptb[:, j, :], k_tm[:, tile, p * PW + j * 128: p * PW + (j + 1) * 128], ident_b[:],
                               [b_ktm[tile][p], b_const], [b_ptb], sig=(j == 1))
                        cp("act", kT[:, 2 * p:2 * p + 2, tile * 128:(tile + 1) * 128], ptb[:, 0:2, :], [b_ptb], [b_kT])
                    pend[0] = post
            flush()
            P.label = "L%d.U%d.Cv" % (l, u)
            for p in range(4):
                wp, b_wp = wnext(w_in[l, :, OFF_VR + p * PW: OFF_VR + (p + 1) * PW], D)
                for tile in range(NT):
                    bank = rot(1)
                    o = proj_tm(hT, b_hT, tile, wp, b_wp, KB, bank)
                    cp("act", v_tm[:, tile, p * PW:(p + 1) * PW], o, [b_ps[bank]], [b_vtm])
            P.label = "L%d.U%d.E" % (l, u)
            qi = 0
            for p in range(4):
                wp, b_wp = wnext(w_in[l, :, OFF_Q + p * PW: OFF_Q + (p + 1) * PW], D)
                for tile in range(NT):
                    bank = rot(1)
                    o = proj_tm(hT, b_hT, tile, wp, b_wp, KB, bank)
                    pbuf[0] = b_ps[bank]
                    qt_i, b_qt_i = qtms[qi % 2]
                    qi += 1
                    rope(o, tile, 0, 2 * p, qt_i, b_qt_i)
                    flush()

                    def post(p=p, tile=tile, qt_i=qt_i, b_qt_i=b_qt_i):
                        for j in range(2):
                            tr(ptb[:, j, :], qt_i[:, j, :], ident_b[:], [b_qt_i, b_const], [b_ptb], sig=(j == 1))
                        cp("act", qT[:, 2 * p:2 * p + 2, tile * 128:(tile + 1) * 128], ptb[:, 0:2, :], [b_ptb], [b_qT])
                    pend[0] = post
            flush()
            P.label = "L%d.U%d.F" % (l, u)
            grT = R4[:, 0:H * T].rearrange("p (a b) -> p a b", a=H); b_grT = region(["R4"], "grT")
            for p in range(4):
                wp, b_wp = wnext(w_in[l, :, OFF_GR + p * PW: OFF_GR + (p + 1) * PW], D)
                for blk in range(2):
                    banks = (0, 1, 2) if blk == 0 else (3, 4, 5)
                    outs = proj_fm(hT, b_hT, wp, b_wp, blk, KB, banks)
                    for gi, (t0, n) in enumerate(TG):
                        act(grT[:, 2 * p + blk, t0:t0 + n], outs[gi], AF.Silu, [b_ps[banks[gi]]], [b_grT])
            chk("F")

            P.label = "L%d.U%d.G" % (l, u)
            scratch_reset()
            S, b_S = salloc("S", [128, H, 128]); S_bf, b_Sbf = salloc("S_bf", [128, H, 128], BF16)
            kmk, b_kmk = salloc("kmk", [128, 1024], BF16)
            gmark = None
            import itertools
            msc = mod_scratch()
            gsteps = itertools.chain(mod_steps(l, 2, 2, u, 6, msc),
                                     mod_steps(l, 4, 0, u, 6, msc),
                                     mod_steps(l, 3, 1, u, 6, msc),
                                     mod_steps(l, 5, 3, u, 6, msc))
            gmark = scratch_mark()
            s_sb, b_ssb = salloc("s_sb", [128, H, 128], BF16)
            o_f, b_of = salloc("o_f", [128, H, 128]); o_bf, b_obf = salloc("o_bf", [128, H, 128], BF16)
            osq, b_osq = salloc("osq", [128, H, 128], BF16)
            m2, b_m2 = salloc("m2", [128, H, 128])
            s0b = [salloc("s0b%d" % i, [128, 16, 128], BF16) for i in range(2)]
            if u == 0:
                memset("dve", S, 0.0, [b_S])
            else:
                P.dma("sp", mch(), S, smid[l], reads=[b_smid[l]], writes=[b_S])
            cp("act", S_bf, S, [b_S], [b_Sbf])
            for tile in range(NT):
                kind = 1 if tile == NT - 1 else 0
                tok = slice(tile * 128, (tile + 1) * 128)
                for hb in range(2):
                    for h4 in range(4):
                        h = hb * 4 + h4
                        mm(ps[:, hb, h4 * 128:(h4 + 1) * 128], kT[:, h, tok], qT[:, h, tok], True, True,
                           [b_kT, b_qT], [b_ps[hb]], sig=(h4 == 3))
                    tt("dve", s_sb[:, hb * 4:(hb + 1) * 4, :], ps[:, hb, :].rearrange("p (a b) -> p a b", a=4),
                       _bc(maskr[:, kind, :], 1, [128, 4, 128]), ALU.mult, [b_ps[hb], b_const], [b_ssb])
                chk("Ga")
                po = ps[:, 2:4, :].rearrange("p a (h i) -> p (a h) i", h=4)
                for h in range(H):
                    bo = b_ps[2 + h // 4]
                    mm(po[:, h, :], v_tm[:, tile, h * 128:(h + 1) * 128], s_sb[:, h, :], True, False,
                       [b_vtm, b_ssb], [bo], sig=False)
                    if kind == 0:
                        mm(po[:, h, :], S_bf[:, h, :], qT[:, h, tok], False, True, [b_Sbf, b_qT], [bo], sig=True)
                    else:
                        sbt, b_sbt = s0b[h % 2]
                        P.dma("pool", ch_sb[h % 2], sbt, sret[l, :, h, :, :].rearrange("s d v -> d s v"),
                              writes=[b_sbt])
                        for s in range(16):
                            mm(po[:, h, 8 * s:8 * s + 8], sbt[:, s, :], qT[:, h, tile * 128 + 8 * s: tile * 128 + 8 * s + 8],
                               False, s == 15, [b_sbt, b_qT], [bo], sig=(s == 15))
                chk("Gb1")
                pof = ps[:, 2:4, :].rearrange("p a b -> p (a b)")
                o_f2 = o_f.rearrange("p a b -> p (a b)"); o_bf2 = o_bf.rearrange("p a b -> p (a b)")
                osq2 = osq.rearrange("p a b -> p (a b)"); m22 = m2.rearrange("p a b -> p (a b)")
                cp("act", o_f2, pof, [b_ps[2], b_ps[3]], [b_of])
                chk("Gb2")
                cp("dve", o_bf2, pof, [b_ps[2], b_ps[3]], [b_obf])
                chk("Gb3")
                act(osq2, pof, AF.Square, [b_ps[2], b_ps[3]], [b_osq])
                chk("Gb")
                if kind == 0:
                    pd = ps[:, 4:6, :].rearrange("p a (h i) -> p (a h) i", h=4)
                    for h in range(H):
                        mm(pd[:, h, :], k_tm[:, tile, h * 128:(h + 1) * 128], v_tm[:, tile, h * 128:(h + 1) * 128],
                           True, True, [b_ktm[tile][h // 2], b_vtm], [b_ps[4 + h // 4]], sig=(h % 4 == 3))
                    pdf = ps[:, 4:6, :].rearrange("p a b -> p (a b)")
                    S2 = S.rearrange("p a b -> p (a b)")
                    tt("dve", S2, S2, pdf, ALU.add, [b_S, b_ps[4], b_ps[5]], [b_S])
                    tt("dve", S, S, _bc(gtab[:, 0, :], 2, [128, H, 128]), ALU.mult, [b_S, b_const], [b_S])
                    cp("act", S_bf, S, [b_S], [b_Sbf])
                    if tile == NT - 2:
                        if u == 0 and NU > 1:
                            P.dma("sp", mch(), smid[l], S, reads=[b_S], writes=[b_smid[l]])
                        else:
                            P.dma("sp", mch(), rs_p[l].rearrange("h d v -> d h v"), S, reads=[b_S])
                chk("Gc")
                for hb in range(2):
                    mm(ps[:, hb, :], ones_b[:], o_bf2[:, hb * 512:(hb + 1) * 512], True, True, [b_const, b_obf],
                       [b_ps[hb]], sig=True)
                for hb in range(2):
                    mm(ps[:, 4 + hb, :], ones_b[:], osq2[:, hb * 512:(hb + 1) * 512], True, True, [b_const, b_osq],
                       [b_ps[4 + hb]], sig=True)
                pm = ps[:, 0:2, :].rearrange("p a b -> p (a b)")
                pq = ps[:, 4:6, :].rearrange("p a b -> p (a b)")
                act(m22, pm, AF.Square, [b_ps[0], b_ps[1]], [b_m2])
                tt("dve", m22, pq, m22, ALU.subtract, [b_ps[4], b_ps[5], b_m2], [b_m2])
                act(m22, m22, AF.Sqrt, [b_m2], [b_m2], bias=epsb[:, 0:1], scale=1.0)
                P.op("dve", lambda e: e.reciprocal(out=m22, in_=m22), [b_m2], [b_m2])
                tt("dve", o_f2, o_f2, pm, ALU.subtract, [b_of, b_ps[0], b_ps[1]], [b_of])
                tt("dve", o_f2, o_f2, m22, ALU.mult, [b_of, b_m2], [b_of])
                tt("dve", o_f, o_f, _bc(gnT[:], 2, [128, H, 128]), ALU.mult, [b_of, b_gnT], [b_of])
                tt("dve", grT[:, :, tok], o_f, grT[:, :, tok], ALU.mult, [b_of, b_grT], [b_grT])
                for _ in range(4):
                    next(gsteps, None)
                if tile == 0:
                    chk("Gd")
                if tile == 7:
                    chk("Ge")
                if kind == 1:
                    chk("Gf")
                    P.label = "L%d.U%d.Gs" % (l, u)
                    scratch_reset_to(gmark)
                    s0f = [salloc("s0f%d" % i, [128, H, 128]) for i in range(3)]
                    for s in range(16):
                        sft, b_sft = s0f[s % 3]
                        P.dma("sp", ch_s[s % 3], sft, sret[l, s].rearrange("h d v -> d h v"), writes=[b_sft])
                        ts("dve", kmk, k_tm[:, tile, :], rowm[:, s:s + 1], None, ALU.mult, None,
                           b_ktm[tile] + [b_const], [b_kmk])
                        pd = ps[:, 4:6, :].rearrange("p a (h i) -> p (a h) i", h=4)
                        for h in range(H):
                            mm(pd[:, h, :], kmk[:, h * 128:(h + 1) * 128], v_tm[:, tile, h * 128:(h + 1) * 128],
                               True, True, [b_kmk, b_vtm], [b_ps[4 + h // 4]], sig=(h % 4 == 3))
                        pdf = ps[:, 4:6, :].rearrange("p a b -> p (a b)")
                        sf2 = sft.rearrange("p a b -> p (a b)")
                        tt("dve", sf2, sf2, pdf, ALU.add, [b_sft, b_ps[4], b_ps[5]], [b_sft])
                        tt("dve", sft, sft, _bc(gtab[:, 1, :], 2, [128, H, 128]), ALU.mult, [b_sft, b_const], [b_sft])
                        P.dma("sp", ch_s[s % 3], rs_s[l, s].rearrange("h d v -> d h v"), sft, reads=[b_sft])
            for _ in gsteps:
                pass
            y_bT = grT; b_ybT = b_grT
            b_R["R2a"] = Buf("k_tm_done", inherit=[b for row in b_ktm for b in row])
            chk("G")

            P.label = "L%d.U%d.H" % (l, u)
            scratch_reset()
            lnG, b_lnG = salloc("lnG", [128, 1024]); lnB, b_lnB = salloc("lnB", [128, 1024])
            WspT, b_Wsp = salloc("WspT", [128, 2, 8, 128], BF16)
            bsP, b_bsP = salloc("bsP", [128, 8, 128]); bs8, b_bs8 = salloc("bs8", [128, 8, 8])
            hmark = scratch_mark()
            wtmp, b_wtmp = salloc("wtmp", [128, 8, 128]); wmk, b_wmk = salloc("wmk", [128, 8, 128], BF16)
            rep, b_rep = salloc("rep", [128, 8, 8])
            P.dma("sp", mch(), lnG, sgu_ln_g[l, :].partition_broadcast(128), writes=[b_lnG])
            P.dma("sp", mch(), lnB, sgu_ln_b[l, :].partition_broadcast(128), writes=[b_lnB])
            P.dma("sp", mch(), bsP, sgu_b_s[l].partition_broadcast(128), writes=[b_bsP])
            P.dma("sp", mch(), bs8, sgu_b_s[l, :, 0:8].partition_broadcast(128), writes=[b_bs8])
            P.dma("sp", mch(), wtmp, sgu_w_s[l].rearrange("g t s -> t g s"), writes=[b_wtmp])
            for s in range(16):
                P.dma("sp", mch(), rep[8 * s:8 * s + 8, :, :], sgu_w_s[l, :, 0:8, 0:8].rearrange("g t s -> t g s"),
                      writes=[b_rep])
            tt("dve", wmk, wtmp, _bc(maskg[:, 0, :], 1, [128, 8, 128]), ALU.mult, [b_wtmp, b_const], [b_wmk])
            for g in range(8):
                tr(ptb[:, g, :], wmk[:, g, :], ident_b[:], [b_wmk, b_const], [b_ptb], sig=(g == 7))
            cp("act", WspT[:, 0, :, :], ptb[:], [b_ptb], [b_Wsp])
            wmk4 = wmk.rearrange("p g (a b) -> p g a b", a=16)
            in0 = rep.unsqueeze(2).broadcast_to([128, 8, 16, 8])
            in1 = maskg[:, 1, :].rearrange("p (a b) -> p a b", a=16).unsqueeze(1).broadcast_to([128, 8, 16, 8])
            tt("dve", wmk4, in0, in1, ALU.mult, [b_rep, b_const], [b_wmk])
            for g in range(8):
                tr(ptb[:, g, :], wmk[:, g, :], ident_b[:], [b_wmk, b_const], [b_ptb], sig=(g == 7))
            cp("act", WspT[:, 1, :, :], ptb[:], [b_ptb], [b_Wsp])
            scratch_reset_to(hmark)
            vn, b_vn = salloc("vn", [128, 1024])
            sums, b_sums = salloc("sums", [128, NT, 4]); sqs, b_sqs = salloc("sqs", [128, NT, 4])
            mean, b_mean = salloc("mean", [128, NT]); rstd, b_rstd = salloc("rstdv", [128, NT])
            jk, b_jk = salloc("jk", [128, PW], BF16)
            ug, b_ug = salloc("ug", [128, T], BF16)
            ztmp, b_ztmp = salloc("ztmp", [128, T])
            vs_tm = R2a.rearrange("p (a b) -> p a b", a=NT); b_vs = region(["R2a"], "vs_tm")
            y_aT = R2b.rearrange("p (a b) -> p a b", a=H); b_yaT = region(["R2b"], "y_aT")
            for p in range(4):
                wp, b_wp = wnext(w_in[l, :, OFF_V + p * PW: OFF_V + (p + 1) * PW], D)
                for tile in range(NT):
                    bank = rot(1)
                    o = proj_tm(hT, b_hT, tile, wp, b_wp, KB, bank)
                    dstv = vs_tm[:, tile, p * PW:(p + 1) * PW]
                    act(dstv, o, AF.Gelu_apprx_tanh, [b_ps[bank]], [b_vs, b_sums], accum_out=sums[:, tile, p:p + 1])
                    act(jk, dstv, AF.Square, [b_vs], [b_jk, b_sqs], accum_out=sqs[:, tile, p:p + 1])
            P.op("dve", lambda e: e.reduce_sum(out=mean, in_=sums, axis=AX.X), [b_sums], [b_mean])
            P.op("dve", lambda e: e.reduce_sum(out=rstd, in_=sqs, axis=AX.X), [b_sqs], [b_rstd])
            ts("dve", mean, mean, 1.0 / 1024, None, ALU.mult, None, [b_mean], [b_mean])
            ts("dve", rstd, rstd, 1.0 / 1024, None, ALU.mult, None, [b_rstd], [b_rstd])
            msq, b_msq = salloc("msq", [128, NT])
            tt("dve", msq, mean, mean, ALU.mult, [b_mean], [b_msq])
            tt("dve", rstd, rstd, msq, ALU.subtract, [b_rstd, b_msq], [b_rstd])
            rsqrt_inplace(rstd, b_rstd, [128, NT])
            for tile in range(NT):
                ts("dve", vn, vs_tm[:, tile, :], mean[:, tile:tile + 1], rstd[:, tile:tile + 1], ALU.subtract, ALU.mult,
                   [b_vs, b_mean, b_rstd], [b_vn])
                tt("dve", vn, vn, lnG, ALU.mult, [b_vn, b_lnG], [b_vn])
                tt("dve", vn, vn, lnB, ALU.add, [b_vn, b_lnB], [b_vn])
                cp("act", vs_tm[:, tile, :], vn, [b_vn], [b_vs])
                if tile == NT - 1:
                    P.dma("sp", mch(), vs_o[l], vn, reads=[b_vn])
            for p in range(4):
                wp, b_wp = wnext(w_in[l, :, OFF_U + p * PW: OFF_U + (p + 1) * PW], D)
                for blk in range(2):
                    g = 2 * p + blk
                    outs = proj_fm(hT, b_hT, wp, b_wp, blk, KB, (0, 1, 2))
                    for gi, (t0, n) in enumerate(TG):
                        act(ug[:, t0:t0 + n], outs[gi], AF.Gelu_apprx_tanh, [b_ps[gi]], [b_ug])
                    for tile in range(NT):
                        kind = 1 if tile == NT - 1 else 0
                        bank = 3 + tile // 4
                        c0 = (tile % 4) * 128
                        mm(ps[:, bank, c0:c0 + 128], vs_tm[:, tile, g * 128:(g + 1) * 128], WspT[:, kind, g, :],
                           True, True, [b_vs, b_Wsp], [b_ps[bank]], sig=(tile % 4 == 3 or tile == NT - 1))
                    for bq in range(2):
                        tt("dve", ztmp[:, bq * 512:(bq + 1) * 512].rearrange("p (a b) -> p a b", a=4),
                           ps[:, 3 + bq, :].rearrange("p (a b) -> p a b", a=4),
                           _bc(bsP[:, g, :], 1, [128, 4, 128]), ALU.add, [b_ps[3 + bq], b_bsP], [b_ztmp])
                    tt("dve", ztmp[:, 1024:1152].rearrange("p (a b) -> p a b", a=16),
                       ps[:, 5, 0:128].rearrange("p (a b) -> p a b", a=16),
                       _bc(bs8[:, g, :], 1, [128, 16, 8]), ALU.add, [b_ps[5], b_bs8], [b_ztmp])
                    tt("dve", y_aT[:, g, :], ztmp, ug, ALU.mult, [b_ztmp, b_ug], [b_yaT])
            chk("H")

            P.label = "L%d.U%d.I" % (l, u)
            scratch_reset()
            sg1, b_sg1 = salloc("sg1", [128, 2, T], BF16)
            sg2, b_sg2 = salloc("sg2", [128, 2, T], BF16)
            mergedT = arena[:, 36864:55296].rearrange("p (a b) -> p a b", a=KB)
            b_mg = region(["R3a", "R3b"], "mergedT")
            for p in range(8):
                wp, b_wp = wnext(w_in[l, :, OFF_GA + p * PW: OFF_GA + (p + 1) * PW], D)
                for blk in range(2):
                    banks = (0, 1, 2) if blk == 0 else (3, 4, 5)
                    outs = proj_fm(hT, b_hT, wp, b_wp, blk, KB, banks)
                    for gi, (t0, n) in enumerate(TG):
                        act(sg1[:, blk, t0:t0 + n], outs[gi], AF.Sigmoid, [b_ps[banks[gi]]], [b_sg1])
                wp, b_wp = wnext(w_ba[l, :, p * PW:(p + 1) * PW], 1024)
                for blk in range(2):
                    banks = (0, 1, 2) if blk == 0 else (3, 4, 5)
                    outs = proj_fm(y_aT, b_yaT, wp, b_wp, blk, 8, banks)
                    for gi, (t0, n) in enumerate(TG):
                        tt("dve", sg1[:, blk, t0:t0 + n], sg1[:, blk, t0:t0 + n], outs[gi], ALU.mult,
                           [b_sg1, b_ps[banks[gi]]], [b_sg1])
                wp, b_wp = wnext(w_in[l, :, OFF_GB + p * PW: OFF_GB + (p + 1) * PW], D)
                for blk in range(2):
                    banks = (0, 1, 2) if blk == 0 else (3, 4, 5)
                    outs = proj_fm(hT, b_hT, wp, b_wp, blk, KB, banks)
                    for gi, (t0, n) in enumerate(TG):
                        act(sg2[:, blk, t0:t0 + n], outs[gi], AF.Sigmoid, [b_ps[banks[gi]]], [b_sg2])
                wp, b_wp = wnext(w_bb[l, :, p * PW:(p + 1) * PW], 1024)
                for blk in range(2):
                    banks = (0, 1, 2) if blk == 0 else (3, 4, 5)
                    outs = proj_fm(y_bT, b_ybT, wp, b_wp, blk, 8, banks)
                    for gi, (t0, n) in enumerate(TG):
                        tt("dve", sg2[:, blk, t0:t0 + n], sg2[:, blk, t0:t0 + n], outs[gi], ALU.mult,
                           [b_sg2, b_ps[banks[gi]]], [b_sg2])
                    tt("pool", mergedT[:, 2 * p + blk, :], sg1[:, blk, :], sg2[:, blk, :], ALU.add,
                       [b_sg1, b_sg2], [b_mg])
            chk("I")

            P.label = "L%d.U%d.J" % (l, u)
            scratch_reset()
            t_st = arena[:, 18432:36864].rearrange("p (a b) -> p a b", a=NT)
            b_tst = region(["R2a", "R2b"], "t_store")
            h2T = R1.rearrange("p (a b) -> p a b", a=KB)
            b_h2T = region(["R1"], "h2T")
            ssq, b_ssq = salloc("ssq", [128, NT, 8]); rs1, b_rs1 = salloc("rs1", [128, NT])
            jk, b_jk = salloc("jk2", [128, PW], BF16)
            psc = prenorm_scratch()
            xts = [salloc("xtj%d" % i, [128, D]) for i in range(2)]
            tmpj, b_tmpj = psc[4], psc[5]
            for p in range(8):
                wp, b_wp = wnext(w_out[l, :, p * PW:(p + 1) * PW], D)
                for tile in range(NT):
                    bank = 4 + rot(1) % 3
                    o = proj_tm(mergedT, b_mg, tile, wp, b_wp, KB, bank)
                    cp("dve", t_st[:, tile, p * PW:(p + 1) * PW], o, [b_ps[bank]], [b_tst])
                    act(jk, o, AF.Square, [b_ps[bank]], [b_jk, b_ssq], accum_out=ssq[:, tile, p:p + 1])
            P.op("dve", lambda e: e.reduce_sum(out=rs1, in_=ssq, axis=AX.X), [b_ssq], [b_rs1])
            ts("dve", rs1, rs1, 1.0 / D, None, ALU.mult, None, [b_rs1], [b_rs1])
            rsqrt_inplace(rs1, b_rs1, [128, NT])
            for tile in range(NT):
                xt, b_xt = xts[tile % 2]
                P.dma("sp", ch_x[tile % 2], xt, xsrc[tile], reads=[b_xsrc[tile]], writes=[b_xt])
                for half in range(2):
                    expand(2, tile, half, (0, 1))
                    sl = slice(half * 1024, (half + 1) * 1024)
                    pa = ps[:, 0:2, :].rearrange("p a b -> p (a b)")
                    stt("dve", tmpj, t_st[:, tile, sl], rs1[:, tile:tile + 1], pa, ALU.mult, ALU.mult,
                        [b_tst, b_rs1, b_ps[0], b_ps[1]], [b_tmpj])
                    tt("pool", xt[:, sl], xt[:, sl], tmpj, ALU.add, [b_xt, b_tmpj], [b_xt])
                P.dma("sp", ch_x[tile % 2], xcur[tile], xt, reads=[b_xt], writes=[b_xcur[tile]])
                prenorm(tile, xt, b_xt, 0, 1, h2T, b_h2T, psc)
            chk("J")

            P.label = "L%d.U%d.L" % (l, u)
            scratch_reset()
            ctab, b_ctab = salloc("ctab", [128, FB, 38])
            gsave, b_gsave = salloc("gsave", [128, FB, 2])
            csave, b_csave = salloc("csave", [128, FB, 34])
            lsc = mod_scratch()
            nxt = (l, u + 1) if u + 1 < NU else ((l + 1, 0) if l + 1 < n_layers else None)
            if nxt is not None:
                import itertools
                lsteps = itertools.chain(mod_steps(nxt[0], 1, 0, nxt[1], 6, lsc), mod_steps(nxt[0], 0, 1, nxt[1], 6, lsc))
            else:
                lsteps = iter(())
            lmark = scratch_mark()
            rows2 = [salloc("rows%d" % i, [38, 512]) for i in range(2)]
            b_rparts = [[Buf("rp%d_%d" % (i, j), inherit=[rows2[i][1]]) for j in range(4)] for i in range(2)]
            for cchunk in range(FF // 512):
                c0 = cchunk * 512
                rows = rows2[cchunk % 2][0]
                brp = b_rparts[cchunk % 2]
                P.dma("sp", mch(), rows[0:32, :], sconv[l, :, c0:c0 + 512], writes=[brp[0]])
                if u == 0:
                    memset("dve", rows[32:34, :], 0.0, [brp[1]])
                else:
                    P.dma("sp", mch(), rows[32:34, :], halo[l, :, c0:c0 + 512], reads=[b_halo[l]], writes=[brp[1]])
                P.dma("sp", mch(), rows[34:37, :], conv_w[l, :, c0:c0 + 512], writes=[brp[2]])
                P.dma("sp", mch(), rows[37:38, :], conv_b[l:l + 1, c0:c0 + 512], writes=[brp[3]])
                pv = ps[:, 4 + cchunk % 2, 0:4 * 38].rearrange("p (a b) -> p a b", a=4)
                for j in range(4):
                    tr(pv[:, j, :], rows[:, j * 128:(j + 1) * 128], ident_f[0:38, 0:38], brp + [b_const],
                       [b_ps[4 + cchunk % 2]], sig=(j == 3))
                cp("dve", ctab[:, cchunk * 4:(cchunk + 1) * 4, :], pv, [b_ps[4 + cchunk % 2]], [b_ctab])
            yT = arena[:, 18432:18432 + FB * 640].rearrange("p (a b) -> p a b", a=FB)
            b_yT = region(["R2a", "R2b", "R3a", "R3b"], "yT")
            f_st = R4.rearrange("p (a b) -> p a b", a=5)
            b_fst = region(["R4"], "f_store")
            batches = [(0, 5), (5, 9)]
            for bi, (t_lo, t_hi) in enumerate(batches):
                ntile = t_hi - t_lo
                tok0 = t_lo * 128
                ntok = ntile * 128
                groups = [(tok0, 320), (tok0 + 320, 320)] if bi == 0 else [(tok0, 512)]
                ng = len(groups)
                P.label = "L%d.U%d.Lgu%d" % (l, u, bi)
                scratch_reset_to(lmark)
                gsets = [(salloc("gext%d" % i, [128, 642]), salloc("gexs%d" % i, [128, 16, 10]),
                          salloc("acc%d" % i, [128, 640]), salloc("ge%d" % i, [128, 640], BF16),
                          salloc("upsb%d" % i, [128, 640], BF16)) for i in range(2)]
                brot = {"i": 0}

                def take_banks():
                    i = brot["i"]
                    brot["i"] = i + 1
                    if ng == 2:
                        b0 = 2 * (i % 3)
                        return (b0, b0 + 1)
                    return (i % 6,)

                for p in range(FB // 2):
                    next(lsteps, None)
                    wg, b_wg = wnext(w_gate[l, :, p * PW:(p + 1) * PW], D)
                    for blk in range(2):
                        fb = 2 * p + blk
                        (gext, b_gext), (gexs, b_gexs), (acc, b_acc), (ge, b_ge), (upsb, b_upsb) = gsets[fb % 2]
                        gb = take_banks()
                        og = proj_fm(h2T, b_h2T, wg, b_wg, blk, KB, gb, groups)
                        w0 = ctab[:, fb, 34:35]; w1 = ctab[:, fb, 35:36]; w2 = ctab[:, fb, 36:37]; cb = ctab[:, fb, 37:38]
                        if bi == 0:
                            cp("dve", gext[:, 0:2], ctab[:, fb, 32:34], [b_ctab], [b_gext])
                            for gi in range(ng):
                                cp("act", gext[:, 2 + gi * 320: 2 + (gi + 1) * 320], og[gi], [b_ps[gb[gi]]], [b_gext])
                            cp("dve", gsave[:, fb, :], gext[:, 640:642], [b_gext], [b_gsave])
                            npr = 640
                        else:
                            cp("dve", gext[:, 0:2], gsave[:, fb, :], [b_gsave], [b_gext])
                            cp("act", gext[:, 2:386], og[0][:, 0:384], [b_ps[gb[0]]], [b_gext])
                            cp("dve", gexs[:, :, 0:2], ctab[:, fb, 0:32].rearrange("p (a b) -> p a b", a=16),
                               [b_ctab], [b_gexs])
                            cp("act", gexs[:, :, 2:10], og[0][:, 384:512].rearrange("p (a b) -> p a b", a=16),
                               [b_ps[gb[0]]], [b_gexs])
                            cp("dve", csave[:, fb, 32:34], gext[:, 384:386], [b_gext], [b_csave])
                            cp("dve", csave[:, fb, 0:32].rearrange("p (a b) -> p a b", a=16), gexs[:, :, 8:10],
                               [b_gexs], [b_csave])
                            npr = 384
                        act(acc[:, 0:npr], gext[:, 2:2 + npr], AF.Identity, [b_gext, b_ctab], [b_acc], scale=w2, bias=cb)
                        stt("dve", acc[:, 0:npr], gext[:, 1:1 + npr], w1, acc[:, 0:npr], ALU.mult, ALU.add,
                            [b_gext, b_ctab, b_acc], [b_acc])
                        stt("dve", acc[:, 0:npr], gext[:, 0:npr], w0, acc[:, 0:npr], ALU.mult, ALU.add,
                            [b_gext, b_ctab, b_acc], [b_acc])
                        if bi == 1:
                            a3 = acc[:, 384:512].rearrange("p (a b) -> p a b", a=16)
                            act(a3, gexs[:, :, 2:10], AF.Identity, [b_gexs, b_ctab], [b_acc], scale=w2, bias=cb)
                            stt("dve", a3, gexs[:, :, 1:9], w1, a3, ALU.mult, ALU.add, [b_gexs, b_ctab, b_acc], [b_acc])
                            stt("dve", a3, gexs[:, :, 0:8], w0, a3, ALU.mult, ALU.add, [b_gexs, b_ctab, b_acc], [b_acc])
                        act(ge[:, 0:ntok], acc[:, 0:ntok], AF.Gelu_apprx_tanh, [b_acc], [b_ge])
                    wu, b_wu = wnext(w_up[l, :, p * PW:(p + 1) * PW], D)
                    for blk in range(2):
                        fb = 2 * p + blk
                        (gext, b_gext), (gexs, b_gexs), (acc, b_acc), (ge, b_ge), (upsb, b_upsb) = gsets[fb % 2]
                        ub = take_banks()
                        ou = proj_fm(h2T, b_h2T, wu, b_wu, blk, KB, ub, groups)
                        for gi, (t0, n) in enumerate(groups):
                            tt("dve", yT[:, fb, t0 - tok0:t0 - tok0 + n], ge[:, t0 - tok0:t0 - tok0 + n], ou[gi], ALU.mult,
                               [b_ge, b_ps[ub[gi]]], [b_yT])
                scratch_reset_to(lmark)
                fsq, b_fsq = salloc("fsq", [128, 5, 4]); rs2, b_rs2 = salloc("rs2", [128, 5])
                jk, b_jk = salloc("jk3", [128, 512], BF16)
                tmpl, b_tmpl = salloc("tmpl", [128, 1024])
                crow, b_crow = salloc("crow", [34, 512])
                xts = [salloc("xtl0", [128, D])] * 2
                if bi == 1:
                    for cchunk in range(FF // 512):
                        for j in range(4):
                            tr(ps[0:34, 6, j * 128:(j + 1) * 128], csave[:, cchunk * 4 + j, :], ident_f[:],
                               [b_csave, b_const], [b_ps[6]], sig=(j == 3))
                        cp("act", crow, ps[0:34, 6, :], [b_ps[6]], [b_crow])
                        P.dma("sp", ch_o, cs_o[l, :, cchunk * 512:(cchunk + 1) * 512], crow, reads=[b_crow])
                        if u == 0 and NU > 1:
                            P.dma("sp", mch(), halo[l, :, cchunk * 512:(cchunk + 1) * 512], crow[32:34, :],
                                  reads=[b_crow], writes=[b_halo[l]])
                P.label = "L%d.U%d.Ldn%d" % (l, u, bi)
                kgs = [(k0, min(8, FB - k0)) for k0 in range(0, FB, 8)]
                for q in range(D // 512):
                    for (k0, nk) in kgs:
                        wp, b_wp = wnext(w_down[l, k0 * 128:(k0 + nk) * 128, q * 512:(q + 1) * 512], nk * 128, 512)
                        for kk in range(nk):
                            kb = k0 + kk
                            for ti in range(ntile):
                                last = (kb == FB - 1)
                                mm(ps[:, ti, :], yT[:, kb, ti * 128:(ti + 1) * 128], wp[:, kk, :], kb == 0, last,
                                   [b_yT, b_wp], [b_ps[ti]], sig=(last or (kk == nk - 1 and ti == ntile - 1)))
                    for ti in range(ntile):
                        cp("dve", f_st[:, ti, q * 512:(q + 1) * 512], ps[:, ti, :], [b_ps[ti]], [b_fst])
                        act(jk, ps[:, ti, :], AF.Square, [b_ps[ti]], [b_jk, b_fsq], accum_out=fsq[:, ti, q:q + 1])
                P.label = "L%d.U%d.Lrs%d" % (l, u, bi)
                P.op("dve", lambda e: e.reduce_sum(out=rs2, in_=fsq, axis=AX.X), [b_fsq], [b_rs2])
                ts("dve", rs2, rs2, 1.0 / D, None, ALU.mult, None, [b_rs2], [b_rs2])
                rsqrt_inplace(rs2, b_rs2, [128, 5])
                for ti in range(ntile):
                    tile = t_lo + ti
                    xt, b_xt = xts[tile % 2]
                    P.dma("sp", ch_x[tile % 2], xt, xcur[tile], reads=[b_xcur[tile]], writes=[b_xt])
                    for half in range(2):
                        expand(3, tile, half, (5, 6))
                        sl = slice(half * 1024, (half + 1) * 1024)
                        pa = ps[:, 5:7, :].rearrange("p a b -> p (a b)")
                        stt("dve", tmpl, f_st[:, ti, sl], rs2[:, ti:ti + 1], pa, ALU.mult, ALU.mult,
                            [b_fst, b_rs2, b_ps[5], b_ps[6]], [b_tmpl])
                        tt("pool", xt[:, sl], xt[:, sl], tmpl, ALU.add, [b_xt, b_tmpl], [b_xt])
                    if l == n_layers - 1:
                        P.dma("sp", ch_x[tile % 2], y_o[tile], xt, reads=[b_xt])
                    else:
                        P.dma("sp", ch_x[tile % 2], xcur[tile], xt, reads=[b_xt], writes=[b_xcur[tile]])

          except _Stop:
            stopped[0] = True
        if stop_after is not None:
            dbg = dout("dbg", [128, 65536])
            dbgx = dout("dbgx", [NT, 128, D])
            allb = list({id(b): b for b in b_R.values()}.values())
            for i in range(4):
                P.dma("pool", mch(), dbg[:, i * 16384:(i + 1) * 16384], arena[:, i * 16384:(i + 1) * 16384], reads=allb)
            for i in range(NT):
                P.dma("sp", mch(), dbgx[i], xcur_u[0][i], reads=[b_xcur_u[0][i]])
        for ch in range(len(P.dcnt)):
            if P.dcnt[ch] > 0:
                P._need("sp", (ch, P.dcnt[ch]))
        if not dry:
            import os
            if os.environ.get("KDBG_LABELS"):
                import json
                json.dump(P.pe_labels, open(os.environ["KDBG_LABELS"], "w"))
            P.emit()
    return nc, rec


def _tables(half):
    hh = np.arange(8, dtype=np.float64)
    g = 1.0 - np.exp2(-5.0 - hh)
    inv = 10000.0 ** (-np.arange(64, dtype=np.float32) / np.float32(64))
    p = np.arange(128)
    cos = np.zeros((128, NT, 64), np.float32); sin = np.zeros((128, NT, 64), np.float32)
    for t in range(NT):
        pos = (half * 1024 + t * 128 + p) if t < NT - 1 else (16384 + p % 8)
        ang = pos.astype(np.float32)[:, None] * inv[None, :].astype(np.float32)
        cos[:, t] = np.cos(ang); sin[:, t] = np.sin(ang)
    dec = np.zeros((128, 2, 2, 8), np.float32)
    for kind in range(2):
        i = p if kind == 0 else p % 8
        dec[:, kind, 0, :] = g[None, :] ** (i[:, None] + 1.0)
        dec[:, kind, 1, :] = g[None, :] ** (-(i[:, None] + 1.0)) * (128.0 ** -0.5)
    gt = np.zeros((128, 2, 8), np.float32)
    gt[:, 0, :] = g ** 128.0
    gt[:, 1, :] = g ** 8.0
    jj = p[:, None]; ii = p[None, :]
    maskr = np.zeros((128, 2, 128), np.float32)
    maskr[:, 0, :] = (jj <= ii)
    maskr[:, 1, :] = (jj <= ii) & (jj // 8 == ii // 8)
    maskg = np.zeros((128, 2, 128), np.float32)
    maskg[:, 0, :] = (jj >= ii)
    maskg[:, 1, :] = (jj // 8 == ii // 8) & (ii % 8 <= jj % 8)
    E = np.zeros((NS, NT, 128), np.float32)
    E[0, 0:NT - 1, :] = 1.0
    for s in range(16):
        E[1 + s, NT - 1, 8 * s:8 * s + 8] = 1.0
    rowm = np.zeros((128, 16), np.float32)
    for s in range(16):
        rowm[8 * s:8 * s + 8, s] = 1.0
    return dict(tab_cos=cos, tab_sin=sin, tab_dec=dec, tab_g=gt, tab_maskr=maskr, tab_maskg=maskg,
                tab_E=E, tab_rowmask=rowm, ident=np.eye(128, dtype=np.float32))


_WNAMES = ["w_ada", "b_ada", "norm_pre1", "norm_post1", "norm_pre2", "norm_post2", "w_in", "sgu_w_s", "sgu_b_s",
           "sgu_ln_g", "sgu_ln_b", "ret_gn_g", "w_branch_a", "w_branch_b", "w_out", "ffn_w_gate", "ffn_w_up",
           "ffn_conv_w", "ffn_conv_b", "ffn_w_down"]


def core_inputs(c, inp):
    b = c % 4
    m = {}
    xp_all = np.asarray(inp["x_prompt"]); xs_all = np.asarray(inp["x_sample"])
    for u in range(2):
        s0 = 32 * b + 16 * u
        xs = xs_all[s0:s0 + 16].reshape(1, 128, D)
        xp = xp_all[b, u * 1024:(u + 1) * 1024].reshape(8, 128, D)
        m["xin%d" % u] = np.ascontiguousarray(np.concatenate([xp, xs], 0))
        m["cin%d" % u] = np.ascontiguousarray(np.concatenate([np.asarray(inp["c_prompt"])[b:b + 1],
                                                              np.asarray(inp["c_sample"])[s0:s0 + 16]], 0))
        m["sret%d" % u] = np.ascontiguousarray(np.asarray(inp["state_ret"])[:, s0:s0 + 16])
        m["sconv%d" % u] = np.ascontiguousarray(np.asarray(inp["state_conv"])[:, s0:s0 + 16].reshape(DEPTH, 32, FF))
        tb = _tables(u)
        m["tab_cos%d" % u] = tb["tab_cos"]; m["tab_sin%d" % u] = tb["tab_sin"]
    for n in _WNAMES:
        m[n] = np.asarray(inp[n])
    tb = _tables(0)
    for k in ("tab_dec", "tab_g", "tab_maskr", "tab_maskg", "tab_E", "tab_rowmask", "ident"):
        m[k] = tb[k]
    return m


_CACHE = {}


def get_nc():
    if "nc" not in _CACHE:
        _, rec = build(plan=None, dry=True)
        nc, _ = build(plan=rec)
        _CACHE["nc"] = nc
    return _CACHE["nc"]


def kernel(**inputs):
    nc = get_nc()
    active = [0, 1, 4, 5]
    base = [core_inputs(b, inputs) for b in range(4)]
    zero = {k: (v if (k.startswith("tab_") or k == "ident") else np.zeros_like(v)) for k, v in base[0].items()}
    in_maps = [zero] * 8
    in_maps = list(in_maps)
    for b, c in enumerate(active):
        in_maps[c] = base[b]
    res = run_bass_kernel_spmd(nc, in_maps, core_ids=list(range(8)))
    r = res.results
    B, S = 4, 2048
    y_prompt = np.zeros((B, S, D), np.float32)
    y_sample = np.zeros((128, 8, D), np.float32)
    ret_p = np.zeros((DEPTH, B, H, 128, 128), np.float32)
    ret_s = np.zeros((DEPTH, 128, H, 128, 128), np.float32)
    conv_p = np.zeros((DEPTH, B, 2, FF), np.float32)
    conv_s = np.zeros((DEPTH, 128, 2, FF), np.float32)
    v_s = np.zeros((DEPTH, 128, 8, 1024), np.float32)
    for b, c in enumerate(active):
        for u in range(2):
            s0 = 32 * b + 16 * u
            y = r[c]["y%d" % u]
            y_prompt[b, u * 1024:(u + 1) * 1024] = y[0:8].reshape(1024, D)
            y_sample[s0:s0 + 16] = y[8].reshape(16, 8, D)
            ret_s[:, s0:s0 + 16] = r[c]["rs_s%d" % u]
            conv_s[:, s0:s0 + 16] = r[c]["cs%d" % u][:, 0:32].reshape(DEPTH, 16, 2, FF)
            v_s[:, s0:s0 + 16] = r[c]["vs_s%d" % u].reshape(DEPTH, 16, 8, 1024)
        ret_p[:, b] = r[c]["rs_p"]
        conv_p[:, b] = r[c]["cs1"][:, 32:34]
    return (y_prompt, y_sample, ret_p, ret_s, conv_p, conv_s, v_s)
```
